# Optimizing a Trainium2 kernel written in Bass

```python
import jax, jax.numpy as jnp
from jax import lax
import numpy as np

D_MODEL = 2048
BATCH = 4
SEQ = 2048
DEPTH = 4

CHUNK = 64
N_MEM = 256
N_MEM_HEADS = 4
MEM_HEAD_DIM = D_MODEL // N_MEM_HEADS
D_FF = ((8 * D_MODEL // 3 + 255) // 256) * 256
A_WIDTH = D_MODEL // 2
B_WIDTH = D_MODEL // 2
A_HEADS = 8
A_HEAD_DIM = A_WIDTH // A_HEADS
GMLP_BLOCK = 128
CONV_WIDTH = 31
POOL_WINDOWS = (2, 4, 8, 16)
C_GROUPS = len(POOL_WINDOWS)
C_GROUP_DIM = D_MODEL // C_GROUPS
N_EVEN = (DEPTH + 1) // 2
N_ODD = DEPTH // 2
EPS = 1e-6

kernel_name = "hybrid_gmlp_conformer_pool_encoder"


def rms_norm(x, g):
    x32 = x.astype(jnp.float32)
    y = x32 * lax.rsqrt(jnp.mean(x32 * x32, axis=-1, keepdims=True) + EPS)
    return (y * g.astype(jnp.float32)).astype(x.dtype)


def layer_norm(x, g, b):
    x32 = x.astype(jnp.float32)
    mu = jnp.mean(x32, axis=-1, keepdims=True)
    xc = x32 - mu
    var = jnp.mean(xc * xc, axis=-1, keepdims=True)
    y = xc * lax.rsqrt(var + EPS)
    return (y * g.astype(jnp.float32) + b.astype(jnp.float32)).astype(x.dtype)


def swiglu_ffn(h, w_gate, w_up, w_down):
    return (jax.nn.silu(h @ w_gate) * (h @ w_up)) @ w_down


def gmlp_spatial_gate(u, v, w_s, b_s, ln_g, ln_b):
    b, s, _ = u.shape
    v = layer_norm(v, ln_g, ln_b)
    pos = jnp.arange(GMLP_BLOCK)
    mask = (pos[None, :] // CHUNK) <= (pos[:, None] // CHUNK)
    w = jnp.where(mask[None], w_s, jnp.zeros_like(w_s))
    vb = v.reshape(b, s // GMLP_BLOCK, GMLP_BLOCK, A_HEADS, A_HEAD_DIM)
    sp = jnp.einsum('hpq,bnqhc->bnphc', w, vb) + b_s.T[None, None, :, :, None]
    return u * sp.reshape(b, s, A_WIDTH)


def conformer_conv(a, g, conv_w, conv_b, ln_g, ln_b):
    h = a * jax.nn.sigmoid(g)
    h = lax.conv_general_dilated(
        h, conv_w, window_strides=(1,), padding=[(CONV_WIDTH - 1, 0)],
        dimension_numbers=('NWC', 'WIO', 'NWC'), feature_group_count=B_WIDTH) + conv_b
    h = layer_norm(h, ln_g, ln_b)
    return jax.nn.silu(h)


def even_mixer(h, w_in, b_in, w_s, b_s, gln_g, gln_b, conv_w, conv_b, cln_g, cln_b, w_out, b_out):
    z = h @ w_in + b_in
    u_a, v_a, a_b, g_b = jnp.split(z, [A_WIDTH, 2 * A_WIDTH, 2 * A_WIDTH + B_WIDTH], axis=-1)
    y_a = gmlp_spatial_gate(jax.nn.gelu(u_a), jax.nn.gelu(v_a), w_s, b_s, gln_g, gln_b)
    y_b = conformer_conv(a_b, g_b, conv_w, conv_b, cln_g, cln_b)
    return jnp.concatenate([y_a, y_b], axis=-1) @ w_out + b_out


def multiscale_pool_mixer(h, w_c, b_c, scale):
    b, s, _ = h.shape
    h32 = h.astype(jnp.float32)
    cs = jnp.cumsum(h32, axis=1)
    t = jnp.arange(1, s + 1, dtype=jnp.float32)[None, :, None]
    outs = []
    for gi, win in enumerate(POOL_WINDOWS):
        sl = slice(gi * C_GROUP_DIM, (gi + 1) * C_GROUP_DIM)
        c = cs[..., sl]
        prev = jnp.pad(c, ((0, 0), (win, 0), (0, 0)))[:, :s]
        mean = (c - prev) / jnp.minimum(t, win)
        d_g = (mean - h32[..., sl]).astype(h.dtype)
        outs.append(d_g @ w_c[gi] + b_c[gi])
    return jnp.concatenate(outs, axis=-1) * scale


def memory_cross_attention(h, mem_n, wq, wk, wv, wo):
    b, s, _ = h.shape
    m = mem_n.shape[1]
    q = (h @ wq).reshape(b, s, N_MEM_HEADS, MEM_HEAD_DIM)
    k = (mem_n @ wk).reshape(b, m, N_MEM_HEADS, MEM_HEAD_DIM)
    v = (mem_n @ wv).reshape(b, m, N_MEM_HEADS, MEM_HEAD_DIM)
    scores = jnp.einsum('bshd,bmhd->bhsm', q, k).astype(jnp.float32) * (MEM_HEAD_DIM ** -0.5)
    p = jax.nn.softmax(scores, axis=-1).astype(v.dtype)
    o = jnp.einsum('bhsm,bmhd->bshd', p, v).reshape(b, s, D_MODEL)
    return o @ wo


def setup_inputs(seed: int = 0) -> dict:
    key = jax.random.key(seed)
    ks = iter(jax.random.split(key, 48))

    def nrm(shape, scale):
        return jax.random.normal(next(ks), shape, jnp.float32) * scale

    def gain(shape):
        return 1.0 + nrm(shape, 0.1)

    D, F = D_MODEL, D_FF
    return {
        "x": nrm((BATCH, SEQ, D), 1.0),
        "mem": nrm((BATCH, N_MEM, D), 1.0),
        "norm_ffn1": gain((DEPTH, D)),
        "ffn1_gate": nrm((DEPTH, D, F), D ** -0.5),
        "ffn1_up": nrm((DEPTH, D, F), D ** -0.5),
        "ffn1_down": nrm((DEPTH, F, D), F ** -0.5),
        "norm_mix": gain((DEPTH, D)),
        "ab_w_in": nrm((N_EVEN, D, 2 * A_WIDTH + 2 * B_WIDTH), D ** -0.5),
        "ab_b_in": nrm((N_EVEN, 2 * A_WIDTH + 2 * B_WIDTH), 0.02),
        "gmlp_w_s": nrm((N_EVEN, A_HEADS, GMLP_BLOCK, GMLP_BLOCK), GMLP_BLOCK ** -0.5),
        "gmlp_b_s": gain((N_EVEN, A_HEADS, GMLP_BLOCK)),
        "gmlp_ln_g": gain((N_EVEN, A_WIDTH)),
        "gmlp_ln_b": nrm((N_EVEN, A_WIDTH), 0.02),
        "conv_w": nrm((N_EVEN, CONV_WIDTH, 1, B_WIDTH), CONV_WIDTH ** -0.5),
        "conv_b": nrm((N_EVEN, B_WIDTH), 0.02),
        "conv_ln_g": gain((N_EVEN, B_WIDTH)),
        "conv_ln_b": nrm((N_EVEN, B_WIDTH), 0.02),
        "ab_w_out": nrm((N_EVEN, A_WIDTH + B_WIDTH, D), (A_WIDTH + B_WIDTH) ** -0.5),
        "ab_b_out": nrm((N_EVEN, D), 0.02),
        "pool_w": nrm((N_ODD, C_GROUPS, C_GROUP_DIM, C_GROUP_DIM), C_GROUP_DIM ** -0.5),
        "pool_b": nrm((N_ODD, C_GROUPS, C_GROUP_DIM), 0.02),
        "pool_scale": 0.5 + nrm((N_ODD, D), 0.05),
        "norm_xq": gain((DEPTH, D)),
        "norm_xkv": gain((DEPTH, D)),
        "xattn_wq": nrm((DEPTH, D, D), D ** -0.5),
        "xattn_wk": nrm((DEPTH, D, D), D ** -0.5),
        "xattn_wv": nrm((DEPTH, D, D), D ** -0.5),
        "xattn_wo": nrm((DEPTH, D, D), D ** -0.5),
        "norm_ffn2": gain((DEPTH, D)),
        "ffn2_gate": nrm((DEPTH, D, F), D ** -0.5),
        "ffn2_up": nrm((DEPTH, D, F), D ** -0.5),
        "ffn2_down": nrm((DEPTH, F, D), F ** -0.5),
        "norm_final": gain((D,)),
    }


def reference(x, mem, norm_ffn1, ffn1_gate, ffn1_up, ffn1_down, norm_mix,
              ab_w_in, ab_b_in, gmlp_w_s, gmlp_b_s, gmlp_ln_g, gmlp_ln_b,
              conv_w, conv_b, conv_ln_g, conv_ln_b, ab_w_out, ab_b_out,
              pool_w, pool_b, pool_scale, norm_xq, norm_xkv,
              xattn_wq, xattn_wk, xattn_wv, xattn_wo,
              norm_ffn2, ffn2_gate, ffn2_up, ffn2_down, norm_final):
    for l in range(DEPTH):
        h = rms_norm(x, norm_ffn1[l])
        x = x + 0.5 * swiglu_ffn(h, ffn1_gate[l], ffn1_up[l], ffn1_down[l])
        h = rms_norm(x, norm_mix[l])
        if l % 2 == 0:
            e = l // 2
            x = x + even_mixer(h, ab_w_in[e], ab_b_in[e], gmlp_w_s[e], gmlp_b_s[e],
                               gmlp_ln_g[e], gmlp_ln_b[e], conv_w[e], conv_b[e],
                               conv_ln_g[e], conv_ln_b[e], ab_w_out[e], ab_b_out[e])
        else:
            o = l // 2
            x = x + multiscale_pool_mixer(h, pool_w[o], pool_b[o], pool_scale[o])
        h = rms_norm(x, norm_xq[l])
        m = rms_norm(mem, norm_xkv[l])
        x = x + memory_cross_attention(h, m, xattn_wq[l], xattn_wk[l], xattn_wv[l], xattn_wo[l])
        h = rms_norm(x, norm_ffn2[l])
        x = x + 0.5 * swiglu_ffn(h, ffn2_gate[l], ffn2_up[l], ffn2_down[l])
    return rms_norm(x, norm_final)
```

```python
import numpy as np
from contextlib import ExitStack
import concourse.bass as bass
import concourse.mybir as mybir
from concourse.bass_utils import run_bass_kernel_spmd

F32 = mybir.dt.float32
BF16 = mybir.dt.bfloat16
AF = mybir.ActivationFunctionType
ALU = mybir.AluOpType

D = 2048
KC = 16
FF = 5632
FCH = 44
SEQ = 2048
BATCH = 4
NMEM = 256
DEPTH = 4
T = 1152
TT = 384
NTT = 3
HALO0 = 896
G = 2
NG = FCH // G
SLOT = 4096
NSLOT = 5
LOOKBACK = 1
EPS = 1e-6
CONVW = 31

ENGS = ("pe", "act", "dve", "pool", "sp")


class Res:
    __slots__ = ("name", "w", "rs")

    def __init__(self, name=""):
        self.name = name
        self.w = None
        self.rs = []


class Op:
    __slots__ = ("eng", "fn", "reads", "writes", "dsem", "deps", "sig", "ev")

    def __init__(self, eng, fn, reads, writes, dsem):
        self.eng = eng
        self.fn = fn
        self.reads = reads
        self.writes = writes
        self.dsem = dsem
        self.deps = set()
        self.sig = False
        self.ev = None


class DmaSem:
    def __init__(self, handle):
        self.h = handle
        self.count = 0


class Prog:
    def __init__(self, nc, same_engine_sync=True):
        self.nc = nc
        self.ops = []
        self.same_engine_sync = same_engine_sync

    def op(self, eng, fn, reads=(), writes=(), dsem=None):
        o = Op(eng, fn, tuple(reads), tuple(writes), dsem)
        self.ops.append(o)
        return o

    def pe(self, fn, reads=(), writes=()):
        return self.op("pe", fn, reads, writes)

    def act(self, fn, reads=(), writes=()):
        return self.op("act", fn, reads, writes)

    def dve(self, fn, reads=(), writes=()):
        return self.op("dve", fn, reads, writes)

    def dma(self, eng, fn, dsem, reads=(), writes=()):
        if not hasattr(dsem, "res"):
            dsem.res = Res("dsem")
        return self.op(eng, fn, reads, tuple(writes) + (dsem.res,), dsem)

    def finalize(self, sems, final_waits):
        ops = self.ops
        for i, o in enumerate(ops):
            for r in o.reads:
                if r.w is not None:
                    o.deps.add(r.w)
            for w in o.writes:
                if w.w is not None:
                    o.deps.add(w.w)
                for rr in w.rs:
                    o.deps.add(rr)
            for r in o.reads:
                r.rs.append(i)
            for w in o.writes:
                w.w = i
                w.rs = []
            o.deps.discard(i)
        for i, o in enumerate(ops):
            keep = set()
            for d in o.deps:
                p = ops[d]
                if p.eng == o.eng and p.dsem is None and o.dsem is None:
                    if o.eng == "pe":
                        continue
                    if not self.same_engine_sync:
                        continue
                keep.add(d)
            o.deps = keep
            for d in keep:
                ops[d].sig = True
        counts = {e: 0 for e in ENGS}
        for o in ops:
            if o.dsem is not None:
                o.dsem.count += 16
                o.ev = (o.dsem.h, o.dsem.count)
            elif o.sig:
                counts[o.eng] += 1
                o.ev = (sems[o.eng], counts[o.eng])
        self.counts = counts
        streams = {e: [] for e in ENGS}
        for o in ops:
            streams[o.eng].append(o)

        def emit(engh, lst, extra_final):
            waited = {}
            for o in lst:
                need = {}
                for d in o.deps:
                    sem, val = ops[d].ev
                    k = id(sem)
                    if k not in need or need[k][1] < val:
                        need[k] = (sem, val)
                for k, (sem, val) in need.items():
                    if waited.get(k, 0) >= val:
                        continue
                    engh.wait_ge(sem, val)
                    waited[k] = val
                inst = o.fn(engh)
                if o.dsem is not None:
                    inst.then_inc(o.dsem.h, 16)
                elif o.sig:
                    inst.then_inc(sems[o.eng], 1)
            for (sem, val) in extra_final:
                engh.wait_ge(sem, val)

        with self.nc.Block() as block:
            @block.tensor
            def _(e):
                emit(e, streams["pe"], [])

            @block.scalar
            def _(e):
                emit(e, streams["act"], [])

            @block.vector
            def _(e):
                emit(e, streams["dve"], [])

            @block.gpsimd
            def _(e):
                emit(e, streams["pool"], [])

            @block.sync
            def _(e):
                emit(e, streams["sp"], list(final_waits()))


def vec_layout():
    off = {}
    n = 0

    def add(name, cols):
        nonlocal n
        off[name] = n
        n += cols

    for l in range(DEPTH):
        for nm in ("n_ffn1", "n_mix", "n_xq", "n_xkv", "n_ffn2"):
            add(f"{nm}{l}", 16)
    for e in range(2):
        for nm, c in (("b_u", 8), ("b_a", 8), ("b_g", 8), ("gln_g", 8), ("gln_b", 8),
                      ("conv_b", 8), ("cln_g", 8), ("cln_b", 8), ("b_out", 16), ("conv_w", CONVW * 8)):
            add(f"{nm}{e}", c)
    for o in range(2):
        add(f"pool_b{o}", 16)
        add(f"pool_s{o}", 16)
    add("n_final", 16)
    return off, n


VOFF, NV = vec_layout()


def fm(v):
    v = np.asarray(v, np.float32).reshape(-1, 128)
    return np.ascontiguousarray(v.T)


def pack_vecs(inp):
    V = np.zeros((128, NV), np.float32)

    def put(name, arr):
        a = fm(arr)
        V[:, VOFF[name]:VOFF[name] + a.shape[1]] = a

    for l in range(DEPTH):
        put(f"n_ffn1{l}", inp["norm_ffn1"][l])
        put(f"n_mix{l}", inp["norm_mix"][l])
        put(f"n_xq{l}", inp["norm_xq"][l])
        put(f"n_xkv{l}", inp["norm_xkv"][l])
        put(f"n_ffn2{l}", inp["norm_ffn2"][l])
    for e in range(2):
        b_in = np.asarray(inp["ab_b_in"][e])
        put(f"b_u{e}", b_in[0:1024])
        put(f"b_a{e}", b_in[2048:3072])
        put(f"b_g{e}", b_in[3072:4096])
        put(f"gln_g{e}", inp["gmlp_ln_g"][e])
        put(f"gln_b{e}", inp["gmlp_ln_b"][e])
        put(f"conv_b{e}", inp["conv_b"][e])
        put(f"cln_g{e}", inp["conv_ln_g"][e])
        put(f"cln_b{e}", inp["conv_ln_b"][e])
        put(f"b_out{e}", inp["ab_b_out"][e])
        cw = np.asarray(inp["conv_w"][e]).reshape(CONVW, 8, 128)
        V[:, VOFF[f"conv_w{e}"]:VOFF[f"conv_w{e}"] + CONVW * 8] = cw.transpose(2, 0, 1).reshape(128, CONVW * 8)
    for o in range(2):
        put(f"pool_b{o}", np.asarray(inp["pool_b"][o]).reshape(-1))
        put(f"pool_s{o}", inp["pool_scale"][o])
    put("n_final", inp["norm_final"])
    return V


def build_nc(n_layers=DEPTH, do_final=True, stop_stage=None):
    nc = bass.Bass("TRN2", target_bir_lowering=False)

    def din(name, shape):
        return nc.dram_tensor(name, list(shape), F32, kind="ExternalInput").ap()

    x_d = din("x", (T, D))
    mem_d = din("mem", (NMEM, D))
    vec_d = din("vecs", (128, NV))
    ident_d = din("ident", (128, 128))
    corr_d = din("corr", (128, 64))
    bvb_d = din("bvb", (2, 128, 1024))
    bsb_d = din("bsb", (2, 128, 1024))
    ws_d = din("gmlp_w_s", (2, 8, 128, 128))
    wd = {}
    for nm, shp in (("ffn1_gate", (DEPTH, D, FF)), ("ffn1_up", (DEPTH, D, FF)), ("ffn1_down", (DEPTH, FF, D)),
                    ("ffn2_gate", (DEPTH, D, FF)), ("ffn2_up", (DEPTH, D, FF)), ("ffn2_down", (DEPTH, FF, D)),
                    ("ab_w_in", (2, D, 4096)), ("ab_w_out", (2, D, D)), ("pool_w", (2, 4, 512, 512)),
                    ("xattn_wq", (DEPTH, D, D)), ("xattn_wk", (DEPTH, D, D)),
                    ("xattn_wv", (DEPTH, D, D)), ("xattn_wo", (DEPTH, D, D))):
        wd[nm] = din(nm, shp)
    out_d = nc.dram_tensor("out", [T, D], F32, kind="ExternalOutput").ap()

    es = ExitStack()
    with es:
        E = es.enter_context

        def sb(name, shape, dt):
            return E(nc.sbuf_tensor(name, list(shape), dt))

        xres = sb("xres", (128, KC, T), F32)
        hbuf = sb("hbuf", (128, NTT, KC, TT), BF16)
        hid = sb("hid", (128, G, T), BF16)
        ring = sb("ring", (128, NSLOT, SLOT), BF16)
        arena = sb("arena", (128, 8192), F32)
        vecs = sb("vecs_sb", (128, NV), F32)
        ident = sb("ident_sb", (128, 128), F32)
        ones_f = sb("ones_f", (128, 128), F32)
        ones_b = sb("ones_b", (128, 128), BF16)
        NTMP = 4
        tmpf = sb("tmpf", (128, NTMP, TT), F32)
        tmpb = sb("tmpb", (128, NTMP, TT), BF16)
        rstd_t = sb("rstd_t", (128, 2, TT), F32)
        h2flat = hbuf[:, 2, :, :].rearrange("p c t -> p (c t)")
        bvb = h2flat[:, 0:2048].bitcast(F32)
        Th = h2flat[:, 2048:4096].bitcast(F32).rearrange("p (h q) -> p h q", h=8)
        wsT = h2flat[:, 4096:5120].rearrange("p (h q) -> p h q", h=8)
        corr = sb("corr_sb", (128, 64), F32)
        small = sb("small", (128, 64), F32)
        epsc = sb("epsc", (128, 1), F32)
        ps = [E(nc.psum_tensor(f"ps{i}", [128, 512], F32)) for i in range(8)]

        sems = {e: E(nc.semaphore(f"s_{e}")) for e in ("pe", "act", "dve", "pool")}
        dsl = [DmaSem(E(nc.semaphore(f"dslot{i}"))) for i in range(NSLOT)]
        d_misc = [DmaSem(E(nc.semaphore(f"dmisc{i}"))) for i in range(4)]
        d_out = [DmaSem(E(nc.semaphore(f"dout{i}"))) for i in range(3)]

        P = Prog(nc)

        r_x = [[Res(f"x{kc}_{tt}") for tt in range(NTT)] for kc in range(KC)]
        r_h = [[Res(f"h{s}_{kc}") for kc in range(KC)] for s in range(NTT)]
        r_hid = [[Res() for tt in range(NTT)] for fc in range(G)]
        r_slot = [Res(f"slot{i}") for i in range(NSLOT)]
        r_ps = [Res(f"ps{i}") for i in range(8)]
        r_tmp = [Res() for _ in range(NTMP)]
        r_tmpb = [Res() for _ in range(NTMP)]
        r_rstd = [Res(), Res()]
        r_vecs, r_ident, r_ones, r_corr = Res(), Res(), Res(), Res()
        r_small = Res()
        r_sm = [Res() for _ in range(8)]
        r_bvb = r_h[2][0:6]
        r_Th = r_h[2][5:11]
        r_wsT = r_h[2][10:14]
        r_hc = [Res() for _ in range(8)]
        r_cv = [Res() for _ in range(8)]
        r_vn = [Res() for _ in range(3)]
        r_stgx = [Res(), Res()]
        r_arena_all = r_hc + r_cv + r_vn + r_stgx
        r_vtb = [Res(), Res(), Res()]

        HCW = 30 + TT
        hc = arena[:, 0:4 * HCW].bitcast(BF16).rearrange("p (c w) -> p c w", c=8)
        cv = arena[:, 3312:3312 + 8 * TT].rearrange("p (c w) -> p c w", c=8)
        vt = arena[:, 3312:3312 + 3072].rearrange("p (b w) -> p b w", b=3)
        vn = arena[:, 6384:7920].bitcast(BF16).rearrange("p (b w) -> p b w", b=3)
        kT = arena[:, 0:2048].bitcast(BF16).rearrange("p (c m) -> p c m", c=KC)
        vtok = arena[:, 2048:4096].bitcast(BF16).rearrange("p (b d) -> p b d", b=2)
        memT = arena[:, 4096:6144].bitcast(BF16).rearrange("p (c m) -> p c m", c=KC)
        stg = arena[:, 6144:8192]
        expT = arena[:, 6144:6144 + 768].bitcast(BF16).rearrange("p (a b w) -> p a b w", a=2, b=2)
        wst = arena[:, 1656:1656 + 1024].rearrange("p (h q) -> p h q", h=8)
        r_wst = Res("wst")
        r_arena_all.append(r_wst)
        HPW = 16 + TT
        hp = arena[:, 0:16 * HPW].rearrange("p (c w) -> p c w", c=KC)
        pwt = arena[:, 6400:6400 + 4 * HPW].rearrange("p (c w) -> p c w", c=4)

        tmp_i = [0]

        def get_tmp():
            i = tmp_i[0] % NTMP
            tmp_i[0] += 1
            return tmpf[:, i, :], r_tmp[i]

        tmpb_i = [0]

        def get_tmpb():
            i = tmpb_i[0] % NTMP
            tmpb_i[0] += 1
            return tmpb[:, i, :], r_tmpb[i]

        ps_i = [0]

        def get_ps(lo=0, hi=8):
            n = hi - lo
            i = lo + (ps_i[0] % n)
            ps_i[0] += 1
            return ps[i], r_ps[i]

        r_bar = Res("bar")

        def arena_barrier(extra=()):
            P.dve(lambda e: e.memset(small[:, 63:64], 0.0), writes=r_arena_all + [r_bar] + list(extra))

        def vcol(name, c, n=1):
            o = VOFF[name] + c
            return vecs[:, o:o + n]

        P.dma("sp", lambda e: e.dma_start(out=vecs[:], in_=vec_d[:, :]), d_misc[0], writes=[r_vecs])
        P.dma("sp", lambda e: e.dma_start(out=ident[:], in_=ident_d[:, :]), d_misc[0], writes=[r_ident])
        P.dma("sp", lambda e: e.dma_start(out=corr[:], in_=corr_d[:, :]), d_misc[0], writes=[r_corr])
        P.dve(lambda e: e.memset(ones_f[:], 1.0), writes=[r_ones])
        P.dve(lambda e: e.memset(ones_b[:], 1.0), writes=[r_ones])

        stg2 = [arena[:, 6144:8192], arena[:, 4096:6144]]
        for blk in range(T // 128):
            tt, bo = divmod(blk, 3)
            sg_, rsg_ = stg2[blk % 2], r_stgx[blk % 2]
            P.dma("sp", lambda e, blk=blk, sg_=sg_: e.dma_start(out=sg_, in_=x_d[blk * 128:(blk + 1) * 128, :]),
                  d_misc[1 + 2 * (blk % 2)], writes=[rsg_])
            for q in range(4):
                pt, rpt = get_ps()
                for j in range(4):
                    kc = 4 * q + j
                    P.pe(lambda e, pt=pt, j=j, kc=kc, sg_=sg_: e.transpose(out=pt[:, j * 128:(j + 1) * 128],
                                                                           in_=sg_[:, kc * 128:(kc + 1) * 128], identity=ident[:]),
                         reads=[rsg_, r_ident], writes=[rpt])
                dst = xres[:, 4 * q:4 * q + 4, blk * 128:(blk + 1) * 128]
                src = pt[:].rearrange("p (c t) -> p c t", c=4)
                wr = [r_x[4 * q + j][tt] for j in range(4)]
                if q % 2 == 0:
                    P.act(lambda e, dst=dst, src=src: e.activation(out=dst, in_=src, func=AF.Copy), reads=[rpt], writes=wr)
                else:
                    P.dve(lambda e, dst=dst, src=src: e.tensor_copy(out=dst, in_=src), reads=[rpt], writes=wr)

        tasks = []

        def add_c(fn):
            tasks.append(("c", None, fn))

        def add_w(load, fn):
            tasks.append(("w", load, fn))

        def slot_view3(si, a, b):
            return ring[:, si, 0:a * b].rearrange("p (a b) -> p a b", a=a)

        def load_cols(wap, c0, ncols, kchunks):
            def f(si):
                src = wap.rearrange("(kc p) f -> p kc f", p=128)[:, :, c0:c0 + ncols]
                dst = slot_view3(si, kchunks, ncols)
                P.dma("pool", lambda e: e.dma_start(out=dst, in_=src), dsl[si], writes=[r_slot[si]])
            return f

        def load_rows(wap, r0, nrc):
            def f(si):
                src = wap[r0 * 128:(r0 + nrc) * 128, :].rearrange("(fc p) d -> p fc d", p=128)
                dst = slot_view3(si, nrc, D)
                P.dma("pool", lambda e: e.dma_start(out=dst, in_=src), dsl[si], writes=[r_slot[si]])
            return f

        def emit_rstd(xsrc, xres_list, nchunks, n, inv_n, out_idx):
            pst, rpst = ps[7], r_ps[7]
            for c in range(nchunks):
                sq, rsq = get_tmpb()
                P.act(lambda e, sq=sq, c=c: e.activation(out=sq[:, 0:n], in_=xsrc(c), func=AF.Square),
                      reads=[xres_list[c]], writes=[rsq])
                P.pe(lambda e, sq=sq, c=c: e.matmul(pst[:, 0:n], lhsT=ones_b[:], rhs=sq[:, 0:n],
                                                    start=(c == 0), stop=(c == nchunks - 1)),
                     reads=[rsq, r_ones], writes=[rpst])
            rt = rstd_t[:, out_idx, 0:n]
            P.act(lambda e: e.activation(out=rt, in_=pst[:, 0:n], func=AF.Sqrt, bias=EPS_AP(), scale=inv_n),
                  reads=[rpst, r_eps], writes=[r_rstd[out_idx]])
            P.dve(lambda e: e.reciprocal(out=rt, in_=rt), reads=[r_rstd[out_idx]], writes=[r_rstd[out_idx]])
            return rt, r_rstd[out_idx]

        r_eps = Res("eps")

        def EPS_AP():
            return epsc[:, 0:1]

        P.dve(lambda e: e.memset(epsc[:, 0:1], EPS), writes=[r_eps])

        rs_i = [0]

        def emit_norm(gname, tt, dst_fn, dst_res_fn):
            c0 = tt * TT
            oi = rs_i[0] % 2
            rs_i[0] += 1
            rt, rrt = emit_rstd(lambda c: xres[:, c, c0:c0 + TT], [r_x[c][tt] for c in range(KC)], KC, TT, 1.0 / D, oi)
            for kc in range(KC):
                P.dve(lambda e, kc=kc: e.scalar_tensor_tensor(out=dst_fn(kc), in0=xres[:, kc, c0:c0 + TT],
                                                              scalar=vcol(gname, kc), in1=rt,
                                                              op0=ALU.mult, op1=ALU.mult),
                      reads=[r_x[kc][tt], rrt, r_vecs], writes=[dst_res_fn(kc)])

        def ffn_tasks(l, which):
            wg, wu, wdn = wd[f"ffn{which}_gate"][l], wd[f"ffn{which}_up"][l], wd[f"ffn{which}_down"][l]
            gname = f"n_ffn{which}{l}"

            def norm_all():
                for tt in range(NTT):
                    emit_norm(gname, tt, lambda kc, tt=tt: hbuf[:, tt, kc, :], lambda kc, tt=tt: r_h[tt][kc])
            add_c(norm_all)
            st = {}
            for g in range(NG):
                def c_gate(si, st=st):
                    st["g"] = si

                def c_up(si, st=st):
                    sg = st["g"]
                    gv = slot_view3(sg, KC, G * 128)
                    uv = slot_view3(si, KC, G * 128)
                    for fc in range(G):
                        for tt in range(NTT):
                            pg, rpg = get_ps(0, 6)
                            pu, rpu = get_ps(0, 6)
                            for kc in range(KC):
                                P.pe(lambda e, pg=pg, kc=kc, fc=fc, tt=tt: e.matmul(
                                    pg[:, 0:TT], lhsT=gv[:, kc, fc * 128:(fc + 1) * 128], rhs=hbuf[:, tt, kc, :],
                                    start=(kc == 0), stop=(kc == KC - 1)),
                                    reads=[r_slot[sg], r_h[tt][kc]], writes=[rpg])
                            for kc in range(KC):
                                P.pe(lambda e, pu=pu, kc=kc, fc=fc, tt=tt: e.matmul(
                                    pu[:, 0:TT], lhsT=uv[:, kc, fc * 128:(fc + 1) * 128], rhs=hbuf[:, tt, kc, :],
                                    start=(kc == 0), stop=(kc == KC - 1)),
                                    reads=[r_slot[si], r_h[tt][kc]], writes=[rpu])
                            sl, rsl = get_tmp()
                            P.act(lambda e, sl=sl, pg=pg: e.activation(out=sl, in_=pg[:, 0:TT], func=AF.Silu),
                                  reads=[rpg], writes=[rsl])
                            P.dve(lambda e, sl=sl, pu=pu, fc=fc, tt=tt: e.tensor_tensor(
                                out=hid[:, fc, tt * TT:(tt + 1) * TT], in0=sl, in1=pu[:, 0:TT], op=ALU.mult),
                                reads=[rsl, rpu], writes=[r_hid[fc][tt]])

                def c_down(si):
                    dv = slot_view3(si, G, D)
                    for dc in range(KC):
                        for tt in range(NTT):
                            pd, rpd = get_ps(0, 6)
                            for fc in range(G):
                                P.pe(lambda e, pd=pd, fc=fc, dc=dc, tt=tt: e.matmul(
                                    pd[:, 0:TT], lhsT=dv[:, fc, dc * 128:(dc + 1) * 128],
                                    rhs=hid[:, fc, tt * TT:(tt + 1) * TT], start=(fc == 0), stop=(fc == G - 1)),
                                    reads=[r_slot[si], r_hid[fc][tt]], writes=[rpd])
                            xs = xres[:, dc, tt * TT:(tt + 1) * TT]
                            P.dve(lambda e, pd=pd, xs=xs: e.scalar_tensor_tensor(
                                out=xs, in0=pd[:, 0:TT], scalar=0.5, in1=xs, op0=ALU.mult, op1=ALU.add),
                                reads=[rpd, r_x[dc][tt]], writes=[r_x[dc][tt]])

                add_w(load_cols(wg, g * G * 128, G * 128, KC), c_gate)
                add_w(load_cols(wu, g * G * 128, G * 128, KC), c_up)
                add_w(load_rows(wdn, g * G, G), c_down)

        def proj_add_tasks(wap, src_sub, tt, bias_name=None, hooks=None):
            for j in range(8):
                if hooks and j in hooks:
                    hooks[j]()

                def c(si, j=j):
                    wv = slot_view3(si, KC, 256)
                    for cc in range(2):
                        dc = 2 * j + cc
                        pd, rpd = get_ps(0, 7)
                        for kc in range(KC):
                            P.pe(lambda e, pd=pd, kc=kc, cc=cc: e.matmul(
                                pd[:, 0:TT], lhsT=wv[:, kc, cc * 128:(cc + 1) * 128], rhs=hbuf[:, src_sub, kc, :],
                                start=(kc == 0), stop=(kc == KC - 1)),
                                reads=[r_slot[si], r_h[src_sub][kc]], writes=[rpd])
                        xs = xres[:, dc, tt * TT:(tt + 1) * TT]
                        if bias_name is None:
                            P.dve(lambda e, pd=pd, xs=xs: e.tensor_tensor(out=xs, in0=pd[:, 0:TT], in1=xs, op=ALU.add),
                                  reads=[rpd, r_x[dc][tt]], writes=[r_x[dc][tt]])
                        else:
                            P.dve(lambda e, pd=pd, xs=xs, dc=dc: e.scalar_tensor_tensor(
                                out=xs, in0=pd[:, 0:TT], scalar=vcol(bias_name, dc), in1=xs, op0=ALU.add, op1=ALU.add),
                                reads=[rpd, r_x[dc][tt], r_vecs], writes=[r_x[dc][tt]])
                add_w(load_cols(wap, j * 256, 256, KC), c)

        def even_setup(e_):
            def f():
                arena_barrier()
                for c in range(8):
                    P.dve(lambda e, c=c: e.memset(hc[:, c, 0:30], 0.0), writes=[r_hc[c]])
                P.dma("sp", lambda e: e.dma_start(out=wst, in_=ws_d[e_].rearrange("h p q -> p h q")), d_misc[3], writes=[r_wst])
                P.dve(lambda e: e.memset(wst[0:64, :, 64:128], 0.0), writes=[r_wst])
            return f

        def even_setup_b(e_):
            def f():
                P.dma("sp", lambda e: e.dma_start(out=bvb, in_=bvb_d[e_]), d_misc[2], writes=r_bvb)
                P.dma("sp", lambda e: e.dma_start(out=Th.rearrange("p h q -> p (h q)"), in_=bsb_d[e_]),
                      d_misc[2], writes=r_Th)
                for h in range(8):
                    pt, rpt = get_ps(0, 7)
                    P.pe(lambda e, pt=pt, h=h: e.transpose(out=pt[:, 0:128], in_=wst[:, h, :], identity=ident[:]),
                         reads=[r_wst, r_ident], writes=[rpt])
                    P.act(lambda e, pt=pt, h=h: e.activation(out=wsT[:, h, :], in_=pt[:, 0:128], func=AF.Copy),
                          reads=[rpt], writes=r_wsT)
                    pr, rpr = get_ps(0, 7)
                    P.pe(lambda e, pr=pr, h=h: e.matmul(pr[:, 0:128], lhsT=ones_b[:], rhs=wsT[:, h, :], start=True, stop=True),
                         reads=r_wsT + [r_ones], writes=[rpr])
                    P.dve(lambda e, pr=pr, h=h: e.scalar_tensor_tensor(
                        out=Th[:, h, :], in0=pr[:, 0:128], scalar=vcol(f"gln_b{e_}", h), in1=Th[:, h, :],
                        op0=ALU.mult, op1=ALU.add), reads=[rpr, r_vecs] + r_Th, writes=r_Th)
            return f

        def even_pre(l, tt):
            def pre():
                emit_norm(f"n_mix{l}", tt, lambda kc: hbuf[:, 0, kc, :], lambda kc: r_h[0][kc])
            add_c(pre)

        def even_tasks(l, tt, before_out=None, out_hooks=None):
            e_ = l // 2
            w_in = wd["ab_w_in"][e_]
            w_out = wd["ab_w_out"][e_]
            cwo = VOFF[f"conv_w{e_}"]

            st = {}
            for j in range(4):
                def c_a(si, st=st):
                    st["a"] = si

                def c_g(si, j=j, st=st):
                    sa = st["a"]
                    av = slot_view3(sa, KC, 256)
                    gv = slot_view3(si, KC, 256)
                    for cc in range(2):
                        ch = 2 * j + cc
                        pa, rpa = get_ps(0, 7)
                        pg, rpg = get_ps(0, 7)
                        for kc in range(KC):
                            P.pe(lambda e, pa=pa, kc=kc, cc=cc: e.matmul(
                                pa[:, 0:TT], lhsT=av[:, kc, cc * 128:(cc + 1) * 128], rhs=hbuf[:, 0, kc, :],
                                start=(kc == 0), stop=(kc == KC - 1)), reads=[r_slot[sa], r_h[0][kc]], writes=[rpa])
                        for kc in range(KC):
                            P.pe(lambda e, pg=pg, kc=kc, cc=cc: e.matmul(
                                pg[:, 0:TT], lhsT=gv[:, kc, cc * 128:(cc + 1) * 128], rhs=hbuf[:, 0, kc, :],
                                start=(kc == 0), stop=(kc == KC - 1)), reads=[r_slot[si], r_h[0][kc]], writes=[rpg])
                        sg, rsg = get_tmp()
                        P.act(lambda e, sg=sg, pg=pg, ch=ch: e.activation(out=sg, in_=pg[:, 0:TT], func=AF.Sigmoid,
                                                                          bias=vcol(f"b_g{e_}", ch)),
                              reads=[rpg, r_vecs], writes=[rsg])
                        P.dve(lambda e, sg=sg, pa=pa, ch=ch: e.scalar_tensor_tensor(
                            out=hc[:, ch, 30:30 + TT], in0=pa[:, 0:TT], scalar=vcol(f"b_a{e_}", ch), in1=sg,
                            op0=ALU.add, op1=ALU.mult), reads=[rpa, rsg, r_vecs], writes=[r_hc[ch]])
                add_w(load_cols(w_in, 2048 + j * 256, 256, KC), c_a)
                add_w(load_cols(w_in, 3072 + j * 256, 256, KC), c_g)

            for c in range(8):
                def ld_diag(si, c=c):
                    dv = slot_view3(si, 32, 128)
                    for k in range(CONVW):
                        wcol = vecs[:, cwo + k * 8 + c:cwo + k * 8 + c + 1]
                        P.dve(lambda e, k=k, wcol=wcol: e.tensor_scalar(out=dv[:, k, :], in0=ident[:], scalar1=wcol,
                                                                        scalar2=None, op0=ALU.mult),
                              reads=[r_ident, r_vecs], writes=[r_slot[si]] if k in (0, CONVW - 1) else [])

                def c_conv(si, c=c):
                    dv = slot_view3(si, 32, 128)
                    pc, rpc = get_ps(0, 7)
                    for k in range(CONVW):
                        P.pe(lambda e, pc=pc, k=k: e.matmul(pc[:, 0:TT], lhsT=dv[:, k, :], rhs=hc[:, c, k:k + TT],
                                                            start=(k == 0), stop=(k == CONVW - 1)),
                             reads=[r_slot[si], r_hc[c]], writes=[rpc])
                    P.act(lambda e, pc=pc: e.activation(out=cv[:, c, :], in_=pc[:, 0:TT], func=AF.Identity,
                                                        bias=vcol(f"conv_b{e_}", c)),
                          reads=[rpc, r_vecs], writes=[r_cv[c]])
                    P.act(lambda e: e.activation(out=hc[:, c, 0:30], in_=hc[:, c, TT:TT + 30], func=AF.Copy),
                          reads=[r_hc[c]], writes=[r_hc[c]])
                    if c == 7:
                        conv_ln()
                add_w(ld_diag, c_conv)

            if tt == 0:
                add_c(even_setup_b(e_))

            def conv_ln():
                pm, rpm = ps[6], r_ps[6]
                for c in range(8):
                    P.pe(lambda e, c=c: e.matmul(pm[:, 0:TT], lhsT=ones_f[:], rhs=cv[:, c, :], start=(c == 0), stop=(c == 7)),
                         reads=[r_cv[c], r_ones], writes=[rpm])
                pq, rpq = ps[7], r_ps[7]
                for c in range(8):
                    sq, rsq = get_tmp()
                    P.act(lambda e, sq=sq, c=c: e.activation(out=sq, in_=cv[:, c, :], func=AF.Square),
                          reads=[r_cv[c]], writes=[rsq])
                    P.pe(lambda e, sq=sq, c=c: e.matmul(pq[:, 0:TT], lhsT=ones_f[:], rhs=sq, start=(c == 0), stop=(c == 7)),
                         reads=[rsq, r_ones], writes=[rpq])
                mean, rmean = get_tmp()
                P.dve(lambda e, mean=mean: e.tensor_scalar(out=mean, in0=pm[:, 0:TT], scalar1=1.0 / 1024, scalar2=None, op0=ALU.mult),
                      reads=[rpm], writes=[rmean])
                msq, rmsq = get_tmp()
                P.dve(lambda e, mean=mean, msq=msq: e.tensor_tensor(out=msq, in0=mean, in1=mean, op=ALU.mult),
                      reads=[rmean], writes=[rmsq])
                oi = rs_i[0] % 2
                rs_i[0] += 1
                rt, rrt = rstd_t[:, oi, :], r_rstd[oi]
                P.dve(lambda e, msq=msq: e.scalar_tensor_tensor(out=rt, in0=pq[:, 0:TT], scalar=1.0 / 1024, in1=msq,
                                                                op0=ALU.mult, op1=ALU.subtract),
                      reads=[rpq, rmsq], writes=[rrt])
                P.act(lambda e: e.activation(out=rt, in_=rt, func=AF.Sqrt, bias=EPS_AP(), scale=1.0),
                      reads=[rrt, r_eps], writes=[rrt])
                P.dve(lambda e: e.reciprocal(out=rt, in_=rt), reads=[rrt], writes=[rrt])
                for c in range(8):
                    P.dve(lambda e, c=c, mean=mean: e.tensor_tensor(out=cv[:, c, :], in0=cv[:, c, :], in1=mean, op=ALU.subtract),
                          reads=[r_cv[c], rmean], writes=[r_cv[c]])
                    P.dve(lambda e, c=c: e.tensor_tensor(out=cv[:, c, :], in0=cv[:, c, :], in1=rt, op=ALU.mult),
                          reads=[r_cv[c], rrt], writes=[r_cv[c]])
                    P.act(lambda e, c=c: e.activation(out=hbuf[:, 1, 8 + c, :], in_=cv[:, c, :], func=AF.Silu,
                                                      bias=vcol(f"cln_b{e_}", c), scale=vcol(f"cln_g{e_}", c)),
                          reads=[r_cv[c], r_vecs], writes=[r_h[1][8 + c]])

            for j in range(4):
                def c_v(si, j=j):
                    wv = slot_view3(si, KC, 256)
                    for blk in range(3):
                        pv, rpv = get_ps(0, 7)
                        for kc in range(KC):
                            P.pe(lambda e, pv=pv, kc=kc, blk=blk: e.matmul(
                                pv[:, 0:256], lhsT=hbuf[:, 0, kc, blk * 128:(blk + 1) * 128], rhs=wv[:, kc, :],
                                start=(kc == 0), stop=(kc == KC - 1)), reads=[r_slot[si], r_h[0][kc]], writes=[rpv])
                        P.dve(lambda e, pv=pv, blk=blk, j=j: e.tensor_tensor(
                            out=vt[:, blk, j * 256:(j + 1) * 256], in0=pv[:, 0:256], in1=bvb[:, j * 256:(j + 1) * 256],
                            op=ALU.add), reads=[rpv] + r_bvb,
                            writes=([r_vtb[blk]] + r_cv[(blk * 1024) // TT:((blk + 1) * 1024 - 1) // TT + 1]) if j == 0 else [r_vtb[blk]])
                    if j == 3:
                        v_post()
                add_w(load_cols(w_in, 1024 + j * 256, 256, KC), c_v)

            for j in range(4):
                def c_u(si, j=j):
                    wv = slot_view3(si, KC, 256)
                    for cc in range(2):
                        ch = 2 * j + cc
                        pu, rpu = get_ps(0, 7)
                        for kc in range(KC):
                            P.pe(lambda e, pu=pu, kc=kc, cc=cc: e.matmul(
                                pu[:, 0:TT], lhsT=wv[:, kc, cc * 128:(cc + 1) * 128], rhs=hbuf[:, 0, kc, :],
                                start=(kc == 0), stop=(kc == KC - 1)), reads=[r_slot[si], r_h[0][kc]], writes=[rpu])
                        P.act(lambda e, pu=pu, ch=ch: e.activation(out=hbuf[:, 1, ch, :], in_=pu[:, 0:TT], func=AF.Gelu_apprx_tanh,
                                                                   bias=vcol(f"b_u{e_}", ch)),
                              reads=[rpu, r_vecs], writes=[r_h[1][ch]])
                    if j == 3:
                        v_spatial()
                add_w(load_cols(w_in, j * 256, 256, KC), c_u)

            def v_post():
                B3 = range(3)
                st6 = [small[:, 8 + blk * 16:8 + blk * 16 + 12].rearrange("p (a b) -> p a b", a=2) for blk in B3]
                mv = [small[:, 8 + blk * 16 + 12:8 + blk * 16 + 14] for blk in B3]
                rs = [small[:, 8 + blk * 16 + 14:8 + blk * 16 + 15] for blk in B3]
                for blk in B3:
                    P.act(lambda e, blk=blk: e.activation(out=vt[:, blk, :], in_=vt[:, blk, :], func=AF.Gelu_apprx_tanh),
                          reads=[r_vtb[blk]], writes=[r_vtb[blk]])
                for hh in range(2):
                    for blk in B3:
                        P.dve(lambda e, blk=blk, hh=hh: e.bn_stats(out=st6[blk][:, hh, :], in_=vt[:, blk, hh * 512:(hh + 1) * 512]),
                              reads=[r_vtb[blk]], writes=[r_sm[2 + blk]] if hh == 1 else [])
                for blk in B3:
                    P.dve(lambda e, blk=blk: e.bn_aggr(out=mv[blk], in_=st6[blk].rearrange("p a b -> p (a b)")),
                          reads=[r_sm[2 + blk]], writes=[r_sm[2 + blk]])
                for blk in B3:
                    P.act(lambda e, blk=blk: e.activation(out=rs[blk], in_=mv[blk][:, 1:2], func=AF.Sqrt, bias=EPS_AP(), scale=1.0),
                          reads=[r_sm[2 + blk], r_eps], writes=[r_sm[2 + blk]])
                for blk in B3:
                    P.dve(lambda e, blk=blk: e.reciprocal(out=rs[blk], in_=rs[blk]), reads=[r_sm[2 + blk]], writes=[r_sm[2 + blk]])
                for blk in B3:
                    P.dve(lambda e, blk=blk: e.tensor_scalar(
                        out=vn[:, blk, :], in0=vt[:, blk, :], scalar1=mv[blk][:, 0:1], scalar2=rs[blk],
                        op0=ALU.subtract, op1=ALU.mult),
                        reads=[r_vtb[blk], r_sm[2 + blk]] + r_cv[(blk * 1024) // TT:((blk + 1) * 1024 - 1) // TT + 1], writes=[r_vn[blk]])
            def v_spatial():
                for blk in range(3):
                    for h in range(8):
                        pS, rpS = get_ps(0, 7)
                        P.pe(lambda e, pS=pS, blk=blk, h=h: e.matmul(
                            pS[:, 0:128], lhsT=vn[:, blk, h * 128:(h + 1) * 128], rhs=wsT[:, h, :], start=True, stop=True),
                            reads=[r_vn[blk]] + r_wsT, writes=[rpS])
                        tm, rtm = get_tmp()
                        P.dve(lambda e, pS=pS, tm=tm, h=h: e.scalar_tensor_tensor(
                            out=tm[:, 0:128], in0=pS[:, 0:128], scalar=vcol(f"gln_g{e_}", h), in1=Th[:, h, :],
                            op0=ALU.mult, op1=ALU.add), reads=[rpS, r_vecs] + r_Th, writes=[rtm])
                        ya = hbuf[:, 1, h, blk * 128:(blk + 1) * 128]
                        P.dve(lambda e, tm=tm, ya=ya: e.tensor_tensor(out=ya, in0=tm[:, 0:128], in1=ya, op=ALU.mult),
                              reads=[rtm, r_h[1][h]], writes=[r_h[1][h]])

            if before_out is not None:
                before_out()
            proj_add_tasks(w_out, 1, tt, bias_name=f"b_out{e_}", hooks=out_hooks)

        def pool_tasks(l, tt, before_mm=None):
            o_ = l // 2
            c0 = tt * TT
            r_hp = r_hc + r_cv
            r_pw = r_vn

            def pre():
                if tt == 0:
                    arena_barrier()
                    for kc in range(KC):
                        P.dve(lambda e, kc=kc: e.memset(hp[:, kc, 0:16], 0.0), writes=[r_hp[kc]])
                emit_norm(f"n_mix{l}", tt, lambda kc: hp[:, kc, 16:16 + TT], lambda kc: r_hp[kc % 16])
                W = HPW
                for kc in range(KC):
                    gi = kc // 4
                    src = hp[:, kc, :]
                    rsrc = r_hp[kc % 16]
                    cur = src
                    rcur = rsrc
                    sh = 1
                    for step in range(gi + 1):
                        dst = pwt[:, (step % 2) + 2 * (kc % 2), :]
                        rdst = r_pw[(step % 2 + 2 * (kc % 2)) % 3]
                        P.dve(lambda e, dst=dst, cur=cur, sh=sh: e.tensor_tensor(
                            out=dst[:, sh:W], in0=cur[:, sh:W], in1=cur[:, 0:W - sh], op=ALU.add),
                            reads=[rcur, rsrc], writes=[rdst])
                        cur, rcur = dst, rdst
                        sh *= 2
                    win = 2 ** (gi + 1)
                    if tt == 0:
                        P.dve(lambda e, cur=cur, gi=gi: e.tensor_tensor(
                            out=cur[:, 16:32], in0=cur[:, 16:32], in1=corr[:, gi * 16:(gi + 1) * 16], op=ALU.mult),
                            reads=[rcur, r_corr], writes=[rcur])
                    P.dve(lambda e, cur=cur, kc=kc, win=win: e.scalar_tensor_tensor(
                        out=hbuf[:, 1, kc, :], in0=cur[:, 16:16 + TT], scalar=1.0 / win, in1=hp[:, kc, 16:16 + TT],
                        op0=ALU.mult, op1=ALU.subtract), reads=[rcur, rsrc], writes=[r_h[1][kc]])
                    P.act(lambda e, kc=kc: e.activation(out=hp[:, kc, 0:16], in_=hp[:, kc, TT:TT + 16], func=AF.Copy),
                          reads=[rsrc], writes=[rsrc])
                if tt == 0:
                    P.dve(lambda e: e.tensor_tensor(out=small[:, 40:56], in0=vcol(f"pool_b{o_}", 0, 16),
                                                    in1=vcol(f"pool_s{o_}", 0, 16), op=ALU.mult),
                          reads=[r_vecs], writes=[r_sm[5]])
            add_c(pre)
            if before_mm is not None:
                before_mm()
            for gi in range(4):
                def c(si, gi=gi):
                    wv = slot_view3(si, 4, 512)
                    for oc in range(4):
                        dc = 4 * gi + oc
                        pd, rpd = get_ps(0, 7)
                        for k4 in range(4):
                            P.pe(lambda e, pd=pd, k4=k4, oc=oc: e.matmul(
                                pd[:, 0:TT], lhsT=wv[:, k4, oc * 128:(oc + 1) * 128], rhs=hbuf[:, 1, 4 * gi + k4, :],
                                start=(k4 == 0), stop=(k4 == 3)), reads=[r_slot[si], r_h[1][4 * gi + k4]], writes=[rpd])
                        tm, rtm = get_tmp()
                        P.act(lambda e, pd=pd, tm=tm, dc=dc: e.activation(
                            out=tm, in_=pd[:, 0:TT], func=AF.Identity, bias=small[:, 40 + dc:41 + dc],
                            scale=vcol(f"pool_s{o_}", dc)), reads=[rpd, r_sm[5], r_vecs], writes=[rtm])
                        xs = xres[:, dc, c0:c0 + TT]
                        P.dve(lambda e, tm=tm, xs=xs: e.tensor_tensor(out=xs, in0=tm, in1=xs, op=ALU.add),
                              reads=[rtm, r_x[dc][tt]], writes=[r_x[dc][tt]])

                def ld(si, gi=gi):
                    src = wd["pool_w"][o_, gi].rearrange("(kc p) f -> p kc f", p=128)
                    dst = slot_view3(si, 4, 512)
                    P.dma("pool", lambda e: e.dma_start(out=dst, in_=src), dsl[si], writes=[r_slot[si]])
                add_w(ld, c)

        r_kT = r_hc[0:4]
        r_vtok = r_hc[4:8]
        r_memT = r_cv[0:4]
        r_A = r_cv[4:8] + r_vn

        def xattn_kv_tasks(l, only=None):
            memT_hold = hbuf[:, 2, :, 0:256]

            def pre(mbs=(0, 1)):
                if 0 in mbs:
                    arena_barrier()
                for mb in mbs:
                    P.dma("sp", lambda e, mb=mb: e.dma_start(out=stg, in_=mem_d[mb * 128:(mb + 1) * 128, :]),
                          d_misc[1], writes=r_A)
                    ss = small[:, 4 + mb:5 + mb]
                    rss = r_sm[mb]
                    P.act(lambda e, ss=ss: e.activation(out=arena[:, 4096:6144], in_=stg, func=AF.Square, accum_out=ss),
                          reads=r_A, writes=r_memT + [rss])
                    P.act(lambda e, ss=ss: e.activation(out=ss, in_=ss, func=AF.Sqrt, bias=EPS_AP(), scale=1.0 / D),
                          reads=[rss, r_eps], writes=[rss])
                    P.dve(lambda e, ss=ss: e.reciprocal(out=ss, in_=ss), reads=[rss], writes=[rss])
                    P.dve(lambda e, ss=ss: e.tensor_scalar(out=stg, in0=stg, scalar1=ss, scalar2=None, op0=ALU.mult),
                          reads=r_A + [rss], writes=r_A)
                    for q in range(4):
                        pt, rpt = get_ps(0, 7)
                        for j in range(4):
                            kc = 4 * q + j
                            P.pe(lambda e, pt=pt, j=j, kc=kc: e.transpose(out=pt[:, j * 128:(j + 1) * 128],
                                                                          in_=stg[:, kc * 128:(kc + 1) * 128], identity=ident[:]),
                                 reads=r_A + [r_ident], writes=[rpt])
                        for j in range(4):
                            kc = 4 * q + j
                            P.dve(lambda e, pt=pt, j=j, kc=kc, mb=mb: e.tensor_scalar(
                                out=memT_hold[:, kc, mb * 128:(mb + 1) * 128], in0=pt[:, j * 128:(j + 1) * 128],
                                scalar1=vcol(f"n_xkv{l}", kc), scalar2=None, op0=ALU.mult),
                                reads=[rpt, r_vecs], writes=[r_h[2][kc]])
            if only == "pre":
                add_c(pre)
                return
            if only in ("pre0", "pre1"):
                add_c(lambda: pre((int(only[3]),)))
                return
            wk, wv_ = wd["xattn_wk"][l], wd["xattn_wv"][l]
            for j in range(8):
                def c_k(si, j=j):
                    wv = slot_view3(si, KC, 256)
                    for cc in range(2):
                        dc = 2 * j + cc
                        pk, rpk = get_ps(0, 7)
                        for kc in range(KC):
                            P.pe(lambda e, pk=pk, kc=kc, cc=cc: e.matmul(
                                pk[:, 0:256], lhsT=wv[:, kc, cc * 128:(cc + 1) * 128], rhs=memT_hold[:, kc, :],
                                start=(kc == 0), stop=(kc == KC - 1)), reads=[r_slot[si], r_h[2][kc]], writes=[rpk])
                        P.act(lambda e, pk=pk, dc=dc: e.activation(out=kT[:, dc, :], in_=pk[:, 0:256], func=AF.Copy),
                              reads=[rpk], writes=r_kT)
                add_w(load_cols(wk, j * 256, 256, KC), c_k)
            for j in range(8):
                def c_v(si, j=j):
                    wv = slot_view3(si, KC, 256)
                    for mb in range(2):
                        pv, rpv = get_ps(0, 7)
                        for kc in range(KC):
                            P.pe(lambda e, pv=pv, kc=kc, mb=mb: e.matmul(
                                pv[:, 0:256], lhsT=memT_hold[:, kc, mb * 128:(mb + 1) * 128], rhs=wv[:, kc, :],
                                start=(kc == 0), stop=(kc == KC - 1)), reads=[r_slot[si], r_h[2][kc]], writes=[rpv])
                        P.dve(lambda e, pv=pv, mb=mb, j=j: e.tensor_copy(out=vtok[:, mb, j * 256:(j + 1) * 256], in_=pv[:, 0:256]),
                              reads=[rpv], writes=r_vtok)
                add_w(load_cols(wv_, j * 256, 256, KC), c_v)

        def xattn_pre(l, tt):
            def pre():
                emit_norm(f"n_xq{l}", tt, lambda kc: hbuf[:, 0, kc, :], lambda kc: r_h[0][kc])
            add_c(pre)

        def xattn_tasks(l, tt, before_out=None):
            wq, wo = wd["xattn_wq"][l], wd["xattn_wo"][l]
            scl = 512.0 ** -0.5
            for j in range(8):
                def c_q(si, j=j):
                    wv = slot_view3(si, KC, 256)
                    for cc in range(2):
                        dc = 2 * j + cc
                        pq, rpq = get_ps(0, 7)
                        for kc in range(KC):
                            P.pe(lambda e, pq=pq, kc=kc, cc=cc: e.matmul(
                                pq[:, 0:TT], lhsT=wv[:, kc, cc * 128:(cc + 1) * 128], rhs=hbuf[:, 0, kc, :],
                                start=(kc == 0), stop=(kc == KC - 1)), reads=[r_slot[si], r_h[0][kc]], writes=[rpq])
                        P.act(lambda e, pq=pq, dc=dc: e.activation(out=hbuf[:, 1, dc, :], in_=pq[:, 0:TT], func=AF.Copy, scale=scl),
                              reads=[rpq], writes=[r_h[1][dc]])
                    if j % 2 == 1:
                        attn_head(j // 2)
                add_w(load_cols(wq, j * 256, 256, KC), c_q)

            def attn_head(hd):
                eb = hd % 2
                for mb in range(2):
                    pS, rpS = get_ps(0, 7)
                    for j in range(4):
                        P.pe(lambda e, pS=pS, j=j, mb=mb: e.matmul(
                            pS[:, 0:TT], lhsT=kT[:, 4 * hd + j, mb * 128:(mb + 1) * 128], rhs=hbuf[:, 1, 4 * hd + j, :],
                            start=(j == 0), stop=(j == 3)), reads=r_kT + [r_h[1][4 * hd + j]], writes=[rpS])
                    P.act(lambda e, pS=pS, mb=mb: e.activation(out=expT[:, eb, mb, :], in_=pS[:, 0:TT], func=AF.Exp),
                          reads=[rpS], writes=[r_A[eb * 2 + mb]])
                pden, rpden = get_ps(0, 7)
                for mb in range(2):
                    P.pe(lambda e, mb=mb: e.matmul(pden[:, 0:TT], lhsT=ones_b[:], rhs=expT[:, eb, mb, :],
                                                   start=(mb == 0), stop=(mb == 1)),
                         reads=[r_A[eb * 2 + mb], r_ones], writes=[rpden])
                rd, rrd = get_tmp()
                P.dve(lambda e, rd=rd: e.reciprocal(out=rd, in_=pden[:, 0:TT]), reads=[rpden], writes=[rrd])
                for j in range(4):
                    po, rpo = get_ps(0, 7)
                    for mb in range(2):
                        P.pe(lambda e, po=po, j=j, mb=mb: e.matmul(
                            po[:, 0:TT], lhsT=vtok[:, mb, (4 * hd + j) * 128:(4 * hd + j + 1) * 128], rhs=expT[:, eb, mb, :],
                            start=(mb == 0), stop=(mb == 1)), reads=r_vtok + [r_A[eb * 2 + mb]], writes=[rpo])
                    P.dve(lambda e, po=po, j=j, rd=rd: e.tensor_tensor(out=hbuf[:, 2, 4 * hd + j, :], in0=po[:, 0:TT], in1=rd, op=ALU.mult),
                          reads=[rpo, rrd], writes=[r_h[2][4 * hd + j]])
            if before_out is not None:
                before_out()
            proj_add_tasks(wo, 2, tt)

        def final_tasks():
            def f():
                r_ost = [[Res() for _ in range(4)] for _ in range(3)]
                arena_barrier([r for k in range(3) for r in r_ost[k]])
                for tt in range(NTT):
                    c0 = tt * TT
                    oi = rs_i[0] % 2
                    rs_i[0] += 1
                    if do_final:
                        rt, rrt = emit_rstd(lambda c, c0=c0: xres[:, c, c0:c0 + TT], [r_x[c][tt] for c in range(KC)], KC, TT, 1.0 / D, oi)
                    for bo in range(3):
                        blk = tt * 3 + bo
                        ob = blk % 3
                        ost = arena[:, ob * 2048:(ob + 1) * 2048]
                        for q in range(4):
                            pt, rpt = get_ps(0, 7)
                            for j in range(4):
                                kc = 4 * q + j
                                cs = slice(c0 + bo * 128, c0 + (bo + 1) * 128)
                                if do_final:
                                    yt, ryt = get_tmp()
                                    P.dve(lambda e, yt=yt, kc=kc, cs=cs, bo=bo, rt=rt: e.scalar_tensor_tensor(
                                        out=yt[:, 0:128], in0=xres[:, kc, cs], scalar=vcol("n_final", kc),
                                        in1=rt[:, bo * 128:(bo + 1) * 128], op0=ALU.mult, op1=ALU.mult),
                                        reads=[r_x[kc][tt], rrt, r_vecs], writes=[ryt])
                                    P.pe(lambda e, pt=pt, j=j, yt=yt: e.transpose(out=pt[:, j * 128:(j + 1) * 128], in_=yt[:, 0:128], identity=ident[:]),
                                         reads=[ryt, r_ident], writes=[rpt])
                                else:
                                    P.pe(lambda e, pt=pt, j=j, kc=kc, cs=cs: e.transpose(out=pt[:, j * 128:(j + 1) * 128], in_=xres[:, kc, cs], identity=ident[:]),
                                         reads=[r_x[kc][tt], r_ident], writes=[rpt])
                            if q % 2 == 0:
                                P.act(lambda e, pt=pt, q=q, ost=ost: e.activation(out=ost[:, q * 512:(q + 1) * 512], in_=pt[:], func=AF.Copy),
                                      reads=[rpt], writes=[r_ost[ob][q]])
                            else:
                                P.dve(lambda e, pt=pt, q=q, ost=ost: e.tensor_copy(out=ost[:, q * 512:(q + 1) * 512], in_=pt[:]),
                                      reads=[rpt], writes=[r_ost[ob][q]])
                        P.dma("sp", lambda e, blk=blk, ost=ost: e.dma_start(out=out_d[blk * 128:(blk + 1) * 128, :], in_=ost),
                              d_out[ob], reads=r_ost[ob])
            add_c(f)

        for l in range(n_layers):
            ffn_tasks(l, 1)
            if stop_stage == (l, "ffn1"):
                break
            kvpre = (lambda l=l: xattn_kv_tasks(l, only="pre"))
            if l % 2 == 0:
                add_c(even_setup(l // 2))
                even_pre(l, 0)
                for tt in range(NTT):
                    if tt < NTT - 1:
                        even_tasks(l, tt, before_out=(lambda l=l, tt=tt: even_pre(l, tt + 1)))
                    else:
                        even_tasks(l, tt, out_hooks={3: (lambda l=l: xattn_kv_tasks(l, only="pre0")),
                                                     5: (lambda l=l: xattn_kv_tasks(l, only="pre1"))})
            else:
                for tt in range(NTT):
                    pool_tasks(l, tt, before_mm=kvpre if tt == NTT - 1 else None)
            if stop_stage == (l, "mix"):
                break
            xattn_kv_tasks(l, only="w")
            xattn_pre(l, 0)
            for tt in range(NTT):
                xattn_tasks(l, tt, before_out=(lambda l=l, tt=tt: xattn_pre(l, tt + 1)) if tt < NTT - 1 else None)
            if stop_stage == (l, "xattn"):
                break
            ffn_tasks(l, 2)
        final_tasks()

        widx = [i for i, t in enumerate(tasks) if t[0] == "w"]
        nloaded = [0]

        def ensure_loaded(upto):
            while nloaded[0] < min(upto + 1, len(widx)):
                k = nloaded[0]
                tasks[widx[k]][1](k % NSLOT)
                nloaded[0] += 1

        wk_i = 0
        for t in tasks:
            if t[0] == "w":
                ensure_loaded(wk_i + NSLOT - 1 - LOOKBACK)
                t[2](wk_i % NSLOT)
                wk_i += 1
            else:
                ensure_loaded(wk_i + NSLOT - 2 - LOOKBACK)
                t[2]()

        P.finalize(sems, final_waits=lambda: [(d.h, d.count) for d in d_out])
    return nc


_NC_CACHE = {}


def make_in_maps(inputs):
    inp = {k: np.asarray(v) for k, v in inputs.items()}
    V = pack_vecs(inp)
    ident = np.eye(128, dtype=np.float32)
    b_in = inp["ab_b_in"]
    bvb = np.ascontiguousarray(np.broadcast_to(b_in[:, None, 1024:2048], (2, 128, 1024))).astype(np.float32)
    bsb = np.ascontiguousarray(np.broadcast_to(inp["gmlp_b_s"].reshape(2, 1, 1024), (2, 128, 1024))).astype(np.float32)
    shared = {"vecs": V, "ident": ident, "bvb": bvb, "bsb": bsb,
              "gmlp_w_s": np.ascontiguousarray(inp["gmlp_w_s"], dtype=np.float32)}
    for nm in ("ffn1_gate", "ffn1_up", "ffn1_down", "ffn2_gate", "ffn2_up", "ffn2_down", "ab_w_in", "ab_w_out",
               "pool_w", "xattn_wq", "xattn_wk", "xattn_wv", "xattn_wo"):
        shared[nm] = np.ascontiguousarray(inp[nm], dtype=np.float32)
    in_maps = []
    for c in range(8):
        b, half = divmod(c, 2)
        t0 = 0 if half == 0 else HALO0
        m = dict(shared)
        m["x"] = np.ascontiguousarray(inp["x"][b, t0:t0 + T, :], dtype=np.float32)
        m["mem"] = np.ascontiguousarray(inp["mem"][b], dtype=np.float32)
        corr = np.ones((128, 4, 16), np.float32)
        if half == 0:
            for gi, win in enumerate((2, 4, 8, 16)):
                for t in range(16):
                    corr[:, gi, t] = float(win) / float(min(t + 1, win))
        m["corr"] = corr.reshape(128, 64)
        in_maps.append(m)
    return in_maps


def assemble(results):
    out = np.empty((BATCH, SEQ, D), np.float32)
    for c in range(8):
        b, half = divmod(c, 2)
        y = results[c]["out"]
        if half == 0:
            out[b, 0:T, :] = y
        else:
            out[b, T:SEQ, :] = y[T - HALO0:, :]
    return out


def kernel(**inputs):
    key = "full"
    if key not in _NC_CACHE:
        _NC_CACHE[key] = build_nc()
    nc = _NC_CACHE[key]
    in_maps = make_in_maps(inputs)
    res = run_bass_kernel_spmd(nc, in_maps, core_ids=list(range(8)))
    return assemble(res.results)
```

```python
import numpy as np
from contextlib import ExitStack
import concourse.bass as bass
import concourse.mybir as mybir
from concourse.bass_utils import run_bass_kernel_spmd

F32 = mybir.dt.float32
BF16 = mybir.dt.bfloat16
AF = mybir.ActivationFunctionType
ALU = mybir.AluOpType

D = 2048
KC = 16
FF = 5632
FCH = 44
SEQ = 2048
BATCH = 4
NMEM = 256
DEPTH = 4
T = 1152
TT = 384
NTT = 3
HALO0 = 896
G = 2
NG = FCH // G
SLOT = 4096
NSLOT = 5
LOOKBACK = 1
EPS = 1e-6
CONVW = 31

ENGS = ("pe", "act", "dve", "pool", "sp")


class Res:
    __slots__ = ("name", "w", "rs")

    def __init__(self, name=""):
        self.name = name
        self.w = None
        self.rs = []


class Op:
    __slots__ = ("eng", "fn", "reads", "writes", "dsem", "deps", "sig", "ev")

    def __init__(self, eng, fn, reads, writes, dsem):
        self.eng = eng
        self.fn = fn
        self.reads = reads
        self.writes = writes
        self.dsem = dsem
        self.deps = set()
        self.sig = False
        self.ev = None


class DmaSem:
    def __init__(self, handle):
        self.h = handle
        self.count = 0


class Prog:
    def __init__(self, nc, same_engine_sync=True):
        self.nc = nc
        self.ops = []
        self.same_engine_sync = same_engine_sync

    def op(self, eng, fn, reads=(), writes=(), dsem=None):
        o = Op(eng, fn, tuple(reads), tuple(writes), dsem)
        self.ops.append(o)
        return o

    def pe(self, fn, reads=(), writes=()):
        return self.op("pe", fn, reads, writes)

    def act(self, fn, reads=(), writes=()):
        return self.op("act", fn, reads, writes)

    def dve(self, fn, reads=(), writes=()):
        return self.op("dve", fn, reads, writes)

    def dma(self, eng, fn, dsem, reads=(), writes=()):
        if not hasattr(dsem, "res"):
            dsem.res = Res("dsem")
        return self.op(eng, fn, reads, tuple(writes) + (dsem.res,), dsem)

    def finalize(self, sems, final_waits):
        ops = self.ops
        for i, o in enumerate(ops):
            for r in o.reads:
                if r.w is not None:
                    o.deps.add(r.w)
            for w in o.writes:
                if w.w is not None:
                    o.deps.add(w.w)
                for rr in w.rs:
                    o.deps.add(rr)
            for r in o.reads:
                r.rs.append(i)
            for w in o.writes:
                w.w = i
                w.rs = []
            o.deps.discard(i)
        for i, o in enumerate(ops):
            keep = set()
            for d in o.deps:
                p = ops[d]
                if p.eng == o.eng and p.dsem is None and o.dsem is None:
                    if o.eng == "pe":
                        continue
                    if not self.same_engine_sync:
                        continue
                keep.add(d)
            o.deps = keep
            for d in keep:
                ops[d].sig = True
        counts = {e: 0 for e in ENGS}
        for o in ops:
            if o.dsem is not None:
                o.dsem.count += 16
                o.ev = (o.dsem.h, o.dsem.count)
            elif o.sig:
                counts[o.eng] += 1
                o.ev = (sems[o.eng], counts[o.eng])
        self.counts = counts
        streams = {e: [] for e in ENGS}
        for o in ops:
            streams[o.eng].append(o)

        def emit(engh, lst, extra_final):
            waited = {}
            for o in lst:
                need = {}
                for d in o.deps:
                    sem, val = ops[d].ev
                    k = id(sem)
                    if k not in need or need[k][1] < val:
                        need[k] = (sem, val)
                for k, (sem, val) in need.items():
                    if waited.get(k, 0) >= val:
                        continue
                    engh.wait_ge(sem, val)
                    waited[k] = val
                inst = o.fn(engh)
                if o.dsem is not None:
                    inst.then_inc(o.dsem.h, 16)
                elif o.sig:
                    inst.then_inc(sems[o.eng], 1)
            for (sem, val) in extra_final:
                engh.wait_ge(sem, val)

        with self.nc.Block() as block:
            @block.tensor
            def _(e):
                emit(e, streams["pe"], [])

            @block.scalar
            def _(e):
                emit(e, streams["act"], [])

            @block.vector
            def _(e):
                emit(e, streams["dve"], [])

            @block.gpsimd
            def _(e):
                emit(e, streams["pool"], [])

            @block.sync
            def _(e):
                emit(e, streams["sp"], list(final_waits()))


def vec_layout():
    off = {}
    n = 0

    def add(name, cols):
        nonlocal n
        off[name] = n
        n += cols

    for l in range(DEPTH):
        for nm in ("n_ffn1", "n_mix", "n_xq", "n_xkv", "n_ffn2"):
            add(f"{nm}{l}", 16)
    for e in range(2):
        for nm, c in (("b_u", 8), ("b_a", 8), ("b_g", 8), ("gln_g", 8), ("gln_b", 8),
                      ("conv_b", 8), ("cln_g", 8), ("cln_b", 8), ("b_out", 16), ("conv_w", CONVW * 8)):
            add(f"{nm}{e}", c)
    for o in range(2):
        add(f"pool_b{o}", 16)
        add(f"pool_s{o}", 16)
    add("n_final", 16)
    return off, n


VOFF, NV = vec_layout()


def fm(v):
    v = np.asarray(v, np.float32).reshape(-1, 128)
    return np.ascontiguousarray(v.T)


def pack_vecs(inp):
    V = np.zeros((128, NV), np.float32)

    def put(name, arr):
        a = fm(arr)
        V[:, VOFF[name]:VOFF[name] + a.shape[1]] = a

    for l in range(DEPTH):
        put(f"n_ffn1{l}", inp["norm_ffn1"][l])
        put(f"n_mix{l}", inp["norm_mix"][l])
        put(f"n_xq{l}", inp["norm_xq"][l])
        put(f"n_xkv{l}", inp["norm_xkv"][l])
        put(f"n_ffn2{l}", inp["norm_ffn2"][l])
    for e in range(2):
        b_in = np.asarray(inp["ab_b_in"][e])
        put(f"b_u{e}", b_in[0:1024])
        put(f"b_a{e}", b_in[2048:3072])
        put(f"b_g{e}", b_in[3072:4096])
        put(f"gln_g{e}", inp["gmlp_ln_g"][e])
        put(f"gln_b{e}", inp["gmlp_ln_b"][e])
        put(f"conv_b{e}", inp["conv_b"][e])
        put(f"cln_g{e}", inp["conv_ln_g"][e])
        put(f"cln_b{e}", inp["conv_ln_b"][e])
        put(f"b_out{e}", inp["ab_b_out"][e])
        cw = np.asarray(inp["conv_w"][e]).reshape(CONVW, 8, 128)
        V[:, VOFF[f"conv_w{e}"]:VOFF[f"conv_w{e}"] + CONVW * 8] = cw.transpose(2, 0, 1).reshape(128, CONVW * 8)
    for o in range(2):
        put(f"pool_b{o}", np.asarray(inp["pool_b"][o]).reshape(-1))
        put(f"pool_s{o}", inp["pool_scale"][o])
    put("n_final", inp["norm_final"])
    return V


def build_nc(n_layers=DEPTH, do_final=True, stop_stage=None):
    nc = bass.Bass("TRN2", target_bir_lowering=False)

    def din(name, shape):
        return nc.dram_tensor(name, list(shape), F32, kind="ExternalInput").ap()

    x_d = din("x", (T, D))
    mem_d = din("mem", (NMEM, D))
    vec_d = din("vecs", (128, NV))
    ident_d = din("ident", (128, 128))
    corr_d = din("corr", (128, 64))
    bvb_d = din("bvb", (2, 128, 1024))
    bsb_d = din("bsb", (2, 128, 1024))
    ws_d = din("gmlp_w_s", (2, 8, 128, 128))
    wd = {}
    for nm, shp in (("ffn1_gate", (DEPTH, D, FF)), ("ffn1_up", (DEPTH, D, FF)), ("ffn1_down", (DEPTH, FF, D)),
                    ("ffn2_gate", (DEPTH, D, FF)), ("ffn2_up", (DEPTH, D, FF)), ("ffn2_down", (DEPTH, FF, D)),
                    ("ab_w_in", (2, D, 4096)), ("ab_w_out", (2, D, D)), ("pool_w", (2, 4, 512, 512)),
                    ("xattn_wq", (DEPTH, D, D)), ("xattn_wk", (DEPTH, D, D)),
                    ("xattn_wv", (DEPTH, D, D)), ("xattn_wo", (DEPTH, D, D))):
        wd[nm] = din(nm, shp)
    out_d = nc.dram_tensor("out", [T, D], F32, kind="ExternalOutput").ap()

    es = ExitStack()
    with es:
        E = es.enter_context

        def sb(name, shape, dt):
            return E(nc.sbuf_tensor(name, list(shape), dt))

        xres = sb("xres", (128, KC, T), F32)
        hbuf = sb("hbuf", (128, NTT, KC, TT), BF16)
        hid = sb("hid", (128, G, T), BF16)
        ring = sb("ring", (128, NSLOT, SLOT), BF16)
        arena = sb("arena", (128, 8192), F32)
        vecs = sb("vecs_sb", (128, NV), F32)
        ident = sb("ident_sb", (128, 128), F32)
        ones_f = sb("ones_f", (128, 128), F32)
        ones_b = sb("ones_b", (128, 128), BF16)
        NTMP = 4
        tmpf = sb("tmpf", (128, NTMP, TT), F32)
        tmpb = sb("tmpb", (128, NTMP, TT), BF16)
        rstd_t = sb("rstd_t", (128, 2, TT), F32)
        h2flat = hbuf[:, 2, :, :].rearrange("p c t -> p (c t)")
        bvb = h2flat[:, 0:2048].bitcast(F32)
        Th = h2flat[:, 2048:4096].bitcast(F32).rearrange("p (h q) -> p h q", h=8)
        wsT = h2flat[:, 4096:5120].rearrange("p (h q) -> p h q", h=8)
        corr = sb("corr_sb", (128, 64), F32)
        small = sb("small", (128, 64), F32)
        epsc = sb("epsc", (128, 1), F32)
        ps = [E(nc.psum_tensor(f"ps{i}", [128, 512], F32)) for i in range(8)]

        sems = {e: E(nc.semaphore(f"s_{e}")) for e in ("pe", "act", "dve", "pool")}
        dsl = [DmaSem(E(nc.semaphore(f"dslot{i}"))) for i in range(NSLOT)]
        d_misc = [DmaSem(E(nc.semaphore(f"dmisc{i}"))) for i in range(4)]
        d_out = [DmaSem(E(nc.semaphore(f"dout{i}"))) for i in range(3)]

        P = Prog(nc)

        r_x = [[Res(f"x{kc}_{tt}") for tt in range(NTT)] for kc in range(KC)]
        r_h = [[Res(f"h{s}_{kc}") for kc in range(KC)] for s in range(NTT)]
        r_hid = [[Res() for tt in range(NTT)] for fc in range(G)]
        r_slot = [Res(f"slot{i}") for i in range(NSLOT)]
        r_ps = [Res(f"ps{i}") for i in range(8)]
        r_tmp = [Res() for _ in range(NTMP)]
        r_tmpb = [Res() for _ in range(NTMP)]
        r_rstd = [Res(), Res()]
        r_vecs, r_ident, r_ones, r_corr = Res(), Res(), Res(), Res()
        r_small = Res()
        r_sm = [Res() for _ in range(8)]
        r_bvb = r_h[2][0:6]
        r_Th = r_h[2][5:11]
        r_wsT = r_h[2][10:14]
        r_hc = [Res() for _ in range(8)]
        r_cv = [Res() for _ in range(8)]
        r_vn = [Res() for _ in range(3)]
        r_stgx = [Res(), Res()]
        r_arena_all = r_hc + r_cv + r_vn + r_stgx
        r_vtb = [Res(), Res(), Res()]

        HCW = 30 + TT
        hc = arena[:, 0:4 * HCW].bitcast(BF16).rearrange("p (c w) -> p c w", c=8)
        cv = arena[:, 3312:3312 + 8 * TT].rearrange("p (c w) -> p c w", c=8)
        vt = arena[:, 3312:3312 + 3072].rearrange("p (b w) -> p b w", b=3)
        vn = arena[:, 6384:7920].bitcast(BF16).rearrange("p (b w) -> p b w", b=3)
        kT = arena[:, 0:2048].bitcast(BF16).rearrange("p (c m) -> p c m", c=KC)
        vtok = arena[:, 2048:4096].bitcast(BF16).rearrange("p (b d) -> p b d", b=2)
        memT = arena[:, 4096:6144].bitcast(BF16).rearrange("p (c m) -> p c m", c=KC)
        stg = arena[:, 6144:8192]
        expT = arena[:, 6144:6144 + 768].bitcast(BF16).rearrange("p (a b w) -> p a b w", a=2, b=2)
        wst = arena[:, 1656:1656 + 1024].rearrange("p (h q) -> p h q", h=8)
        r_wst = Res("wst")
        r_arena_all.append(r_wst)
        HPW = 16 + TT
        hp = arena[:, 0:16 * HPW].rearrange("p (c w) -> p c w", c=KC)
        pwt = arena[:, 6400:6400 + 4 * HPW].rearrange("p (c w) -> p c w", c=4)

        tmp_i = [0]

        def get_tmp():
            i = tmp_i[0] % NTMP
            tmp_i[0] += 1
            return tmpf[:, i, :], r_tmp[i]

        tmpb_i = [0]

        def get_tmpb():
            i = tmpb_i[0] % NTMP
            tmpb_i[0] += 1
            return tmpb[:, i, :], r_tmpb[i]

        ps_i = [0]

        def get_ps(lo=0, hi=8):
            n = hi - lo
            i = lo + (ps_i[0] % n)
            ps_i[0] += 1
            return ps[i], r_ps[i]

        r_bar = Res("bar")

        def arena_barrier(extra=()):
            P.dve(lambda e: e.memset(small[:, 63:64], 0.0), writes=r_arena_all + [r_bar] + list(extra))

        def vcol(name, c, n=1):
            o = VOFF[name] + c
            return vecs[:, o:o + n]

        P.dma("sp", lambda e: e.dma_start(out=vecs[:], in_=vec_d[:, :]), d_misc[0], writes=[r_vecs])
        P.dma("sp", lambda e: e.dma_start(out=ident[:], in_=ident_d[:, :]), d_misc[0], writes=[r_ident])
        P.dma("sp", lambda e: e.dma_start(out=corr[:], in_=corr_d[:, :]), d_misc[0], writes=[r_corr])
        P.dve(lambda e: e.memset(ones_f[:], 1.0), writes=[r_ones])
        P.dve(lambda e: e.memset(ones_b[:], 1.0), writes=[r_ones])

        stg2 = [arena[:, 6144:8192], arena[:, 4096:6144]]
        for blk in range(T // 128):
            tt, bo = divmod(blk, 3)
            sg_, rsg_ = stg2[blk % 2], r_stgx[blk % 2]
            P.dma("sp", lambda e, blk=blk, sg_=sg_: e.dma_start(out=sg_, in_=x_d[blk * 128:(blk + 1) * 128, :]),
                  d_misc[1 + 2 * (blk % 2)], writes=[rsg_])
            for q in range(4):
                pt, rpt = get_ps()
                for j in range(4):
                    kc = 4 * q + j
                    P.pe(lambda e, pt=pt, j=j, kc=kc, sg_=sg_: e.transpose(out=pt[:, j * 128:(j + 1) * 128],
                                                                           in_=sg_[:, kc * 128:(kc + 1) * 128], identity=ident[:]),
                         reads=[rsg_, r_ident], writes=[rpt])
                dst = xres[:, 4 * q:4 * q + 4, blk * 128:(blk + 1) * 128]
                src = pt[:].rearrange("p (c t) -> p c t", c=4)
                wr = [r_x[4 * q + j][tt] for j in range(4)]
                if q % 2 == 0:
                    P.act(lambda e, dst=dst, src=src: e.activation(out=dst, in_=src, func=AF.Copy), reads=[rpt], writes=wr)
                else:
                    P.dve(lambda e, dst=dst, src=src: e.tensor_copy(out=dst, in_=src), reads=[rpt], writes=wr)

        tasks = []

        def add_c(fn):
            tasks.append(("c", None, fn))

        def add_w(load, fn):
            tasks.append(("w", load, fn))

        def slot_view3(si, a, b):
            return ring[:, si, 0:a * b].rearrange("p (a b) -> p a b", a=a)

        def load_cols(wap, c0, ncols, kchunks):
            def f(si):
                src = wap.rearrange("(kc p) f -> p kc f", p=128)[:, :, c0:c0 + ncols]
                dst = slot_view3(si, kchunks, ncols)
                P.dma("pool", lambda e: e.dma_start(out=dst, in_=src), dsl[si], writes=[r_slot[si]])
            return f

        def load_rows(wap, r0, nrc):
            def f(si):
                src = wap[r0 * 128:(r0 + nrc) * 128, :].rearrange("(fc p) d -> p fc d", p=128)
                dst = slot_view3(si, nrc, D)
                P.dma("pool", lambda e: e.dma_start(out=dst, in_=src), dsl[si], writes=[r_slot[si]])
            return f

        def emit_rstd(xsrc, xres_list, nchunks, n, inv_n, out_idx):
            pst, rpst = ps[7], r_ps[7]
            for c in range(nchunks):
                sq, rsq = get_tmpb()
                P.act(lambda e, sq=sq, c=c: e.activation(out=sq[:, 0:n], in_=xsrc(c), func=AF.Square),
                      reads=[xres_list[c]], writes=[rsq])
                P.pe(lambda e, sq=sq, c=c: e.matmul(pst[:, 0:n], lhsT=ones_b[:], rhs=sq[:, 0:n],
                                                    start=(c == 0), stop=(c == nchunks - 1)),
                     reads=[rsq, r_ones], writes=[rpst])
            rt = rstd_t[:, out_idx, 0:n]
            P.act(lambda e: e.activation(out=rt, in_=pst[:, 0:n], func=AF.Sqrt, bias=EPS_AP(), scale=inv_n),
                  reads=[rpst, r_eps], writes=[r_rstd[out_idx]])
            P.dve(lambda e: e.reciprocal(out=rt, in_=rt), reads=[r_rstd[out_idx]], writes=[r_rstd[out_idx]])
            return rt, r_rstd[out_idx]

        r_eps = Res("eps")

        def EPS_AP():
            return epsc[:, 0:1]

        P.dve(lambda e: e.memset(epsc[:, 0:1], EPS), writes=[r_eps])

        rs_i = [0]

        def emit_norm(gname, tt, dst_fn, dst_res_fn):
            c0 = tt * TT
            oi = rs_i[0] % 2
            rs_i[0] += 1
            rt, rrt = emit_rstd(lambda c: xres[:, c, c0:c0 + TT], [r_x[c][tt] for c in range(KC)], KC, TT, 1.0 / D, oi)
            for kc in range(KC):
                P.dve(lambda e, kc=kc: e.scalar_tensor_tensor(out=dst_fn(kc), in0=xres[:, kc, c0:c0 + TT],
                                                              scalar=vcol(gname, kc), in1=rt,
                                                              op0=ALU.mult, op1=ALU.mult),
                      reads=[r_x[kc][tt], rrt, r_vecs], writes=[dst_res_fn(kc)])

        def ffn_tasks(l, which):
            wg, wu, wdn = wd[f"ffn{which}_gate"][l], wd[f"ffn{which}_up"][l], wd[f"ffn{which}_down"][l]
            gname = f"n_ffn{which}{l}"

            def norm_all():
                for tt in range(NTT):
                    emit_norm(gname, tt, lambda kc, tt=tt: hbuf[:, tt, kc, :], lambda kc, tt=tt: r_h[tt][kc])
            add_c(norm_all)
            st = {}
            for g in range(NG):
                def c_gate(si, st=st):
                    st["g"] = si

                def c_up(si, st=st):
                    sg = st["g"]
                    gv = slot_view3(sg, KC, G * 128)
                    uv = slot_view3(si, KC, G * 128)
                    for fc in range(G):
                        for tt in range(NTT):
                            pg, rpg = get_ps(0, 6)
                            pu, rpu = get_ps(0, 6)
                            for kc in range(KC):
                                P.pe(lambda e, pg=pg, kc=kc, fc=fc, tt=tt: e.matmul(
                                    pg[:, 0:TT], lhsT=gv[:, kc, fc * 128:(fc + 1) * 128], rhs=hbuf[:, tt, kc, :],
                                    start=(kc == 0), stop=(kc == KC - 1)),
                                    reads=[r_slot[sg], r_h[tt][kc]], writes=[rpg])
                            for kc in range(KC):
                                P.pe(lambda e, pu=pu, kc=kc, fc=fc, tt=tt: e.matmul(
                                    pu[:, 0:TT], lhsT=uv[:, kc, fc * 128:(fc + 1) * 128], rhs=hbuf[:, tt, kc, :],
                                    start=(kc == 0), stop=(kc == KC - 1)),
                                    reads=[r_slot[si], r_h[tt][kc]], writes=[rpu])
                            sl, rsl = get_tmp()
                            P.act(lambda e, sl=sl, pg=pg: e.activation(out=sl, in_=pg[:, 0:TT], func=AF.Silu),
                                  reads=[rpg], writes=[rsl])
                            P.dve(lambda e, sl=sl, pu=pu, fc=fc, tt=tt: e.tensor_tensor(
                                out=hid[:, fc, tt * TT:(tt + 1) * TT], in0=sl, in1=pu[:, 0:TT], op=ALU.mult),
                                reads=[rsl, rpu], writes=[r_hid[fc][tt]])

                def c_down(si):
                    dv = slot_view3(si, G, D)
                    for dc in range(KC):
                        for tt in range(NTT):
                            pd, rpd = get_ps(0, 6)
                            for fc in range(G):
                                P.pe(lambda e, pd=pd, fc=fc, dc=dc, tt=tt: e.matmul(
                                    pd[:, 0:TT], lhsT=dv[:, fc, dc * 128:(dc + 1) * 128],
                                    rhs=hid[:, fc, tt * TT:(tt + 1) * TT], start=(fc == 0), stop=(fc == G - 1)),
                                    reads=[r_slot[si], r_hid[fc][tt]], writes=[rpd])
                            xs = xres[:, dc, tt * TT:(tt + 1) * TT]
                            P.dve(lambda e, pd=pd, xs=xs: e.scalar_tensor_tensor(
                                out=xs, in0=pd[:, 0:TT], scalar=0.5, in1=xs, op0=ALU.mult, op1=ALU.add),
                                reads=[rpd, r_x[dc][tt]], writes=[r_x[dc][tt]])

                add_w(load_cols(wg, g * G * 128, G * 128, KC), c_gate)
                add_w(load_cols(wu, g * G * 128, G * 128, KC), c_up)
                add_w(load_rows(wdn, g * G, G), c_down)

        def proj_add_tasks(wap, src_sub, tt, bias_name=None, hooks=None):
            for j in range(8):
                if hooks and j in hooks:
                    hooks[j]()

                def c(si, j=j):
                    wv = slot_view3(si, KC, 256)
                    for cc in range(2):
                        dc = 2 * j + cc
                        pd, rpd = get_ps(0, 7)
                        for kc in range(KC):
                            P.pe(lambda e, pd=pd, kc=kc, cc=cc: e.matmul(
                                pd[:, 0:TT], lhsT=wv[:, kc, cc * 128:(cc + 1) * 128], rhs=hbuf[:, src_sub, kc, :],
                                start=(kc == 0), stop=(kc == KC - 1)),
                                reads=[r_slot[si], r_h[src_sub][kc]], writes=[rpd])
                        xs = xres[:, dc, tt * TT:(tt + 1) * TT]
                        if bias_name is None:
                            P.dve(lambda e, pd=pd, xs=xs: e.tensor_tensor(out=xs, in0=pd[:, 0:TT], in1=xs, op=ALU.add),
                                  reads=[rpd, r_x[dc][tt]], writes=[r_x[dc][tt]])
                        else:
                            P.dve(lambda e, pd=pd, xs=xs, dc=dc: e.scalar_tensor_tensor(
                                out=xs, in0=pd[:, 0:TT], scalar=vcol(bias_name, dc), in1=xs, op0=ALU.add, op1=ALU.add),
                                reads=[rpd, r_x[dc][tt], r_vecs], writes=[r_x[dc][tt]])
                add_w(load_cols(wap, j * 256, 256, KC), c)

        def even_setup(e_):
            def f():
                arena_barrier()
                for c in range(8):
                    P.dve(lambda e, c=c: e.memset(hc[:, c, 0:30], 0.0), writes=[r_hc[c]])
                P.dma("sp", lambda e: e.dma_start(out=wst, in_=ws_d[e_].rearrange("h p q -> p h q")), d_misc[3], writes=[r_wst])
            return f

        def even_setup_b(e_):
            def f():
                P.dma("sp", lambda e: e.dma_start(out=bvb, in_=bvb_d[e_]), d_misc[2], writes=r_bvb)
                P.dma("sp", lambda e: e.dma_start(out=Th.rearrange("p h q -> p (h q)"), in_=bsb_d[e_]),
                      d_misc[2], writes=r_Th)
                P.dve(lambda e: e.memset(wst[0:64, :, 64:128], 0.0), writes=[r_wst])
                for h0 in (0, 4):
                    pt, rpt = get_ps(0, 7)
                    for h in range(h0, h0 + 4):
                        P.pe(lambda e, pt=pt, h=h, h0=h0: e.transpose(out=pt[:, (h - h0) * 128:(h - h0 + 1) * 128], in_=wst[:, h, :], identity=ident[:]),
                             reads=[r_wst, r_ident], writes=[rpt])
                    P.act(lambda e, pt=pt, h0=h0: e.activation(out=wsT[:, h0:h0 + 4, :], in_=pt[:].rearrange("p (h q) -> p h q", h=4), func=AF.Copy),
                          reads=[rpt], writes=r_wsT)
                for h0 in (0, 4):
                    pr, rpr = get_ps(0, 7)
                    for h in range(h0, h0 + 4):
                        P.pe(lambda e, pr=pr, h=h, h0=h0: e.matmul(pr[:, (h - h0) * 128:(h - h0 + 1) * 128], lhsT=ones_b[:], rhs=wsT[:, h, :], start=True, stop=True),
                             reads=r_wsT + [r_ones], writes=[rpr])
                    for h in range(h0, h0 + 4):
                        P.dve(lambda e, pr=pr, h=h, h0=h0: e.scalar_tensor_tensor(
                            out=Th[:, h, :], in0=pr[:, (h - h0) * 128:(h - h0 + 1) * 128], scalar=vcol(f"gln_b{e_}", h), in1=Th[:, h, :],
                            op0=ALU.mult, op1=ALU.add), reads=[rpr, r_vecs] + r_Th, writes=r_Th)
            return f

        def even_pre(l, tt):
            def pre():
                emit_norm(f"n_mix{l}", tt, lambda kc: hbuf[:, 0, kc, :], lambda kc: r_h[0][kc])
            add_c(pre)

        def even_tasks(l, tt, before_out=None, out_hooks=None):
            e_ = l // 2
            w_in = wd["ab_w_in"][e_]
            w_out = wd["ab_w_out"][e_]
            cwo = VOFF[f"conv_w{e_}"]

            st = {}
            for j in range(4):
                def c_a(si, st=st):
                    st["a"] = si

                def c_g(si, j=j, st=st):
                    sa = st["a"]
                    av = slot_view3(sa, KC, 256)
                    gv = slot_view3(si, KC, 256)
                    for cc in range(2):
                        ch = 2 * j + cc
                        pa, rpa = get_ps(0, 7)
                        pg, rpg = get_ps(0, 7)
                        for kc in range(KC):
                            P.pe(lambda e, pa=pa, kc=kc, cc=cc: e.matmul(
                                pa[:, 0:TT], lhsT=av[:, kc, cc * 128:(cc + 1) * 128], rhs=hbuf[:, 0, kc, :],
                                start=(kc == 0), stop=(kc == KC - 1)), reads=[r_slot[sa], r_h[0][kc]], writes=[rpa])
                        for kc in range(KC):
                            P.pe(lambda e, pg=pg, kc=kc, cc=cc: e.matmul(
                                pg[:, 0:TT], lhsT=gv[:, kc, cc * 128:(cc + 1) * 128], rhs=hbuf[:, 0, kc, :],
                                start=(kc == 0), stop=(kc == KC - 1)), reads=[r_slot[si], r_h[0][kc]], writes=[rpg])
                        sg, rsg = get_tmp()
                        P.act(lambda e, sg=sg, pg=pg, ch=ch: e.activation(out=sg, in_=pg[:, 0:TT], func=AF.Sigmoid,
                                                                          bias=vcol(f"b_g{e_}", ch)),
                              reads=[rpg, r_vecs], writes=[rsg])
                        P.dve(lambda e, sg=sg, pa=pa, ch=ch: e.scalar_tensor_tensor(
                            out=hc[:, ch, 30:30 + TT], in0=pa[:, 0:TT], scalar=vcol(f"b_a{e_}", ch), in1=sg,
                            op0=ALU.add, op1=ALU.mult), reads=[rpa, rsg, r_vecs], writes=[r_hc[ch]])
                add_w(load_cols(w_in, 2048 + j * 256, 256, KC), c_a)
                add_w(load_cols(w_in, 3072 + j * 256, 256, KC), c_g)

            for c in range(8):
                def ld_diag(si, c=c):
                    dv = slot_view3(si, 32, 128)
                    for k in range(CONVW):
                        wcol = vecs[:, cwo + k * 8 + c:cwo + k * 8 + c + 1]
                        P.dve(lambda e, k=k, wcol=wcol: e.tensor_scalar(out=dv[:, k, :], in0=ident[:], scalar1=wcol,
                                                                        scalar2=None, op0=ALU.mult),
                              reads=[r_ident, r_vecs], writes=[r_slot[si]] if k in (0, CONVW - 1) else [])

                def c_conv(si, c=c):
                    dv = slot_view3(si, 32, 128)
                    pc, rpc = get_ps(0, 7)
                    for k in range(CONVW):
                        P.pe(lambda e, pc=pc, k=k: e.matmul(pc[:, 0:TT], lhsT=dv[:, k, :], rhs=hc[:, c, k:k + TT],
                                                            start=(k == 0), stop=(k == CONVW - 1)),
                             reads=[r_slot[si], r_hc[c]], writes=[rpc])
                    P.act(lambda e, pc=pc: e.activation(out=cv[:, c, :], in_=pc[:, 0:TT], func=AF.Identity,
                                                        bias=vcol(f"conv_b{e_}", c)),
                          reads=[rpc, r_vecs], writes=[r_cv[c]])
                    P.act(lambda e: e.activation(out=hc[:, c, 0:30], in_=hc[:, c, TT:TT + 30], func=AF.Copy),
                          reads=[r_hc[c]], writes=[r_hc[c]])
                    if c == 7:
                        conv_ln()
                add_w(ld_diag, c_conv)

            if tt == 0:
                add_c(even_setup_b(e_))

            def conv_ln():
                pm, rpm = ps[6], r_ps[6]
                for c in range(8):
                    P.pe(lambda e, c=c: e.matmul(pm[:, 0:TT], lhsT=ones_f[:], rhs=cv[:, c, :], start=(c == 0), stop=(c == 7)),
                         reads=[r_cv[c], r_ones], writes=[rpm])
                pq, rpq = ps[7], r_ps[7]
                for c in range(8):
                    sq, rsq = get_tmp()
                    P.act(lambda e, sq=sq, c=c: e.activation(out=sq, in_=cv[:, c, :], func=AF.Square),
                          reads=[r_cv[c]], writes=[rsq])
                    P.pe(lambda e, sq=sq, c=c: e.matmul(pq[:, 0:TT], lhsT=ones_f[:], rhs=sq, start=(c == 0), stop=(c == 7)),
                         reads=[rsq, r_ones], writes=[rpq])
                mean, rmean = get_tmp()
                P.dve(lambda e, mean=mean: e.tensor_scalar(out=mean, in0=pm[:, 0:TT], scalar1=1.0 / 1024, scalar2=None, op0=ALU.mult),
                      reads=[rpm], writes=[rmean])
                msq, rmsq = get_tmp()
                P.dve(lambda e, mean=mean, msq=msq: e.tensor_tensor(out=msq, in0=mean, in1=mean, op=ALU.mult),
                      reads=[rmean], writes=[rmsq])
                oi = rs_i[0] % 2
                rs_i[0] += 1
                rt, rrt = rstd_t[:, oi, :], r_rstd[oi]
                P.dve(lambda e, msq=msq: e.scalar_tensor_tensor(out=rt, in0=pq[:, 0:TT], scalar=1.0 / 1024, in1=msq,
                                                                op0=ALU.mult, op1=ALU.subtract),
                      reads=[rpq, rmsq], writes=[rrt])
                P.act(lambda e: e.activation(out=rt, in_=rt, func=AF.Sqrt, bias=EPS_AP(), scale=1.0),
                      reads=[rrt, r_eps], writes=[rrt])
                P.dve(lambda e: e.reciprocal(out=rt, in_=rt), reads=[rrt], writes=[rrt])
                for c in range(8):
                    P.dve(lambda e, c=c, mean=mean: e.tensor_tensor(out=cv[:, c, :], in0=cv[:, c, :], in1=mean, op=ALU.subtract),
                          reads=[r_cv[c], rmean], writes=[r_cv[c]])
                    P.dve(lambda e, c=c: e.tensor_tensor(out=cv[:, c, :], in0=cv[:, c, :], in1=rt, op=ALU.mult),
                          reads=[r_cv[c], rrt], writes=[r_cv[c]])
                    P.act(lambda e, c=c: e.activation(out=hbuf[:, 1, 8 + c, :], in_=cv[:, c, :], func=AF.Silu,
                                                      bias=vcol(f"cln_b{e_}", c), scale=vcol(f"cln_g{e_}", c)),
                          reads=[r_cv[c], r_vecs], writes=[r_h[1][8 + c]])

            for j in range(4):
                def c_v(si, j=j):
                    wv = slot_view3(si, KC, 256)
                    for blk in range(3):
                        pv, rpv = get_ps(0, 7)
                        for kc in range(KC):
                            P.pe(lambda e, pv=pv, kc=kc, blk=blk: e.matmul(
                                pv[:, 0:256], lhsT=hbuf[:, 0, kc, blk * 128:(blk + 1) * 128], rhs=wv[:, kc, :],
                                start=(kc == 0), stop=(kc == KC - 1)), reads=[r_slot[si], r_h[0][kc]], writes=[rpv])
                        P.dve(lambda e, pv=pv, blk=blk, j=j: e.tensor_tensor(
                            out=vt[:, blk, j * 256:(j + 1) * 256], in0=pv[:, 0:256], in1=bvb[:, j * 256:(j + 1) * 256],
                            op=ALU.add), reads=[rpv] + r_bvb,
                            writes=([r_vtb[blk]] + r_cv[(blk * 1024) // TT:((blk + 1) * 1024 - 1) // TT + 1]) if j == 0 else [r_vtb[blk]])
                    if j == 3:
                        v_post()
                add_w(load_cols(w_in, 1024 + j * 256, 256, KC), c_v)

            for j in range(4):
                def c_u(si, j=j):
                    wv = slot_view3(si, KC, 256)
                    for cc in range(2):
                        ch = 2 * j + cc
                        pu, rpu = get_ps(0, 7)
                        for kc in range(KC):
                            P.pe(lambda e, pu=pu, kc=kc, cc=cc: e.matmul(
                                pu[:, 0:TT], lhsT=wv[:, kc, cc * 128:(cc + 1) * 128], rhs=hbuf[:, 0, kc, :],
                                start=(kc == 0), stop=(kc == KC - 1)), reads=[r_slot[si], r_h[0][kc]], writes=[rpu])
                        P.act(lambda e, pu=pu, ch=ch: e.activation(out=hbuf[:, 1, ch, :], in_=pu[:, 0:TT], func=AF.Gelu_apprx_tanh,
                                                                   bias=vcol(f"b_u{e_}", ch)),
                              reads=[rpu, r_vecs], writes=[r_h[1][ch]])
                    if j == 3:
                        v_spatial()
                add_w(load_cols(w_in, j * 256, 256, KC), c_u)

            def v_post():
                B3 = range(3)
                st6 = [small[:, 8 + blk * 16:8 + blk * 16 + 12].rearrange("p (a b) -> p a b", a=2) for blk in B3]
                mv = [small[:, 8 + blk * 16 + 12:8 + blk * 16 + 14] for blk in B3]
                rs = [small[:, 8 + blk * 16 + 14:8 + blk * 16 + 15] for blk in B3]
                for blk in B3:
                    P.act(lambda e, blk=blk: e.activation(out=vt[:, blk, :], in_=vt[:, blk, :], func=AF.Gelu_apprx_tanh),
                          reads=[r_vtb[blk]], writes=[r_vtb[blk]])
                for hh in range(2):
                    for blk in B3:
                        P.dve(lambda e, blk=blk, hh=hh: e.bn_stats(out=st6[blk][:, hh, :], in_=vt[:, blk, hh * 512:(hh + 1) * 512]),
                              reads=[r_vtb[blk]], writes=[r_sm[2 + blk]] if hh == 1 else [])
                for blk in B3:
                    P.dve(lambda e, blk=blk: e.bn_aggr(out=mv[blk], in_=st6[blk].rearrange("p a b -> p (a b)")),
                          reads=[r_sm[2 + blk]], writes=[r_sm[2 + blk]])
                for blk in B3:
                    P.act(lambda e, blk=blk: e.activation(out=rs[blk], in_=mv[blk][:, 1:2], func=AF.Sqrt, bias=EPS_AP(), scale=1.0),
                          reads=[r_sm[2 + blk], r_eps], writes=[r_sm[2 + blk]])
                for blk in B3:
                    P.dve(lambda e, blk=blk: e.reciprocal(out=rs[blk], in_=rs[blk]), reads=[r_sm[2 + blk]], writes=[r_sm[2 + blk]])
                for blk in B3:
                    P.dve(lambda e, blk=blk: e.tensor_scalar(
                        out=vn[:, blk, :], in0=vt[:, blk, :], scalar1=mv[blk][:, 0:1], scalar2=rs[blk],
                        op0=ALU.subtract, op1=ALU.mult),
                        reads=[r_vtb[blk], r_sm[2 + blk]] + r_cv[(blk * 1024) // TT:((blk + 1) * 1024 - 1) // TT + 1], writes=[r_vn[blk]])
            def v_spatial():
                for blk in range(3):
                    for h in range(8):
                        pS, rpS = get_ps(0, 7)
                        P.pe(lambda e, pS=pS, blk=blk, h=h: e.matmul(
                            pS[:, 0:128], lhsT=vn[:, blk, h * 128:(h + 1) * 128], rhs=wsT[:, h, :], start=True, stop=True),
                            reads=[r_vn[blk]] + r_wsT, writes=[rpS])
                        tm, rtm = get_tmp()
                        P.dve(lambda e, pS=pS, tm=tm, h=h: e.scalar_tensor_tensor(
                            out=tm[:, 0:128], in0=pS[:, 0:128], scalar=vcol(f"gln_g{e_}", h), in1=Th[:, h, :],
                            op0=ALU.mult, op1=ALU.add), reads=[rpS, r_vecs] + r_Th, writes=[rtm])
                        ya = hbuf[:, 1, h, blk * 128:(blk + 1) * 128]
                        P.dve(lambda e, tm=tm, ya=ya: e.tensor_tensor(out=ya, in0=tm[:, 0:128], in1=ya, op=ALU.mult),
                              reads=[rtm, r_h[1][h]], writes=[r_h[1][h]])

            if before_out is not None:
                before_out()
            proj_add_tasks(w_out, 1, tt, bias_name=f"b_out{e_}", hooks=out_hooks)

        def pool_tasks(l, tt, before_mm=None):
            o_ = l // 2
            c0 = tt * TT
            r_hp = r_hc + r_cv
            r_pw = r_vn

            def pre():
                if tt == 0:
                    arena_barrier()
                    for kc in range(KC):
                        P.dve(lambda e, kc=kc: e.memset(hp[:, kc, 0:16], 0.0), writes=[r_hp[kc]])
                emit_norm(f"n_mix{l}", tt, lambda kc: hp[:, kc, 16:16 + TT], lambda kc: r_hp[kc % 16])
                W = HPW
                for kc in range(KC):
                    gi = kc // 4
                    src = hp[:, kc, :]
                    rsrc = r_hp[kc % 16]
                    cur = src
                    rcur = rsrc
                    sh = 1
                    for step in range(gi + 1):
                        dst = pwt[:, (step % 2) + 2 * (kc % 2), :]
                        rdst = r_pw[(step % 2 + 2 * (kc % 2)) % 3]
                        P.dve(lambda e, dst=dst, cur=cur, sh=sh: e.tensor_tensor(
                            out=dst[:, sh:W], in0=cur[:, sh:W], in1=cur[:, 0:W - sh], op=ALU.add),
                            reads=[rcur, rsrc], writes=[rdst])
                        cur, rcur = dst, rdst
                        sh *= 2
                    win = 2 ** (gi + 1)
                    if tt == 0:
                        P.dve(lambda e, cur=cur, gi=gi: e.tensor_tensor(
                            out=cur[:, 16:32], in0=cur[:, 16:32], in1=corr[:, gi * 16:(gi + 1) * 16], op=ALU.mult),
                            reads=[rcur, r_corr], writes=[rcur])
                    P.dve(lambda e, cur=cur, kc=kc, win=win: e.scalar_tensor_tensor(
                        out=hbuf[:, 1, kc, :], in0=cur[:, 16:16 + TT], scalar=1.0 / win, in1=hp[:, kc, 16:16 + TT],
                        op0=ALU.mult, op1=ALU.subtract), reads=[rcur, rsrc], writes=[r_h[1][kc]])
                    P.act(lambda e, kc=kc: e.activation(out=hp[:, kc, 0:16], in_=hp[:, kc, TT:TT + 16], func=AF.Copy),
                          reads=[rsrc], writes=[rsrc])
                if tt == 0:
                    P.dve(lambda e: e.tensor_tensor(out=small[:, 40:56], in0=vcol(f"pool_b{o_}", 0, 16),
                                                    in1=vcol(f"pool_s{o_}", 0, 16), op=ALU.mult),
                          reads=[r_vecs], writes=[r_sm[5]])
            add_c(pre)
            if before_mm is not None:
                before_mm()
            for gi in range(4):
                def c(si, gi=gi):
                    wv = slot_view3(si, 4, 512)
                    for oc in range(4):
                        dc = 4 * gi + oc
                        pd, rpd = get_ps(0, 7)
                        for k4 in range(4):
                            P.pe(lambda e, pd=pd, k4=k4, oc=oc: e.matmul(
                                pd[:, 0:TT], lhsT=wv[:, k4, oc * 128:(oc + 1) * 128], rhs=hbuf[:, 1, 4 * gi + k4, :],
                                start=(k4 == 0), stop=(k4 == 3)), reads=[r_slot[si], r_h[1][4 * gi + k4]], writes=[rpd])
                        tm, rtm = get_tmp()
                        P.act(lambda e, pd=pd, tm=tm, dc=dc: e.activation(
                            out=tm, in_=pd[:, 0:TT], func=AF.Identity, bias=small[:, 40 + dc:41 + dc],
                            scale=vcol(f"pool_s{o_}", dc)), reads=[rpd, r_sm[5], r_vecs], writes=[rtm])
                        xs = xres[:, dc, c0:c0 + TT]
                        P.dve(lambda e, tm=tm, xs=xs: e.tensor_tensor(out=xs, in0=tm, in1=xs, op=ALU.add),
                              reads=[rtm, r_x[dc][tt]], writes=[r_x[dc][tt]])

                def ld(si, gi=gi):
                    src = wd["pool_w"][o_, gi].rearrange("(kc p) f -> p kc f", p=128)
                    dst = slot_view3(si, 4, 512)
                    P.dma("pool", lambda e: e.dma_start(out=dst, in_=src), dsl[si], writes=[r_slot[si]])
                add_w(ld, c)

        r_kT = r_hc[0:4]
        r_vtok = r_hc[4:8]
        r_memT = r_cv[0:4]
        r_A = r_cv[4:8] + r_vn

        def xattn_kv_tasks(l, only=None):
            memT_hold = hbuf[:, 2, :, 0:256]

            def pre(mbs=(0, 1)):
                if 0 in mbs:
                    arena_barrier()
                for mb in mbs:
                    P.dma("sp", lambda e, mb=mb: e.dma_start(out=stg, in_=mem_d[mb * 128:(mb + 1) * 128, :]),
                          d_misc[1], writes=r_A)
                    ss = small[:, 4 + mb:5 + mb]
                    rss = r_sm[mb]
                    P.act(lambda e, ss=ss: e.activation(out=arena[:, 4096:6144], in_=stg, func=AF.Square, accum_out=ss),
                          reads=r_A, writes=r_memT + [rss])
                    P.act(lambda e, ss=ss: e.activation(out=ss, in_=ss, func=AF.Sqrt, bias=EPS_AP(), scale=1.0 / D),
                          reads=[rss, r_eps], writes=[rss])
                    P.dve(lambda e, ss=ss: e.reciprocal(out=ss, in_=ss), reads=[rss], writes=[rss])
                    P.dve(lambda e, ss=ss: e.tensor_scalar(out=stg, in0=stg, scalar1=ss, scalar2=None, op0=ALU.mult),
                          reads=r_A + [rss], writes=r_A)
                    for q in range(4):
                        pt, rpt = get_ps(0, 7)
                        for j in range(4):
                            kc = 4 * q + j
                            P.pe(lambda e, pt=pt, j=j, kc=kc: e.transpose(out=pt[:, j * 128:(j + 1) * 128],
                                                                          in_=stg[:, kc * 128:(kc + 1) * 128], identity=ident[:]),
                                 reads=r_A + [r_ident], writes=[rpt])
                        for j in range(4):
                            kc = 4 * q + j
                            P.dve(lambda e, pt=pt, j=j, kc=kc, mb=mb: e.tensor_scalar(
                                out=memT_hold[:, kc, mb * 128:(mb + 1) * 128], in0=pt[:, j * 128:(j + 1) * 128],
                                scalar1=vcol(f"n_xkv{l}", kc), scalar2=None, op0=ALU.mult),
                                reads=[rpt, r_vecs], writes=[r_h[2][kc]])
            if only == "pre":
                add_c(pre)
                return
            if only in ("pre0", "pre1"):
                add_c(lambda: pre((int(only[3]),)))
                return
            wk, wv_ = wd["xattn_wk"][l], wd["xattn_wv"][l]
            for j in range(8):
                def c_k(si, j=j):
                    wv = slot_view3(si, KC, 256)
                    for cc in range(2):
                        dc = 2 * j + cc
                        pk, rpk = get_ps(0, 7)
                        for kc in range(KC):
                            P.pe(lambda e, pk=pk, kc=kc, cc=cc: e.matmul(
                                pk[:, 0:256], lhsT=wv[:, kc, cc * 128:(cc + 1) * 128], rhs=memT_hold[:, kc, :],
                                start=(kc == 0), stop=(kc == KC - 1)), reads=[r_slot[si], r_h[2][kc]], writes=[rpk])
                        P.act(lambda e, pk=pk, dc=dc: e.activation(out=kT[:, dc, :], in_=pk[:, 0:256], func=AF.Copy),
                              reads=[rpk], writes=r_kT)
                add_w(load_cols(wk, j * 256, 256, KC), c_k)
            for j in range(8):
                def c_v(si, j=j):
                    wv = slot_view3(si, KC, 256)
                    for mb in range(2):
                        pv, rpv = get_ps(0, 7)
                        for kc in range(KC):
                            P.pe(lambda e, pv=pv, kc=kc, mb=mb: e.matmul(
                                pv[:, 0:256], lhsT=memT_hold[:, kc, mb * 128:(mb + 1) * 128], rhs=wv[:, kc, :],
                                start=(kc == 0), stop=(kc == KC - 1)), reads=[r_slot[si], r_h[2][kc]], writes=[rpv])
                        P.dve(lambda e, pv=pv, mb=mb, j=j: e.tensor_copy(out=vtok[:, mb, j * 256:(j + 1) * 256], in_=pv[:, 0:256]),
                              reads=[rpv], writes=r_vtok)
                add_w(load_cols(wv_, j * 256, 256, KC), c_v)

        def xattn_pre(l, tt):
            def pre():
                emit_norm(f"n_xq{l}", tt, lambda kc: hbuf[:, 0, kc, :], lambda kc: r_h[0][kc])
            add_c(pre)

        def xattn_tasks(l, tt, before_out=None):
            wq, wo = wd["xattn_wq"][l], wd["xattn_wo"][l]
            scl = 512.0 ** -0.5
            for j in range(8):
                def c_q(si, j=j):
                    wv = slot_view3(si, KC, 256)
                    for cc in range(2):
                        dc = 2 * j + cc
                        pq, rpq = get_ps(0, 7)
                        for kc in range(KC):
                            P.pe(lambda e, pq=pq, kc=kc, cc=cc: e.matmul(
                                pq[:, 0:TT], lhsT=wv[:, kc, cc * 128:(cc + 1) * 128], rhs=hbuf[:, 0, kc, :],
                                start=(kc == 0), stop=(kc == KC - 1)), reads=[r_slot[si], r_h[0][kc]], writes=[rpq])
                        P.act(lambda e, pq=pq, dc=dc: e.activation(out=hbuf[:, 1, dc, :], in_=pq[:, 0:TT], func=AF.Copy, scale=scl),
                              reads=[rpq], writes=[r_h[1][dc]])
                    if j % 2 == 1:
                        attn_head(j // 2)
                add_w(load_cols(wq, j * 256, 256, KC), c_q)

            def attn_head(hd):
                eb = hd % 2
                for mb in range(2):
                    pS, rpS = get_ps(0, 7)
                    for j in range(4):
                        P.pe(lambda e, pS=pS, j=j, mb=mb: e.matmul(
                            pS[:, 0:TT], lhsT=kT[:, 4 * hd + j, mb * 128:(mb + 1) * 128], rhs=hbuf[:, 1, 4 * hd + j, :],
                            start=(j == 0), stop=(j == 3)), reads=r_kT + [r_h[1][4 * hd + j]], writes=[rpS])
                    P.act(lambda e, pS=pS, mb=mb: e.activation(out=expT[:, eb, mb, :], in_=pS[:, 0:TT], func=AF.Exp),
                          reads=[rpS], writes=[r_A[eb * 2 + mb]])
                pden, rpden = get_ps(0, 7)
                for mb in range(2):
                    P.pe(lambda e, mb=mb: e.matmul(pden[:, 0:TT], lhsT=ones_b[:], rhs=expT[:, eb, mb, :],
                                                   start=(mb == 0), stop=(mb == 1)),
                         reads=[r_A[eb * 2 + mb], r_ones], writes=[rpden])
                rd, rrd = get_tmp()
                P.dve(lambda e, rd=rd: e.reciprocal(out=rd, in_=pden[:, 0:TT]), reads=[rpden], writes=[rrd])
                for j in range(4):
                    po, rpo = get_ps(0, 7)
                    for mb in range(2):
                        P.pe(lambda e, po=po, j=j, mb=mb: e.matmul(
                            po[:, 0:TT], lhsT=vtok[:, mb, (4 * hd + j) * 128:(4 * hd + j + 1) * 128], rhs=expT[:, eb, mb, :],
                            start=(mb == 0), stop=(mb == 1)), reads=r_vtok + [r_A[eb * 2 + mb]], writes=[rpo])
                    P.dve(lambda e, po=po, j=j, rd=rd: e.tensor_tensor(out=hbuf[:, 2, 4 * hd + j, :], in0=po[:, 0:TT], in1=rd, op=ALU.mult),
                          reads=[rpo, rrd], writes=[r_h[2][4 * hd + j]])
            if before_out is not None:
                before_out()
            proj_add_tasks(wo, 2, tt)

        def final_tasks():
            def f():
                r_ost = [[Res() for _ in range(4)] for _ in range(3)]
                arena_barrier([r for k in range(3) for r in r_ost[k]])
                for tt in range(NTT):
                    c0 = tt * TT
                    oi = rs_i[0] % 2
                    rs_i[0] += 1
                    if do_final:
                        rt, rrt = emit_rstd(lambda c, c0=c0: xres[:, c, c0:c0 + TT], [r_x[c][tt] for c in range(KC)], KC, TT, 1.0 / D, oi)
                    for bo in range(3):
                        blk = tt * 3 + bo
                        ob = blk % 3
                        ost = arena[:, ob * 2048:(ob + 1) * 2048]
                        for q in range(4):
                            pt, rpt = get_ps(0, 7)
                            for j in range(4):
                                kc = 4 * q + j
                                cs = slice(c0 + bo * 128, c0 + (bo + 1) * 128)
                                if do_final:
                                    yt, ryt = get_tmp()
                                    P.dve(lambda e, yt=yt, kc=kc, cs=cs, bo=bo, rt=rt: e.scalar_tensor_tensor(
                                        out=yt[:, 0:128], in0=xres[:, kc, cs], scalar=vcol("n_final", kc),
                                        in1=rt[:, bo * 128:(bo + 1) * 128], op0=ALU.mult, op1=ALU.mult),
                                        reads=[r_x[kc][tt], rrt, r_vecs], writes=[ryt])
                                    P.pe(lambda e, pt=pt, j=j, yt=yt: e.transpose(out=pt[:, j * 128:(j + 1) * 128], in_=yt[:, 0:128], identity=ident[:]),
                                         reads=[ryt, r_ident], writes=[rpt])
                                else:
                                    P.pe(lambda e, pt=pt, j=j, kc=kc, cs=cs: e.transpose(out=pt[:, j * 128:(j + 1) * 128], in_=xres[:, kc, cs], identity=ident[:]),
                                         reads=[r_x[kc][tt], r_ident], writes=[rpt])
                            if q % 2 == 0:
                                P.act(lambda e, pt=pt, q=q, ost=ost: e.activation(out=ost[:, q * 512:(q + 1) * 512], in_=pt[:], func=AF.Copy),
                                      reads=[rpt], writes=[r_ost[ob][q]])
                            else:
                                P.dve(lambda e, pt=pt, q=q, ost=ost: e.tensor_copy(out=ost[:, q * 512:(q + 1) * 512], in_=pt[:]),
                                      reads=[rpt], writes=[r_ost[ob][q]])
                        P.dma("sp", lambda e, blk=blk, ost=ost: e.dma_start(out=out_d[blk * 128:(blk + 1) * 128, :], in_=ost),
                              d_out[ob], reads=r_ost[ob])
            add_c(f)

        for l in range(n_layers):
            ffn_tasks(l, 1)
            if stop_stage == (l, "ffn1"):
                break
            kvpre = (lambda l=l: xattn_kv_tasks(l, only="pre"))
            if l % 2 == 0:
                add_c(even_setup(l // 2))
                even_pre(l, 0)
                for tt in range(NTT):
                    if tt < NTT - 1:
                        even_tasks(l, tt, before_out=(lambda l=l, tt=tt: even_pre(l, tt + 1)))
                    else:
                        even_tasks(l, tt, out_hooks={5: (lambda l=l: xattn_kv_tasks(l, only="pre0")),
                                                     7: (lambda l=l: xattn_kv_tasks(l, only="pre1"))})
            else:
                for tt in range(NTT):
                    pool_tasks(l, tt, before_mm=kvpre if tt == NTT - 1 else None)
            if stop_stage == (l, "mix"):
                break
            xattn_kv_tasks(l, only="w")
            xattn_pre(l, 0)
            for tt in range(NTT):
                xattn_tasks(l, tt, before_out=(lambda l=l, tt=tt: xattn_pre(l, tt + 1)) if tt < NTT - 1 else None)
            if stop_stage == (l, "xattn"):
                break
            ffn_tasks(l, 2)
        final_tasks()

        widx = [i for i, t in enumerate(tasks) if t[0] == "w"]
        nloaded = [0]

        def ensure_loaded(upto):
            while nloaded[0] < min(upto + 1, len(widx)):
                k = nloaded[0]
                tasks[widx[k]][1](k % NSLOT)
                nloaded[0] += 1

        wk_i = 0
        for t in tasks:
            if t[0] == "w":
                ensure_loaded(wk_i + NSLOT - 1 - LOOKBACK)
                t[2](wk_i % NSLOT)
                wk_i += 1
            else:
                ensure_loaded(wk_i + NSLOT - 2 - LOOKBACK)
                t[2]()

        P.finalize(sems, final_waits=lambda: [(d.h, d.count) for d in d_out])
    return nc


_NC_CACHE = {}


def make_in_maps(inputs):
    inp = {k: np.asarray(v) for k, v in inputs.items()}
    V = pack_vecs(inp)
    ident = np.eye(128, dtype=np.float32)
    b_in = inp["ab_b_in"]
    bvb = np.ascontiguousarray(np.broadcast_to(b_in[:, None, 1024:2048], (2, 128, 1024))).astype(np.float32)
    bsb = np.ascontiguousarray(np.broadcast_to(inp["gmlp_b_s"].reshape(2, 1, 1024), (2, 128, 1024))).astype(np.float32)
    shared = {"vecs": V, "ident": ident, "bvb": bvb, "bsb": bsb,
              "gmlp_w_s": np.ascontiguousarray(inp["gmlp_w_s"], dtype=np.float32)}
    for nm in ("ffn1_gate", "ffn1_up", "ffn1_down", "ffn2_gate", "ffn2_up", "ffn2_down", "ab_w_in", "ab_w_out",
               "pool_w", "xattn_wq", "xattn_wk", "xattn_wv", "xattn_wo"):
        shared[nm] = np.ascontiguousarray(inp[nm], dtype=np.float32)
    in_maps = []
    for c in range(8):
        b, half = divmod(c, 2)
        t0 = 0 if half == 0 else HALO0
        m = dict(shared)
        m["x"] = np.ascontiguousarray(inp["x"][b, t0:t0 + T, :], dtype=np.float32)
        m["mem"] = np.ascontiguousarray(inp["mem"][b], dtype=np.float32)
        corr = np.ones((128, 4, 16), np.float32)
        if half == 0:
            for gi, win in enumerate((2, 4, 8, 16)):
                for t in range(16):
                    corr[:, gi, t] = float(win) / float(min(t + 1, win))
        m["corr"] = corr.reshape(128, 64)
        in_maps.append(m)
    return in_maps


def assemble(results):
    out = np.empty((BATCH, SEQ, D), np.float32)
    for c in range(8):
        b, half = divmod(c, 2)
        y = results[c]["out"]
        if half == 0:
            out[b, 0:T, :] = y
        else:
            out[b, T:SEQ, :] = y[T - HALO0:, :]
    return out


def kernel(**inputs):
    key = "full"
    if key not in _NC_CACHE:
        _NC_CACHE[key] = build_nc()
    nc = _NC_CACHE[key]
    in_maps = make_in_maps(inputs)
    res = run_bass_kernel_spmd(nc, in_maps, core_ids=list(range(8)))
    return assemble(res.results)
```

```python
import numpy as np
from contextlib import ExitStack
import concourse.bass as bass
import concourse.mybir as mybir
from concourse.bass_utils import run_bass_kernel_spmd

F32 = mybir.dt.float32
BF16 = mybir.dt.bfloat16
AF = mybir.ActivationFunctionType
ALU = mybir.AluOpType

D = 2048
KC = 16
FF = 5632
FCH = 44
SEQ = 2048
BATCH = 4
NMEM = 256
DEPTH = 4
T = 1152
TT = 384
NTT = 3
HALO0 = 896
G = 2
NG = FCH // G
SLOT = 4096
NSLOT = 5
LOOKBACK = 1
EPS = 1e-6
CONVW = 31

ENGS = ("pe", "act", "dve", "pool", "sp")


class Res:
    __slots__ = ("name", "w", "rs")

    def __init__(self, name=""):
        self.name = name
        self.w = None
        self.rs = []


class Op:
    __slots__ = ("eng", "fn", "reads", "writes", "dsem", "deps", "sig", "ev")

    def __init__(self, eng, fn, reads, writes, dsem):
        self.eng = eng
        self.fn = fn
        self.reads = reads
        self.writes = writes
        self.dsem = dsem
        self.deps = set()
        self.sig = False
        self.ev = None


class DmaSem:
    def __init__(self, handle):
        self.h = handle
        self.count = 0


class Prog:
    def __init__(self, nc, same_engine_sync=True):
        self.nc = nc
        self.ops = []
        self.same_engine_sync = same_engine_sync

    def op(self, eng, fn, reads=(), writes=(), dsem=None):
        o = Op(eng, fn, tuple(reads), tuple(writes), dsem)
        self.ops.append(o)
        return o

    def pe(self, fn, reads=(), writes=()):
        return self.op("pe", fn, reads, writes)

    def act(self, fn, reads=(), writes=()):
        return self.op("act", fn, reads, writes)

    def dve(self, fn, reads=(), writes=()):
        return self.op("dve", fn, reads, writes)

    def dma(self, eng, fn, dsem, reads=(), writes=()):
        if not hasattr(dsem, "res"):
            dsem.res = Res("dsem")
        return self.op(eng, fn, reads, tuple(writes) + (dsem.res,), dsem)

    def finalize(self, sems, final_waits):
        ops = self.ops
        for i, o in enumerate(ops):
            for r in o.reads:
                if r.w is not None:
                    o.deps.add(r.w)
            for w in o.writes:
                if w.w is not None:
                    o.deps.add(w.w)
                for rr in w.rs:
                    o.deps.add(rr)
            for r in o.reads:
                r.rs.append(i)
            for w in o.writes:
                w.w = i
                w.rs = []
            o.deps.discard(i)
        for i, o in enumerate(ops):
            keep = set()
            for d in o.deps:
                p = ops[d]
                if p.eng == o.eng and p.dsem is None and o.dsem is None:
                    if o.eng == "pe":
                        continue
                    if not self.same_engine_sync:
                        continue
                keep.add(d)
            o.deps = keep
            for d in keep:
                ops[d].sig = True
        counts = {e: 0 for e in ENGS}
        for o in ops:
            if o.dsem is not None:
                o.dsem.count += 16
                o.ev = (o.dsem.h, o.dsem.count)
            elif o.sig:
                counts[o.eng] += 1
                o.ev = (sems[o.eng], counts[o.eng])
        self.counts = counts
        streams = {e: [] for e in ENGS}
        for o in ops:
            streams[o.eng].append(o)

        def emit(engh, lst, extra_final):
            waited = {}
            for o in lst:
                need = {}
                for d in o.deps:
                    sem, val = ops[d].ev
                    k = id(sem)
                    if k not in need or need[k][1] < val:
                        need[k] = (sem, val)
                for k, (sem, val) in need.items():
                    if waited.get(k, 0) >= val:
                        continue
                    engh.wait_ge(sem, val)
                    waited[k] = val
                inst = o.fn(engh)
                if o.dsem is not None:
                    inst.then_inc(o.dsem.h, 16)
                elif o.sig:
                    inst.then_inc(sems[o.eng], 1)
            for (sem, val) in extra_final:
                engh.wait_ge(sem, val)

        with self.nc.Block() as block:
            @block.tensor
            def _(e):
                emit(e, streams["pe"], [])

            @block.scalar
            def _(e):
                emit(e, streams["act"], [])

            @block.vector
            def _(e):
                emit(e, streams["dve"], [])

            @block.gpsimd
            def _(e):
                emit(e, streams["pool"], [])

            @block.sync
            def _(e):
                emit(e, streams["sp"], list(final_waits()))


def vec_layout():
    off = {}
    n = 0

    def add(name, cols):
        nonlocal n
        off[name] = n
        n += cols

    for l in range(DEPTH):
        for nm in ("n_ffn1", "n_mix", "n_xq", "n_xkv", "n_ffn2"):
            add(f"{nm}{l}", 16)
    for e in range(2):
        for nm, c in (("b_u", 8), ("b_a", 8), ("b_g", 8), ("gln_g", 8), ("gln_b", 8),
                      ("conv_b", 8), ("cln_g", 8), ("cln_b", 8), ("b_out", 16), ("conv_w", CONVW * 8)):
            add(f"{nm}{e}", c)
    for o in range(2):
        add(f"pool_b{o}", 16)
        add(f"pool_s{o}", 16)
    add("n_final", 16)
    return off, n


VOFF, NV = vec_layout()


def fm(v):
    v = np.asarray(v, np.float32).reshape(-1, 128)
    return np.ascontiguousarray(v.T)


def pack_vecs(inp):
    V = np.zeros((128, NV), np.float32)

    def put(name, arr):
        a = fm(arr)
        V[:, VOFF[name]:VOFF[name] + a.shape[1]] = a

    for l in range(DEPTH):
        put(f"n_ffn1{l}", inp["norm_ffn1"][l])
        put(f"n_mix{l}", inp["norm_mix"][l])
        put(f"n_xq{l}", inp["norm_xq"][l])
        put(f"n_xkv{l}", inp["norm_xkv"][l])
        put(f"n_ffn2{l}", inp["norm_ffn2"][l])
    for e in range(2):
        b_in = np.asarray(inp["ab_b_in"][e])
        put(f"b_u{e}", b_in[0:1024])
        put(f"b_a{e}", b_in[2048:3072])
        put(f"b_g{e}", b_in[3072:4096])
        put(f"gln_g{e}", inp["gmlp_ln_g"][e])
        put(f"gln_b{e}", inp["gmlp_ln_b"][e])
        put(f"conv_b{e}", inp["conv_b"][e])
        put(f"cln_g{e}", inp["conv_ln_g"][e])
        put(f"cln_b{e}", inp["conv_ln_b"][e])
        put(f"b_out{e}", inp["ab_b_out"][e])
        cw = np.asarray(inp["conv_w"][e]).reshape(CONVW, 8, 128)
        V[:, VOFF[f"conv_w{e}"]:VOFF[f"conv_w{e}"] + CONVW * 8] = cw.transpose(2, 0, 1).reshape(128, CONVW * 8)
    for o in range(2):
        put(f"pool_b{o}", np.asarray(inp["pool_b"][o]).reshape(-1))
        put(f"pool_s{o}", inp["pool_scale"][o])
    put("n_final", inp["norm_final"])
    return V


def build_nc(n_layers=DEPTH, do_final=True, stop_stage=None):
    nc = bass.Bass("TRN2", target_bir_lowering=False)

    def din(name, shape):
        return nc.dram_tensor(name, list(shape), F32, kind="ExternalInput").ap()

    x_d = din("x", (T, D))
    mem_d = din("mem", (NMEM, D))
    vec_d = din("vecs", (128, NV))
    ident_d = din("ident", (128, 128))
    corr_d = din("corr", (128, 64))
    bvb_d = din("bvb", (2, 128, 1024))
    bsb_d = din("bsb", (2, 128, 1024))
    ws_d = din("gmlp_w_s", (2, 8, 128, 128))
    wd = {}
    for nm, shp in (("ffn1_gate", (DEPTH, D, FF)), ("ffn1_up", (DEPTH, D, FF)), ("ffn1_down", (DEPTH, FF, D)),
                    ("ffn2_gate", (DEPTH, D, FF)), ("ffn2_up", (DEPTH, D, FF)), ("ffn2_down", (DEPTH, FF, D)),
                    ("ab_w_in", (2, D, 4096)), ("ab_w_out", (2, D, D)), ("pool_w", (2, 4, 512, 512)),
                    ("xattn_wq", (DEPTH, D, D)), ("xattn_wk", (DEPTH, D, D)),
                    ("xattn_wv", (DEPTH, D, D)), ("xattn_wo", (DEPTH, D, D))):
        wd[nm] = din(nm, shp)
    out_d = nc.dram_tensor("out", [T, D], F32, kind="ExternalOutput").ap()

    es = ExitStack()
    with es:
        E = es.enter_context

        def sb(name, shape, dt):
            return E(nc.sbuf_tensor(name, list(shape), dt))

        xres = sb("xres", (128, KC, T), F32)
        hbuf = sb("hbuf", (128, NTT, KC, TT), BF16)
        hid = sb("hid", (128, 2 * G, T), BF16)
        ring = sb("ring", (128, NSLOT, SLOT), BF16)
        arena = sb("arena", (128, 8192), F32)
        vecs = sb("vecs_sb", (128, NV), F32)
        ident = sb("ident_sb", (128, 128), F32)
        ones_f = sb("ones_f", (128, 128), F32)
        ones_b = sb("ones_b", (128, 128), BF16)
        NTMP = 4
        tmpf = sb("tmpf", (128, NTMP, TT), F32)
        tmpb = sb("tmpb", (128, NTMP, TT), BF16)
        rstd_t = sb("rstd_t", (128, 2, TT), F32)
        h2flat = hbuf[:, 2, :, :].rearrange("p c t -> p (c t)")
        bvb = h2flat[:, 0:2048].bitcast(F32)
        Th = h2flat[:, 2048:4096].bitcast(F32).rearrange("p (h q) -> p h q", h=8)
        wsT = h2flat[:, 4096:5120].rearrange("p (h q) -> p h q", h=8)
        corr = sb("corr_sb", (128, 64), F32)
        small = sb("small", (128, 64), F32)
        epsc = sb("epsc", (128, 1), F32)
        ps = [E(nc.psum_tensor(f"ps{i}", [128, 512], F32)) for i in range(8)]

        sems = {e: E(nc.semaphore(f"s_{e}")) for e in ("pe", "act", "dve", "pool")}
        dsl = [DmaSem(E(nc.semaphore(f"dslot{i}"))) for i in range(NSLOT)]
        d_misc = [DmaSem(E(nc.semaphore(f"dmisc{i}"))) for i in range(4)]
        d_out = [DmaSem(E(nc.semaphore(f"dout{i}"))) for i in range(3)]

        P = Prog(nc)

        r_x = [[Res(f"x{kc}_{tt}") for tt in range(NTT)] for kc in range(KC)]
        r_h = [[Res(f"h{s}_{kc}") for kc in range(KC)] for s in range(NTT)]
        r_hid = [[Res() for tt in range(NTT)] for fc in range(2 * G)]
        r_slot = [Res(f"slot{i}") for i in range(NSLOT)]
        r_ps = [Res(f"ps{i}") for i in range(8)]
        r_tmp = [Res() for _ in range(NTMP)]
        r_tmpb = [Res() for _ in range(NTMP)]
        r_rstd = [Res(), Res()]
        r_vecs, r_ident, r_ones, r_corr = Res(), Res(), Res(), Res()
        r_small = Res()
        r_sm = [Res() for _ in range(8)]
        r_bvb = r_h[2][0:6]
        r_Th = r_h[2][5:11]
        r_wsT = r_h[2][10:14]
        r_hc = [Res() for _ in range(8)]
        r_cv = [Res() for _ in range(8)]
        r_vn = [Res() for _ in range(3)]
        r_stgx = [Res(), Res()]
        r_arena_all = r_hc + r_cv + r_vn + r_stgx
        r_vtb = [Res(), Res(), Res()]

        HCW = 30 + TT
        hc = arena[:, 0:4 * HCW].bitcast(BF16).rearrange("p (c w) -> p c w", c=8)
        cv = arena[:, 3312:3312 + 8 * TT].rearrange("p (c w) -> p c w", c=8)
        vt = arena[:, 3312:3312 + 3072].rearrange("p (b w) -> p b w", b=3)
        vn = arena[:, 6384:7920].bitcast(BF16).rearrange("p (b w) -> p b w", b=3)
        kT = arena[:, 0:2048].bitcast(BF16).rearrange("p (c m) -> p c m", c=KC)
        vtok = arena[:, 2048:4096].bitcast(BF16).rearrange("p (b d) -> p b d", b=2)
        memT = arena[:, 4096:6144].bitcast(BF16).rearrange("p (c m) -> p c m", c=KC)
        stg = arena[:, 6144:8192]
        expT = arena[:, 6144:6144 + 768].bitcast(BF16).rearrange("p (a b w) -> p a b w", a=2, b=2)
        wst = arena[:, 1656:1656 + 1024].rearrange("p (h q) -> p h q", h=8)
        r_wst = Res("wst")
        r_arena_all.append(r_wst)
        HPW = 16 + TT
        hp = arena[:, 0:16 * HPW].rearrange("p (c w) -> p c w", c=KC)
        pwt = arena[:, 6400:6400 + 4 * HPW].rearrange("p (c w) -> p c w", c=4)

        tmp_i = [0]

        def get_tmp():
            i = tmp_i[0] % NTMP
            tmp_i[0] += 1
            return tmpf[:, i, :], r_tmp[i]

        tmpb_i = [0]

        def get_tmpb():
            i = tmpb_i[0] % NTMP
            tmpb_i[0] += 1
            return tmpb[:, i, :], r_tmpb[i]

        ps_i = [0]

        def get_ps(lo=0, hi=8):
            n = hi - lo
            i = lo + (ps_i[0] % n)
            ps_i[0] += 1
            return ps[i], r_ps[i]

        r_bar = Res("bar")

        def arena_barrier(extra=()):
            P.dve(lambda e: e.memset(small[:, 63:64], 0.0), writes=r_arena_all + [r_bar] + list(extra))

        def vcol(name, c, n=1):
            o = VOFF[name] + c
            return vecs[:, o:o + n]

        P.dma("sp", lambda e: e.dma_start(out=vecs[:], in_=vec_d[:, :]), d_misc[0], writes=[r_vecs])
        P.dma("sp", lambda e: e.dma_start(out=ident[:], in_=ident_d[:, :]), d_misc[0], writes=[r_ident])
        P.dma("sp", lambda e: e.dma_start(out=corr[:], in_=corr_d[:, :]), d_misc[0], writes=[r_corr])
        P.dve(lambda e: e.memset(ones_f[:], 1.0), writes=[r_ones])
        P.dve(lambda e: e.memset(ones_b[:], 1.0), writes=[r_ones])

        stg2 = [arena[:, 6144:8192], arena[:, 4096:6144]]
        for blk in range(T // 128):
            tt, bo = divmod(blk, 3)
            sg_, rsg_ = stg2[blk % 2], r_stgx[blk % 2]
            P.dma("sp", lambda e, blk=blk, sg_=sg_: e.dma_start(out=sg_, in_=x_d[blk * 128:(blk + 1) * 128, :]),
                  d_misc[1 + 2 * (blk % 2)], writes=[rsg_])
            for q in range(4):
                pt, rpt = get_ps()
                for j in range(4):
                    kc = 4 * q + j
                    P.pe(lambda e, pt=pt, j=j, kc=kc, sg_=sg_: e.transpose(out=pt[:, j * 128:(j + 1) * 128],
                                                                           in_=sg_[:, kc * 128:(kc + 1) * 128], identity=ident[:]),
                         reads=[rsg_, r_ident], writes=[rpt])
                dst = xres[:, 4 * q:4 * q + 4, blk * 128:(blk + 1) * 128]
                src = pt[:].rearrange("p (c t) -> p c t", c=4)
                wr = [r_x[4 * q + j][tt] for j in range(4)]
                if q % 2 == 0:
                    P.act(lambda e, dst=dst, src=src: e.activation(out=dst, in_=src, func=AF.Copy), reads=[rpt], writes=wr)
                else:
                    P.dve(lambda e, dst=dst, src=src: e.tensor_copy(out=dst, in_=src), reads=[rpt], writes=wr)

        tasks = []

        def add_c(fn):
            tasks.append(("c", None, fn))

        def add_w(load, fn):
            tasks.append(("w", load, fn))

        def slot_view3(si, a, b):
            return ring[:, si, 0:a * b].rearrange("p (a b) -> p a b", a=a)

        def load_cols(wap, c0, ncols, kchunks):
            def f(si):
                src = wap.rearrange("(kc p) f -> p kc f", p=128)[:, :, c0:c0 + ncols]
                dst = slot_view3(si, kchunks, ncols)
                P.dma("pool", lambda e: e.dma_start(out=dst, in_=src), dsl[si], writes=[r_slot[si]])
            return f

        def load_rows(wap, r0, nrc):
            def f(si):
                src = wap[r0 * 128:(r0 + nrc) * 128, :].rearrange("(fc p) d -> p fc d", p=128)
                dst = slot_view3(si, nrc, D)
                P.dma("pool", lambda e: e.dma_start(out=dst, in_=src), dsl[si], writes=[r_slot[si]])
            return f

        def emit_rstd(xsrc, xres_list, nchunks, n, inv_n, out_idx):
            pst, rpst = ps[7], r_ps[7]
            for c in range(nchunks):
                sq, rsq = get_tmpb()
                P.act(lambda e, sq=sq, c=c: e.activation(out=sq[:, 0:n], in_=xsrc(c), func=AF.Square),
                      reads=[xres_list[c]], writes=[rsq])
                P.pe(lambda e, sq=sq, c=c: e.matmul(pst[:, 0:n], lhsT=ones_b[:], rhs=sq[:, 0:n],
                                                    start=(c == 0), stop=(c == nchunks - 1)),
                     reads=[rsq, r_ones], writes=[rpst])
            rt = rstd_t[:, out_idx, 0:n]
            P.act(lambda e: e.activation(out=rt, in_=pst[:, 0:n], func=AF.Sqrt, bias=EPS_AP(), scale=inv_n),
                  reads=[rpst, r_eps], writes=[r_rstd[out_idx]])
            P.dve(lambda e: e.reciprocal(out=rt, in_=rt), reads=[r_rstd[out_idx]], writes=[r_rstd[out_idx]])
            return rt, r_rstd[out_idx]

        r_eps = Res("eps")

        def EPS_AP():
            return epsc[:, 0:1]

        P.dve(lambda e: e.memset(epsc[:, 0:1], EPS), writes=[r_eps])

        rs_i = [0]

        def emit_norm(gname, tt, dst_fn, dst_res_fn):
            c0 = tt * TT
            oi = rs_i[0] % 2
            rs_i[0] += 1
            rt, rrt = emit_rstd(lambda c: xres[:, c, c0:c0 + TT], [r_x[c][tt] for c in range(KC)], KC, TT, 1.0 / D, oi)
            for kc in range(KC):
                P.dve(lambda e, kc=kc: e.scalar_tensor_tensor(out=dst_fn(kc), in0=xres[:, kc, c0:c0 + TT],
                                                              scalar=vcol(gname, kc), in1=rt,
                                                              op0=ALU.mult, op1=ALU.mult),
                      reads=[r_x[kc][tt], rrt, r_vecs], writes=[dst_res_fn(kc)])

        def ffn_tasks(l, which):
            wg, wu, wdn = wd[f"ffn{which}_gate"][l], wd[f"ffn{which}_up"][l], wd[f"ffn{which}_down"][l]
            gname = f"n_ffn{which}{l}"

            def norm_all():
                for tt in range(NTT):
                    emit_norm(gname, tt, lambda kc, tt=tt: hbuf[:, tt, kc, :], lambda kc, tt=tt: r_h[tt][kc])
            add_c(norm_all)
            st = {}
            for gp in range(NG // 2):
                def c_gate(si, st=st):
                    st["g"] = si

                def c_up(si, half, st=st):
                    sg = st["g"]
                    gv = slot_view3(sg, KC, G * 128)
                    uv = slot_view3(si, KC, G * 128)
                    for fc in range(G):
                        hc_ = half * G + fc
                        for tt in range(NTT):
                            pg, rpg = get_ps(0, 6)
                            pu, rpu = get_ps(0, 6)
                            for kc in range(KC):
                                P.pe(lambda e, pg=pg, kc=kc, fc=fc, tt=tt: e.matmul(
                                    pg[:, 0:TT], lhsT=gv[:, kc, fc * 128:(fc + 1) * 128], rhs=hbuf[:, tt, kc, :],
                                    start=(kc == 0), stop=(kc == KC - 1)),
                                    reads=[r_slot[sg], r_h[tt][kc]], writes=[rpg])
                            for kc in range(KC):
                                P.pe(lambda e, pu=pu, kc=kc, fc=fc, tt=tt: e.matmul(
                                    pu[:, 0:TT], lhsT=uv[:, kc, fc * 128:(fc + 1) * 128], rhs=hbuf[:, tt, kc, :],
                                    start=(kc == 0), stop=(kc == KC - 1)),
                                    reads=[r_slot[si], r_h[tt][kc]], writes=[rpu])
                            sl, rsl = get_tmp()
                            P.act(lambda e, sl=sl, pg=pg: e.activation(out=sl, in_=pg[:, 0:TT], func=AF.Silu),
                                  reads=[rpg], writes=[rsl])
                            P.dve(lambda e, sl=sl, pu=pu, hc_=hc_, tt=tt: e.tensor_tensor(
                                out=hid[:, hc_, tt * TT:(tt + 1) * TT], in0=sl, in1=pu[:, 0:TT], op=ALU.mult),
                                reads=[rsl, rpu], writes=[r_hid[hc_][tt]])

                def c_dstore(si, st=st):
                    st["d0"] = si

                def c_down(si, st=st):
                    s0 = st["d0"]
                    dvs = [slot_view3(s0, G, D), slot_view3(si, G, D)]
                    srs = [r_slot[s0], r_slot[si]]
                    for dc in range(KC):
                        for tt in range(NTT):
                            pd, rpd = get_ps(0, 6)
                            for q in range(2 * G):
                                hf, fc = divmod(q, G)
                                P.pe(lambda e, pd=pd, q=q, hf=hf, fc=fc, dc=dc, tt=tt: e.matmul(
                                    pd[:, 0:TT], lhsT=dvs[hf][:, fc, dc * 128:(dc + 1) * 128],
                                    rhs=hid[:, q, tt * TT:(tt + 1) * TT], start=(q == 0), stop=(q == 2 * G - 1)),
                                    reads=[srs[hf], r_hid[q][tt]], writes=[rpd])
                            xs = xres[:, dc, tt * TT:(tt + 1) * TT]
                            P.dve(lambda e, pd=pd, xs=xs: e.scalar_tensor_tensor(
                                out=xs, in0=pd[:, 0:TT], scalar=0.5, in1=xs, op0=ALU.mult, op1=ALU.add),
                                reads=[rpd, r_x[dc][tt]], writes=[r_x[dc][tt]])

                g0, g1 = 2 * gp, 2 * gp + 1
                add_w(load_cols(wg, g0 * G * 128, G * 128, KC), c_gate)
                add_w(load_cols(wu, g0 * G * 128, G * 128, KC), lambda si, c_up=c_up: c_up(si, 0))
                add_w(load_cols(wg, g1 * G * 128, G * 128, KC), c_gate)
                add_w(load_cols(wu, g1 * G * 128, G * 128, KC), lambda si, c_up=c_up: c_up(si, 1))
                add_w(load_rows(wdn, g0 * G, G), c_dstore)
                add_w(load_rows(wdn, g1 * G, G), c_down)

        def proj_add_tasks(wap, src_sub, tt, bias_name=None, hooks=None):
            for j in range(8):
                if hooks and j in hooks:
                    hooks[j]()

                def c(si, j=j):
                    wv = slot_view3(si, KC, 256)
                    for cc in range(2):
                        dc = 2 * j + cc
                        pd, rpd = get_ps(0, 7)
                        for kc in range(KC):
                            P.pe(lambda e, pd=pd, kc=kc, cc=cc: e.matmul(
                                pd[:, 0:TT], lhsT=wv[:, kc, cc * 128:(cc + 1) * 128], rhs=hbuf[:, src_sub, kc, :],
                                start=(kc == 0), stop=(kc == KC - 1)),
                                reads=[r_slot[si], r_h[src_sub][kc]], writes=[rpd])
                        xs = xres[:, dc, tt * TT:(tt + 1) * TT]
                        if bias_name is None:
                            P.dve(lambda e, pd=pd, xs=xs: e.tensor_tensor(out=xs, in0=pd[:, 0:TT], in1=xs, op=ALU.add),
                                  reads=[rpd, r_x[dc][tt]], writes=[r_x[dc][tt]])
                        else:
                            P.dve(lambda e, pd=pd, xs=xs, dc=dc: e.scalar_tensor_tensor(
                                out=xs, in0=pd[:, 0:TT], scalar=vcol(bias_name, dc), in1=xs, op0=ALU.add, op1=ALU.add),
                                reads=[rpd, r_x[dc][tt], r_vecs], writes=[r_x[dc][tt]])
                add_w(load_cols(wap, j * 256, 256, KC), c)

        def even_setup(e_):
            def f():
                arena_barrier()
                for c in range(8):
                    P.dve(lambda e, c=c: e.memset(hc[:, c, 0:30], 0.0), writes=[r_hc[c]])
                P.dma("sp", lambda e: e.dma_start(out=wst, in_=ws_d[e_].rearrange("h p q -> p h q")), d_misc[3], writes=[r_wst])
            return f

        def even_setup_b(e_):
            def f():
                P.dma("sp", lambda e: e.dma_start(out=bvb, in_=bvb_d[e_]), d_misc[2], writes=r_bvb)
                P.dma("sp", lambda e: e.dma_start(out=Th.rearrange("p h q -> p (h q)"), in_=bsb_d[e_]),
                      d_misc[2], writes=r_Th)
                P.dve(lambda e: e.memset(wst[0:64, :, 64:128], 0.0), writes=[r_wst])
                for h0 in (0, 4):
                    pt, rpt = get_ps(0, 7)
                    for h in range(h0, h0 + 4):
                        P.pe(lambda e, pt=pt, h=h, h0=h0: e.transpose(out=pt[:, (h - h0) * 128:(h - h0 + 1) * 128], in_=wst[:, h, :], identity=ident[:]),
                             reads=[r_wst, r_ident], writes=[rpt])
                    P.act(lambda e, pt=pt, h0=h0: e.activation(out=wsT[:, h0:h0 + 4, :], in_=pt[:].rearrange("p (h q) -> p h q", h=4), func=AF.Copy),
                          reads=[rpt], writes=r_wsT)
                for h0 in (0, 4):
                    pr, rpr = get_ps(0, 7)
                    for h in range(h0, h0 + 4):
                        P.pe(lambda e, pr=pr, h=h, h0=h0: e.matmul(pr[:, (h - h0) * 128:(h - h0 + 1) * 128], lhsT=ones_b[:], rhs=wsT[:, h, :], start=True, stop=True),
                             reads=r_wsT + [r_ones], writes=[rpr])
                    for h in range(h0, h0 + 4):
                        P.dve(lambda e, pr=pr, h=h, h0=h0: e.scalar_tensor_tensor(
                            out=Th[:, h, :], in0=pr[:, (h - h0) * 128:(h - h0 + 1) * 128], scalar=vcol(f"gln_b{e_}", h), in1=Th[:, h, :],
                            op0=ALU.mult, op1=ALU.add), reads=[rpr, r_vecs] + r_Th, writes=r_Th)
            return f

        def even_pre(l, tt):
            def pre():
                emit_norm(f"n_mix{l}", tt, lambda kc: hbuf[:, 0, kc, :], lambda kc: r_h[0][kc])
            add_c(pre)

        def even_tasks(l, tt, before_out=None, out_hooks=None):
            e_ = l // 2
            w_in = wd["ab_w_in"][e_]
            w_out = wd["ab_w_out"][e_]
            cwo = VOFF[f"conv_w{e_}"]

            st = {}
            for j in range(4):
                def c_a(si, st=st):
                    st["a"] = si

                def c_g(si, j=j, st=st):
                    sa = st["a"]
                    av = slot_view3(sa, KC, 256)
                    gv = slot_view3(si, KC, 256)
                    for cc in range(2):
                        ch = 2 * j + cc
                        pa, rpa = get_ps(0, 7)
                        pg, rpg = get_ps(0, 7)
                        for kc in range(KC):
                            P.pe(lambda e, pa=pa, kc=kc, cc=cc: e.matmul(
                                pa[:, 0:TT], lhsT=av[:, kc, cc * 128:(cc + 1) * 128], rhs=hbuf[:, 0, kc, :],
                                start=(kc == 0), stop=(kc == KC - 1)), reads=[r_slot[sa], r_h[0][kc]], writes=[rpa])
                        for kc in range(KC):
                            P.pe(lambda e, pg=pg, kc=kc, cc=cc: e.matmul(
                                pg[:, 0:TT], lhsT=gv[:, kc, cc * 128:(cc + 1) * 128], rhs=hbuf[:, 0, kc, :],
                                start=(kc == 0), stop=(kc == KC - 1)), reads=[r_slot[si], r_h[0][kc]], writes=[rpg])
                        sg, rsg = get_tmp()
                        P.act(lambda e, sg=sg, pg=pg, ch=ch: e.activation(out=sg, in_=pg[:, 0:TT], func=AF.Sigmoid,
                                                                          bias=vcol(f"b_g{e_}", ch)),
                              reads=[rpg, r_vecs], writes=[rsg])
                        P.dve(lambda e, sg=sg, pa=pa, ch=ch: e.scalar_tensor_tensor(
                            out=hc[:, ch, 30:30 + TT], in0=pa[:, 0:TT], scalar=vcol(f"b_a{e_}", ch), in1=sg,
                            op0=ALU.add, op1=ALU.mult), reads=[rpa, rsg, r_vecs], writes=[r_hc[ch]])
                add_w(load_cols(w_in, 2048 + j * 256, 256, KC), c_a)
                add_w(load_cols(w_in, 3072 + j * 256, 256, KC), c_g)

            for c in range(8):
                def ld_diag(si, c=c):
                    dv = slot_view3(si, 32, 128)
                    for k in range(CONVW):
                        wcol = vecs[:, cwo + k * 8 + c:cwo + k * 8 + c + 1]
                        P.dve(lambda e, k=k, wcol=wcol: e.tensor_scalar(out=dv[:, k, :], in0=ident[:], scalar1=wcol,
                                                                        scalar2=None, op0=ALU.mult),
                              reads=[r_ident, r_vecs], writes=[r_slot[si]] if k in (0, CONVW - 1) else [])

                def c_conv(si, c=c):
                    dv = slot_view3(si, 32, 128)
                    pc, rpc = get_ps(0, 7)
                    for k in range(CONVW):
                        P.pe(lambda e, pc=pc, k=k: e.matmul(pc[:, 0:TT], lhsT=dv[:, k, :], rhs=hc[:, c, k:k + TT],
                                                            start=(k == 0), stop=(k == CONVW - 1)),
                             reads=[r_slot[si], r_hc[c]], writes=[rpc])
                    P.act(lambda e, pc=pc: e.activation(out=cv[:, c, :], in_=pc[:, 0:TT], func=AF.Identity,
                                                        bias=vcol(f"conv_b{e_}", c)),
                          reads=[rpc, r_vecs], writes=[r_cv[c]])
                    P.act(lambda e: e.activation(out=hc[:, c, 0:30], in_=hc[:, c, TT:TT + 30], func=AF.Copy),
                          reads=[r_hc[c]], writes=[r_hc[c]])
                    if c == 7:
                        conv_ln()
                add_w(ld_diag, c_conv)

            if tt == 0:
                add_c(even_setup_b(e_))

            def conv_ln():
                pm, rpm = ps[6], r_ps[6]
                for c in range(8):
                    P.pe(lambda e, c=c: e.matmul(pm[:, 0:TT], lhsT=ones_f[:], rhs=cv[:, c, :], start=(c == 0), stop=(c == 7)),
                         reads=[r_cv[c], r_ones], writes=[rpm])
                pq, rpq = ps[7], r_ps[7]
                for c in range(8):
                    sq, rsq = get_tmp()
                    P.act(lambda e, sq=sq, c=c: e.activation(out=sq, in_=cv[:, c, :], func=AF.Square),
                          reads=[r_cv[c]], writes=[rsq])
                    P.pe(lambda e, sq=sq, c=c: e.matmul(pq[:, 0:TT], lhsT=ones_f[:], rhs=sq, start=(c == 0), stop=(c == 7)),
                         reads=[rsq, r_ones], writes=[rpq])
                mean, rmean = get_tmp()
                P.dve(lambda e, mean=mean: e.tensor_scalar(out=mean, in0=pm[:, 0:TT], scalar1=1.0 / 1024, scalar2=None, op0=ALU.mult),
                      reads=[rpm], writes=[rmean])
                msq, rmsq = get_tmp()
                P.dve(lambda e, mean=mean, msq=msq: e.tensor_tensor(out=msq, in0=mean, in1=mean, op=ALU.mult),
                      reads=[rmean], writes=[rmsq])
                oi = rs_i[0] % 2
                rs_i[0] += 1
                rt, rrt = rstd_t[:, oi, :], r_rstd[oi]
                P.dve(lambda e, msq=msq: e.scalar_tensor_tensor(out=rt, in0=pq[:, 0:TT], scalar=1.0 / 1024, in1=msq,
                                                                op0=ALU.mult, op1=ALU.subtract),
                      reads=[rpq, rmsq], writes=[rrt])
                P.act(lambda e: e.activation(out=rt, in_=rt, func=AF.Sqrt, bias=EPS_AP(), scale=1.0),
                      reads=[rrt, r_eps], writes=[rrt])
                P.dve(lambda e: e.reciprocal(out=rt, in_=rt), reads=[rrt], writes=[rrt])
                for c in range(8):
                    P.dve(lambda e, c=c, mean=mean: e.tensor_tensor(out=cv[:, c, :], in0=cv[:, c, :], in1=mean, op=ALU.subtract),
                          reads=[r_cv[c], rmean], writes=[r_cv[c]])
                    P.dve(lambda e, c=c: e.tensor_tensor(out=cv[:, c, :], in0=cv[:, c, :], in1=rt, op=ALU.mult),
                          reads=[r_cv[c], rrt], writes=[r_cv[c]])
                    P.act(lambda e, c=c: e.activation(out=hbuf[:, 1, 8 + c, :], in_=cv[:, c, :], func=AF.Silu,
                                                      bias=vcol(f"cln_b{e_}", c), scale=vcol(f"cln_g{e_}", c)),
                          reads=[r_cv[c], r_vecs], writes=[r_h[1][8 + c]])

            for j in range(4):
                def c_v(si, j=j):
                    wv = slot_view3(si, KC, 256)
                    for blk in range(3):
                        pv, rpv = get_ps(0, 7)
                        for kc in range(KC):
                            P.pe(lambda e, pv=pv, kc=kc, blk=blk: e.matmul(
                                pv[:, 0:256], lhsT=hbuf[:, 0, kc, blk * 128:(blk + 1) * 128], rhs=wv[:, kc, :],
                                start=(kc == 0), stop=(kc == KC - 1)), reads=[r_slot[si], r_h[0][kc]], writes=[rpv])
                        P.dve(lambda e, pv=pv, blk=blk, j=j: e.tensor_tensor(
                            out=vt[:, blk, j * 256:(j + 1) * 256], in0=pv[:, 0:256], in1=bvb[:, j * 256:(j + 1) * 256],
                            op=ALU.add), reads=[rpv] + r_bvb,
                            writes=([r_vtb[blk]] + r_cv[(blk * 1024) // TT:((blk + 1) * 1024 - 1) // TT + 1]) if j == 0 else [r_vtb[blk]])
                    if j == 3:
                        v_post()
                add_w(load_cols(w_in, 1024 + j * 256, 256, KC), c_v)

            for j in range(4):
                def c_u(si, j=j):
                    wv = slot_view3(si, KC, 256)
                    for cc in range(2):
                        ch = 2 * j + cc
                        pu, rpu = get_ps(0, 7)
                        for kc in range(KC):
                            P.pe(lambda e, pu=pu, kc=kc, cc=cc: e.matmul(
                                pu[:, 0:TT], lhsT=wv[:, kc, cc * 128:(cc + 1) * 128], rhs=hbuf[:, 0, kc, :],
                                start=(kc == 0), stop=(kc == KC - 1)), reads=[r_slot[si], r_h[0][kc]], writes=[rpu])
                        P.act(lambda e, pu=pu, ch=ch: e.activation(out=hbuf[:, 1, ch, :], in_=pu[:, 0:TT], func=AF.Gelu_apprx_tanh,
                                                                   bias=vcol(f"b_u{e_}", ch)),
                              reads=[rpu, r_vecs], writes=[r_h[1][ch]])
                    if j == 3:
                        v_spatial()
                add_w(load_cols(w_in, j * 256, 256, KC), c_u)

            def v_post():
                B3 = range(3)
                st6 = [small[:, 8 + blk * 16:8 + blk * 16 + 12].rearrange("p (a b) -> p a b", a=2) for blk in B3]
                mv = [small[:, 8 + blk * 16 + 12:8 + blk * 16 + 14] for blk in B3]
                rs = [small[:, 8 + blk * 16 + 14:8 + blk * 16 + 15] for blk in B3]
                for blk in B3:
                    P.act(lambda e, blk=blk: e.activation(out=vt[:, blk, :], in_=vt[:, blk, :], func=AF.Gelu_apprx_tanh),
                          reads=[r_vtb[blk]], writes=[r_vtb[blk]])
                for hh in range(2):
                    for blk in B3:
                        P.dve(lambda e, blk=blk, hh=hh: e.bn_stats(out=st6[blk][:, hh, :], in_=vt[:, blk, hh * 512:(hh + 1) * 512]),
                              reads=[r_vtb[blk]], writes=[r_sm[2 + blk]] if hh == 1 else [])
                for blk in B3:
                    P.dve(lambda e, blk=blk: e.bn_aggr(out=mv[blk], in_=st6[blk].rearrange("p a b -> p (a b)")),
                          reads=[r_sm[2 + blk]], writes=[r_sm[2 + blk]])
                for blk in B3:
                    P.act(lambda e, blk=blk: e.activation(out=rs[blk], in_=mv[blk][:, 1:2], func=AF.Sqrt, bias=EPS_AP(), scale=1.0),
                          reads=[r_sm[2 + blk], r_eps], writes=[r_sm[2 + blk]])
                for blk in B3:
                    P.dve(lambda e, blk=blk: e.reciprocal(out=rs[blk], in_=rs[blk]), reads=[r_sm[2 + blk]], writes=[r_sm[2 + blk]])
                for blk in B3:
                    P.dve(lambda e, blk=blk: e.tensor_scalar(
                        out=vn[:, blk, :], in0=vt[:, blk, :], scalar1=mv[blk][:, 0:1], scalar2=rs[blk],
                        op0=ALU.subtract, op1=ALU.mult),
                        reads=[r_vtb[blk], r_sm[2 + blk]] + r_cv[(blk * 1024) // TT:((blk + 1) * 1024 - 1) // TT + 1], writes=[r_vn[blk]])
            def v_spatial():
                for blk in range(3):
                    for h in range(8):
                        pS, rpS = get_ps(0, 7)
                        P.pe(lambda e, pS=pS, blk=blk, h=h: e.matmul(
                            pS[:, 0:128], lhsT=vn[:, blk, h * 128:(h + 1) * 128], rhs=wsT[:, h, :], start=True, stop=True),
                            reads=[r_vn[blk]] + r_wsT, writes=[rpS])
                        tm, rtm = get_tmp()
                        P.dve(lambda e, pS=pS, tm=tm, h=h: e.scalar_tensor_tensor(
                            out=tm[:, 0:128], in0=pS[:, 0:128], scalar=vcol(f"gln_g{e_}", h), in1=Th[:, h, :],
                            op0=ALU.mult, op1=ALU.add), reads=[rpS, r_vecs] + r_Th, writes=[rtm])
                        ya = hbuf[:, 1, h, blk * 128:(blk + 1) * 128]
                        P.dve(lambda e, tm=tm, ya=ya: e.tensor_tensor(out=ya, in0=tm[:, 0:128], in1=ya, op=ALU.mult),
                              reads=[rtm, r_h[1][h]], writes=[r_h[1][h]])

            if before_out is not None:
                before_out()
            proj_add_tasks(w_out, 1, tt, bias_name=f"b_out{e_}", hooks=out_hooks)

        def pool_tasks(l, tt, before_mm=None):
            o_ = l // 2
            c0 = tt * TT
            r_hp = r_hc + r_cv
            r_pw = r_vn

            def pre():
                if tt == 0:
                    arena_barrier()
                    for kc in range(KC):
                        P.dve(lambda e, kc=kc: e.memset(hp[:, kc, 0:16], 0.0), writes=[r_hp[kc]])
                emit_norm(f"n_mix{l}", tt, lambda kc: hp[:, kc, 16:16 + TT], lambda kc: r_hp[kc % 16])
                W = HPW
                for kc in range(KC):
                    gi = kc // 4
                    src = hp[:, kc, :]
                    rsrc = r_hp[kc % 16]
                    cur = src
                    rcur = rsrc
                    sh = 1
                    for step in range(gi + 1):
                        dst = pwt[:, (step % 2) + 2 * (kc % 2), :]
                        rdst = r_pw[(step % 2 + 2 * (kc % 2)) % 3]
                        P.dve(lambda e, dst=dst, cur=cur, sh=sh: e.tensor_tensor(
                            out=dst[:, sh:W], in0=cur[:, sh:W], in1=cur[:, 0:W - sh], op=ALU.add),
                            reads=[rcur, rsrc], writes=[rdst])
                        cur, rcur = dst, rdst
                        sh *= 2
                    win = 2 ** (gi + 1)
                    if tt == 0:
                        P.dve(lambda e, cur=cur, gi=gi: e.tensor_tensor(
                            out=cur[:, 16:32], in0=cur[:, 16:32], in1=corr[:, gi * 16:(gi + 1) * 16], op=ALU.mult),
                            reads=[rcur, r_corr], writes=[rcur])
                    P.dve(lambda e, cur=cur, kc=kc, win=win: e.scalar_tensor_tensor(
                        out=hbuf[:, 1, kc, :], in0=cur[:, 16:16 + TT], scalar=1.0 / win, in1=hp[:, kc, 16:16 + TT],
                        op0=ALU.mult, op1=ALU.subtract), reads=[rcur, rsrc], writes=[r_h[1][kc]])
                    P.act(lambda e, kc=kc: e.activation(out=hp[:, kc, 0:16], in_=hp[:, kc, TT:TT + 16], func=AF.Copy),
                          reads=[rsrc], writes=[rsrc])
                if tt == 0:
                    P.dve(lambda e: e.tensor_tensor(out=small[:, 40:56], in0=vcol(f"pool_b{o_}", 0, 16),
                                                    in1=vcol(f"pool_s{o_}", 0, 16), op=ALU.mult),
                          reads=[r_vecs], writes=[r_sm[5]])
            add_c(pre)
            if before_mm is not None:
                before_mm()
            for gi in range(4):
                def c(si, gi=gi):
                    wv = slot_view3(si, 4, 512)
                    for oc in range(4):
                        dc = 4 * gi + oc
                        pd, rpd = get_ps(0, 7)
                        for k4 in range(4):
                            P.pe(lambda e, pd=pd, k4=k4, oc=oc: e.matmul(
                                pd[:, 0:TT], lhsT=wv[:, k4, oc * 128:(oc + 1) * 128], rhs=hbuf[:, 1, 4 * gi + k4, :],
                                start=(k4 == 0), stop=(k4 == 3)), reads=[r_slot[si], r_h[1][4 * gi + k4]], writes=[rpd])
                        tm, rtm = get_tmp()
                        P.act(lambda e, pd=pd, tm=tm, dc=dc: e.activation(
                            out=tm, in_=pd[:, 0:TT], func=AF.Identity, bias=small[:, 40 + dc:41 + dc],
                            scale=vcol(f"pool_s{o_}", dc)), reads=[rpd, r_sm[5], r_vecs], writes=[rtm])
                        xs = xres[:, dc, c0:c0 + TT]
                        P.dve(lambda e, tm=tm, xs=xs: e.tensor_tensor(out=xs, in0=tm, in1=xs, op=ALU.add),
                              reads=[rtm, r_x[dc][tt]], writes=[r_x[dc][tt]])

                def ld(si, gi=gi):
                    src = wd["pool_w"][o_, gi].rearrange("(kc p) f -> p kc f", p=128)
                    dst = slot_view3(si, 4, 512)
                    P.dma("pool", lambda e: e.dma_start(out=dst, in_=src), dsl[si], writes=[r_slot[si]])
                add_w(ld, c)

        r_kT = r_hc[0:4]
        r_vtok = r_hc[4:8]
        r_memT = r_cv[0:4]
        r_A = r_cv[4:8] + r_vn

        def xattn_kv_tasks(l, only=None):
            memT_hold = hbuf[:, 2, :, 0:256]

            def pre(mbs=(0, 1)):
                if 0 in mbs:
                    arena_barrier()
                for mb in mbs:
                    P.dma("sp", lambda e, mb=mb: e.dma_start(out=stg, in_=mem_d[mb * 128:(mb + 1) * 128, :]),
                          d_misc[1], writes=r_A)
                    ss = small[:, 4 + mb:5 + mb]
                    rss = r_sm[mb]
                    P.act(lambda e, ss=ss: e.activation(out=arena[:, 4096:6144], in_=stg, func=AF.Square, accum_out=ss),
                          reads=r_A, writes=r_memT + [rss])
                    P.act(lambda e, ss=ss: e.activation(out=ss, in_=ss, func=AF.Sqrt, bias=EPS_AP(), scale=1.0 / D),
                          reads=[rss, r_eps], writes=[rss])
                    P.dve(lambda e, ss=ss: e.reciprocal(out=ss, in_=ss), reads=[rss], writes=[rss])
                    P.dve(lambda e, ss=ss: e.tensor_scalar(out=stg, in0=stg, scalar1=ss, scalar2=None, op0=ALU.mult),
                          reads=r_A + [rss], writes=r_A)
                    for q in range(4):
                        pt, rpt = get_ps(0, 7)
                        for j in range(4):
                            kc = 4 * q + j
                            P.pe(lambda e, pt=pt, j=j, kc=kc: e.transpose(out=pt[:, j * 128:(j + 1) * 128],
                                                                          in_=stg[:, kc * 128:(kc + 1) * 128], identity=ident[:]),
                                 reads=r_A + [r_ident], writes=[rpt])
                        for j in range(4):
                            kc = 4 * q + j
                            P.dve(lambda e, pt=pt, j=j, kc=kc, mb=mb: e.tensor_scalar(
                                out=memT_hold[:, kc, mb * 128:(mb + 1) * 128], in0=pt[:, j * 128:(j + 1) * 128],
                                scalar1=vcol(f"n_xkv{l}", kc), scalar2=None, op0=ALU.mult),
                                reads=[rpt, r_vecs], writes=[r_h[2][kc]])
            if only == "pre":
                add_c(pre)
                return
            if only in ("pre0", "pre1"):
                add_c(lambda: pre((int(only[3]),)))
                return
            wk, wv_ = wd["xattn_wk"][l], wd["xattn_wv"][l]
            for j in range(8):
                def c_k(si, j=j):
                    wv = slot_view3(si, KC, 256)
                    for cc in range(2):
                        dc = 2 * j + cc
                        pk, rpk = get_ps(0, 7)
                        for kc in range(KC):
                            P.pe(lambda e, pk=pk, kc=kc, cc=cc: e.matmul(
                                pk[:, 0:256], lhsT=wv[:, kc, cc * 128:(cc + 1) * 128], rhs=memT_hold[:, kc, :],
                                start=(kc == 0), stop=(kc == KC - 1)), reads=[r_slot[si], r_h[2][kc]], writes=[rpk])
                        P.act(lambda e, pk=pk, dc=dc: e.activation(out=kT[:, dc, :], in_=pk[:, 0:256], func=AF.Copy),
                              reads=[rpk], writes=r_kT)
                add_w(load_cols(wk, j * 256, 256, KC), c_k)
            for j in range(8):
                def c_v(si, j=j):
                    wv = slot_view3(si, KC, 256)
                    for mb in range(2):
                        pv, rpv = get_ps(0, 7)
                        for kc in range(KC):
                            P.pe(lambda e, pv=pv, kc=kc, mb=mb: e.matmul(
                                pv[:, 0:256], lhsT=memT_hold[:, kc, mb * 128:(mb + 1) * 128], rhs=wv[:, kc, :],
                                start=(kc == 0), stop=(kc == KC - 1)), reads=[r_slot[si], r_h[2][kc]], writes=[rpv])
                        P.dve(lambda e, pv=pv, mb=mb, j=j: e.tensor_copy(out=vtok[:, mb, j * 256:(j + 1) * 256], in_=pv[:, 0:256]),
                              reads=[rpv], writes=r_vtok)
                add_w(load_cols(wv_, j * 256, 256, KC), c_v)

        def xattn_pre(l, tt):
            def pre():
                emit_norm(f"n_xq{l}", tt, lambda kc: hbuf[:, 0, kc, :], lambda kc: r_h[0][kc])
            add_c(pre)

        def xattn_tasks(l, tt, before_out=None):
            wq, wo = wd["xattn_wq"][l], wd["xattn_wo"][l]
            scl = 512.0 ** -0.5
            for j in range(8):
                def c_q(si, j=j):
                    wv = slot_view3(si, KC, 256)
                    for cc in range(2):
                        dc = 2 * j + cc
                        pq, rpq = get_ps(0, 7)
                        for kc in range(KC):
                            P.pe(lambda e, pq=pq, kc=kc, cc=cc: e.matmul(
                                pq[:, 0:TT], lhsT=wv[:, kc, cc * 128:(cc + 1) * 128], rhs=hbuf[:, 0, kc, :],
                                start=(kc == 0), stop=(kc == KC - 1)), reads=[r_slot[si], r_h[0][kc]], writes=[rpq])
                        P.act(lambda e, pq=pq, dc=dc: e.activation(out=hbuf[:, 1, dc, :], in_=pq[:, 0:TT], func=AF.Copy, scale=scl),
                              reads=[rpq], writes=[r_h[1][dc]])
                    if j % 2 == 1:
                        attn_head(j // 2)
                add_w(load_cols(wq, j * 256, 256, KC), c_q)

            def attn_head(hd):
                eb = hd % 2
                for mb in range(2):
                    pS, rpS = get_ps(0, 7)
                    for j in range(4):
                        P.pe(lambda e, pS=pS, j=j, mb=mb: e.matmul(
                            pS[:, 0:TT], lhsT=kT[:, 4 * hd + j, mb * 128:(mb + 1) * 128], rhs=hbuf[:, 1, 4 * hd + j, :],
                            start=(j == 0), stop=(j == 3)), reads=r_kT + [r_h[1][4 * hd + j]], writes=[rpS])
                    P.act(lambda e, pS=pS, mb=mb: e.activation(out=expT[:, eb, mb, :], in_=pS[:, 0:TT], func=AF.Exp),
                          reads=[rpS], writes=[r_A[eb * 2 + mb]])
                pden, rpden = get_ps(0, 7)
                for mb in range(2):
                    P.pe(lambda e, mb=mb: e.matmul(pden[:, 0:TT], lhsT=ones_b[:], rhs=expT[:, eb, mb, :],
                                                   start=(mb == 0), stop=(mb == 1)),
                         reads=[r_A[eb * 2 + mb], r_ones], writes=[rpden])
                rd, rrd = get_tmp()
                P.dve(lambda e, rd=rd: e.reciprocal(out=rd, in_=pden[:, 0:TT]), reads=[rpden], writes=[rrd])
                for j in range(4):
                    po, rpo = get_ps(0, 7)
                    for mb in range(2):
                        P.pe(lambda e, po=po, j=j, mb=mb: e.matmul(
                            po[:, 0:TT], lhsT=vtok[:, mb, (4 * hd + j) * 128:(4 * hd + j + 1) * 128], rhs=expT[:, eb, mb, :],
                            start=(mb == 0), stop=(mb == 1)), reads=r_vtok + [r_A[eb * 2 + mb]], writes=[rpo])
                    P.dve(lambda e, po=po, j=j, rd=rd: e.tensor_tensor(out=hbuf[:, 2, 4 * hd + j, :], in0=po[:, 0:TT], in1=rd, op=ALU.mult),
                          reads=[rpo, rrd], writes=[r_h[2][4 * hd + j]])
            if before_out is not None:
                before_out()
            proj_add_tasks(wo, 2, tt)

        def final_tasks():
            def f():
                r_ost = [[Res() for _ in range(4)] for _ in range(3)]
                arena_barrier([r for k in range(3) for r in r_ost[k]])
                for tt in range(NTT):
                    c0 = tt * TT
                    oi = rs_i[0] % 2
                    rs_i[0] += 1
                    if do_final:
                        rt, rrt = emit_rstd(lambda c, c0=c0: xres[:, c, c0:c0 + TT], [r_x[c][tt] for c in range(KC)], KC, TT, 1.0 / D, oi)
                    for bo in range(3):
                        blk = tt * 3 + bo
                        ob = blk % 3
                        ost = arena[:, ob * 2048:(ob + 1) * 2048]
                        for q in range(4):
                            pt, rpt = get_ps(0, 7)
                            for j in range(4):
                                kc = 4 * q + j
                                cs = slice(c0 + bo * 128, c0 + (bo + 1) * 128)
                                if do_final:
                                    yt, ryt = get_tmp()
                                    P.dve(lambda e, yt=yt, kc=kc, cs=cs, bo=bo, rt=rt: e.scalar_tensor_tensor(
                                        out=yt[:, 0:128], in0=xres[:, kc, cs], scalar=vcol("n_final", kc),
                                        in1=rt[:, bo * 128:(bo + 1) * 128], op0=ALU.mult, op1=ALU.mult),
                                        reads=[r_x[kc][tt], rrt, r_vecs], writes=[ryt])
                                    P.pe(lambda e, pt=pt, j=j, yt=yt: e.transpose(out=pt[:, j * 128:(j + 1) * 128], in_=yt[:, 0:128], identity=ident[:]),
                                         reads=[ryt, r_ident], writes=[rpt])
                                else:
                                    P.pe(lambda e, pt=pt, j=j, kc=kc, cs=cs: e.transpose(out=pt[:, j * 128:(j + 1) * 128], in_=xres[:, kc, cs], identity=ident[:]),
                                         reads=[r_x[kc][tt], r_ident], writes=[rpt])
                            if q % 2 == 0:
                                P.act(lambda e, pt=pt, q=q, ost=ost: e.activation(out=ost[:, q * 512:(q + 1) * 512], in_=pt[:], func=AF.Copy),
                                      reads=[rpt], writes=[r_ost[ob][q]])
                            else:
                                P.dve(lambda e, pt=pt, q=q, ost=ost: e.tensor_copy(out=ost[:, q * 512:(q + 1) * 512], in_=pt[:]),
                                      reads=[rpt], writes=[r_ost[ob][q]])
                        P.dma("sp", lambda e, blk=blk, ost=ost: e.dma_start(out=out_d[blk * 128:(blk + 1) * 128, :], in_=ost),
                              d_out[ob], reads=r_ost[ob])
            add_c(f)

        for l in range(n_layers):
            ffn_tasks(l, 1)
            if stop_stage == (l, "ffn1"):
                break
            kvpre = (lambda l=l: xattn_kv_tasks(l, only="pre"))
            if l % 2 == 0:
                add_c(even_setup(l // 2))
                even_pre(l, 0)
                for tt in range(NTT):
                    if tt < NTT - 1:
                        even_tasks(l, tt, before_out=(lambda l=l, tt=tt: even_pre(l, tt + 1)))
                    else:
                        even_tasks(l, tt, out_hooks={5: (lambda l=l: xattn_kv_tasks(l, only="pre0")),
                                                     7: (lambda l=l: xattn_kv_tasks(l, only="pre1"))})
            else:
                for tt in range(NTT):
                    pool_tasks(l, tt, before_mm=kvpre if tt == NTT - 1 else None)
            if stop_stage == (l, "mix"):
                break
            xattn_kv_tasks(l, only="w")
            xattn_pre(l, 0)
            for tt in range(NTT):
                xattn_tasks(l, tt, before_out=(lambda l=l, tt=tt: xattn_pre(l, tt + 1)) if tt < NTT - 1 else None)
            if stop_stage == (l, "xattn"):
                break
            ffn_tasks(l, 2)
        final_tasks()

        widx = [i for i, t in enumerate(tasks) if t[0] == "w"]
        nloaded = [0]

        def ensure_loaded(upto):
            while nloaded[0] < min(upto + 1, len(widx)):
                k = nloaded[0]
                tasks[widx[k]][1](k % NSLOT)
                nloaded[0] += 1

        wk_i = 0
        for t in tasks:
            if t[0] == "w":
                ensure_loaded(wk_i + NSLOT - 1 - LOOKBACK)
                t[2](wk_i % NSLOT)
                wk_i += 1
            else:
                ensure_loaded(wk_i + NSLOT - 2 - LOOKBACK)
                t[2]()

        P.finalize(sems, final_waits=lambda: [(d.h, d.count) for d in d_out])
    return nc


_NC_CACHE = {}


def make_in_maps(inputs):
    inp = {k: np.asarray(v) for k, v in inputs.items()}
    V = pack_vecs(inp)
    ident = np.eye(128, dtype=np.float32)
    b_in = inp["ab_b_in"]
    bvb = np.ascontiguousarray(np.broadcast_to(b_in[:, None, 1024:2048], (2, 128, 1024))).astype(np.float32)
    bsb = np.ascontiguousarray(np.broadcast_to(inp["gmlp_b_s"].reshape(2, 1, 1024), (2, 128, 1024))).astype(np.float32)
    shared = {"vecs": V, "ident": ident, "bvb": bvb, "bsb": bsb,
              "gmlp_w_s": np.ascontiguousarray(inp["gmlp_w_s"], dtype=np.float32)}
    for nm in ("ffn1_gate", "ffn1_up", "ffn1_down", "ffn2_gate", "ffn2_up", "ffn2_down", "ab_w_in", "ab_w_out",
               "pool_w", "xattn_wq", "xattn_wk", "xattn_wv", "xattn_wo"):
        shared[nm] = np.ascontiguousarray(inp[nm], dtype=np.float32)
    in_maps = []
    for c in range(8):
        b, half = divmod(c, 2)
        t0 = 0 if half == 0 else HALO0
        m = dict(shared)
        m["x"] = np.ascontiguousarray(inp["x"][b, t0:t0 + T, :], dtype=np.float32)
        m["mem"] = np.ascontiguousarray(inp["mem"][b], dtype=np.float32)
        corr = np.ones((128, 4, 16), np.float32)
        if half == 0:
            for gi, win in enumerate((2, 4, 8, 16)):
                for t in range(16):
                    corr[:, gi, t] = float(win) / float(min(t + 1, win))
        m["corr"] = corr.reshape(128, 64)
        in_maps.append(m)
    return in_maps


def assemble(results):
    out = np.empty((BATCH, SEQ, D), np.float32)
    for c in range(8):
        b, half = divmod(c, 2)
        y = results[c]["out"]
        if half == 0:
            out[b, 0:T, :] = y
        else:
            out[b, T:SEQ, :] = y[T - HALO0:, :]
    return out


def kernel(**inputs):
    key = "full"
    if key not in _NC_CACHE:
        _NC_CACHE[key] = build_nc()
    nc = _NC_CACHE[key]
    in_maps = make_in_maps(inputs)
    res = run_bass_kernel_spmd(nc, in_maps, core_ids=list(range(8)))
    return assemble(res.results)
```

```python
import numpy as np
from contextlib import ExitStack
import concourse.bass as bass
import concourse.mybir as mybir
from concourse.bass_utils import run_bass_kernel_spmd

F32 = mybir.dt.float32
BF16 = mybir.dt.bfloat16
AF = mybir.ActivationFunctionType
ALU = mybir.AluOpType

D = 2048
KC = 16
FF = 5632
FCH = 44
SEQ = 2048
BATCH = 4
NMEM = 256
DEPTH = 4
T = 1152
TT = 384
NTT = 3
HALO0 = 896
G = 2
NG = FCH // G
SLOT = 4096
NSLOT = 5
LOOKBACK = 1
EPS = 1e-6
CONVW = 31

ENGS = ("pe", "act", "dve", "pool", "sp")


class Res:
    __slots__ = ("name", "w", "rs")

    def __init__(self, name=""):
        self.name = name
        self.w = None
        self.rs = []


class Op:
    __slots__ = ("eng", "fn", "reads", "writes", "dsem", "deps", "sig", "ev")

    def __init__(self, eng, fn, reads, writes, dsem):
        self.eng = eng
        self.fn = fn
        self.reads = reads
        self.writes = writes
        self.dsem = dsem
        self.deps = set()
        self.sig = False
        self.ev = None


class DmaSem:
    def __init__(self, handle):
        self.h = handle
        self.count = 0


class Prog:
    def __init__(self, nc, same_engine_sync=True):
        self.nc = nc
        self.ops = []
        self.same_engine_sync = same_engine_sync

    def op(self, eng, fn, reads=(), writes=(), dsem=None):
        o = Op(eng, fn, tuple(reads), tuple(writes), dsem)
        self.ops.append(o)
        return o

    def pe(self, fn, reads=(), writes=()):
        return self.op("pe", fn, reads, writes)

    def act(self, fn, reads=(), writes=()):
        return self.op("act", fn, reads, writes)

    def dve(self, fn, reads=(), writes=()):
        return self.op("dve", fn, reads, writes)

    def dma(self, eng, fn, dsem, reads=(), writes=()):
        if not hasattr(dsem, "res"):
            dsem.res = Res("dsem")
        return self.op(eng, fn, reads, tuple(writes) + (dsem.res,), dsem)

    def finalize(self, sems, final_waits):
        ops = self.ops
        for i, o in enumerate(ops):
            for r in o.reads:
                if r.w is not None:
                    o.deps.add(r.w)
            for w in o.writes:
                if w.w is not None:
                    o.deps.add(w.w)
                for rr in w.rs:
                    o.deps.add(rr)
            for r in o.reads:
                r.rs.append(i)
            for w in o.writes:
                w.w = i
                w.rs = []
            o.deps.discard(i)
        for i, o in enumerate(ops):
            keep = set()
            for d in o.deps:
                p = ops[d]
                if p.eng == o.eng and p.dsem is None and o.dsem is None:
                    if o.eng == "pe":
                        continue
                    if not self.same_engine_sync:
                        continue
                keep.add(d)
            o.deps = keep
            for d in keep:
                ops[d].sig = True
        counts = {e: 0 for e in ENGS}
        for o in ops:
            if o.dsem is not None:
                o.dsem.count += 16
                o.ev = (o.dsem.h, o.dsem.count)
            elif o.sig:
                counts[o.eng] += 1
                o.ev = (sems[o.eng], counts[o.eng])
        self.counts = counts
        streams = {e: [] for e in ENGS}
        for o in ops:
            streams[o.eng].append(o)

        def emit(engh, lst, extra_final):
            waited = {}
            for o in lst:
                need = {}
                for d in o.deps:
                    sem, val = ops[d].ev
                    k = id(sem)
                    if k not in need or need[k][1] < val:
                        need[k] = (sem, val)
                for k, (sem, val) in need.items():
                    if waited.get(k, 0) >= val:
                        continue
                    engh.wait_ge(sem, val)
                    waited[k] = val
                inst = o.fn(engh)
                if o.dsem is not None:
                    inst.then_inc(o.dsem.h, 16)
                elif o.sig:
                    inst.then_inc(sems[o.eng], 1)
            for (sem, val) in extra_final:
                engh.wait_ge(sem, val)

        with self.nc.Block() as block:
            @block.tensor
            def _(e):
                emit(e, streams["pe"], [])

            @block.scalar
            def _(e):
                emit(e, streams["act"], [])

            @block.vector
            def _(e):
                emit(e, streams["dve"], [])

            @block.gpsimd
            def _(e):
                emit(e, streams["pool"], [])

            @block.sync
            def _(e):
                emit(e, streams["sp"], list(final_waits()))


def vec_layout():
    off = {}
    n = 0

    def add(name, cols):
        nonlocal n
        off[name] = n
        n += cols

    for l in range(DEPTH):
        for nm in ("n_ffn1", "n_mix", "n_xq", "n_xkv", "n_ffn2"):
            add(f"{nm}{l}", 16)
    for e in range(2):
        for nm, c in (("b_u", 8), ("b_a", 8), ("b_g", 8), ("gln_g", 8), ("gln_b", 8),
                      ("conv_b", 8), ("cln_g", 8), ("cln_b", 8), ("b_out", 16), ("conv_w", CONVW * 8)):
            add(f"{nm}{e}", c)
    for o in range(2):
        add(f"pool_b{o}", 16)
        add(f"pool_s{o}", 16)
    add("n_final", 16)
    return off, n


VOFF, NV = vec_layout()


def fm(v):
    v = np.asarray(v, np.float32).reshape(-1, 128)
    return np.ascontiguousarray(v.T)


def pack_vecs(inp):
    V = np.zeros((128, NV), np.float32)

    def put(name, arr):
        a = fm(arr)
        V[:, VOFF[name]:VOFF[name] + a.shape[1]] = a

    for l in range(DEPTH):
        put(f"n_ffn1{l}", inp["norm_ffn1"][l])
        put(f"n_mix{l}", inp["norm_mix"][l])
        put(f"n_xq{l}", inp["norm_xq"][l])
        put(f"n_xkv{l}", inp["norm_xkv"][l])
        put(f"n_ffn2{l}", inp["norm_ffn2"][l])
    for e in range(2):
        b_in = np.asarray(inp["ab_b_in"][e])
        put(f"b_u{e}", b_in[0:1024])
        put(f"b_a{e}", b_in[2048:3072])
        put(f"b_g{e}", b_in[3072:4096])
        put(f"gln_g{e}", inp["gmlp_ln_g"][e])
        put(f"gln_b{e}", inp["gmlp_ln_b"][e])
        put(f"conv_b{e}", inp["conv_b"][e])
        put(f"cln_g{e}", inp["conv_ln_g"][e])
        put(f"cln_b{e}", inp["conv_ln_b"][e])
        put(f"b_out{e}", inp["ab_b_out"][e])
        cw = np.asarray(inp["conv_w"][e]).reshape(CONVW, 8, 128)
        V[:, VOFF[f"conv_w{e}"]:VOFF[f"conv_w{e}"] + CONVW * 8] = cw.transpose(2, 0, 1).reshape(128, CONVW * 8)
    for o in range(2):
        put(f"pool_b{o}", np.asarray(inp["pool_b"][o]).reshape(-1))
        put(f"pool_s{o}", inp["pool_scale"][o])
    put("n_final", inp["norm_final"])
    return V


def build_nc(n_layers=DEPTH, do_final=True, stop_stage=None):
    nc = bass.Bass("TRN2", target_bir_lowering=False)

    def din(name, shape):
        return nc.dram_tensor(name, list(shape), F32, kind="ExternalInput").ap()

    x_d = din("x", (T, D))
    mem_d = din("mem", (NMEM, D))
    vec_d = din("vecs", (128, NV))
    ident_d = din("ident", (128, 128))
    corr_d = din("corr", (128, 64))
    bvb_d = din("bvb", (2, 128, 1024))
    bsb_d = din("bsb", (2, 128, 1024))
    ws_d = din("gmlp_w_s", (2, 8, 128, 128))
    wd = {}
    for nm, shp in (("ffn1_gate", (DEPTH, D, FF)), ("ffn1_up", (DEPTH, D, FF)), ("ffn1_down", (DEPTH, FF, D)),
                    ("ffn2_gate", (DEPTH, D, FF)), ("ffn2_up", (DEPTH, D, FF)), ("ffn2_down", (DEPTH, FF, D)),
                    ("ab_w_in", (2, D, 4096)), ("ab_w_out", (2, D, D)), ("pool_w", (2, 4, 512, 512)),
                    ("xattn_wq", (DEPTH, D, D)), ("xattn_wk", (DEPTH, D, D)),
                    ("xattn_wv", (DEPTH, D, D)), ("xattn_wo", (DEPTH, D, D))):
        wd[nm] = din(nm, shp)
    out_d = nc.dram_tensor("out", [T, D], F32, kind="ExternalOutput").ap()

    es = ExitStack()
    with es:
        E = es.enter_context

        def sb(name, shape, dt):
            return E(nc.sbuf_tensor(name, list(shape), dt))

        xres = sb("xres", (128, KC, T), F32)
        hbuf = sb("hbuf", (128, NTT, KC, TT), BF16)
        hid = sb("hid", (128, 2 * G, T), BF16)
        ring = sb("ring", (128, NSLOT, SLOT), BF16)
        arena = sb("arena", (128, 8192), F32)
        vecs = sb("vecs_sb", (128, NV), F32)
        ident = sb("ident_sb", (128, 128), F32)
        ones_f = sb("ones_f", (128, 128), F32)
        ones_b = sb("ones_b", (128, 128), BF16)
        NTMP = 4
        tmpf = sb("tmpf", (128, NTMP, TT), F32)
        tmpb = sb("tmpb", (128, NTMP, TT), BF16)
        rstd_t = sb("rstd_t", (128, 2, TT), F32)
        h2flat = hbuf[:, 2, :, :].rearrange("p c t -> p (c t)")
        bvb = h2flat[:, 0:2048].bitcast(F32)
        Th = h2flat[:, 2048:4096].bitcast(F32).rearrange("p (h q) -> p h q", h=8)
        wsT = h2flat[:, 4096:5120].rearrange("p (h q) -> p h q", h=8)
        corr = sb("corr_sb", (128, 64), F32)
        small = sb("small", (128, 64), F32)
        epsc = sb("epsc", (128, 1), F32)
        ps = [E(nc.psum_tensor(f"ps{i}", [128, 512], F32)) for i in range(8)]

        sems = {e: E(nc.semaphore(f"s_{e}")) for e in ("pe", "act", "dve", "pool")}
        dsl = [DmaSem(E(nc.semaphore(f"dslot{i}"))) for i in range(NSLOT)]
        d_misc = [DmaSem(E(nc.semaphore(f"dmisc{i}"))) for i in range(4)]
        d_out = [DmaSem(E(nc.semaphore(f"dout{i}"))) for i in range(3)]

        P = Prog(nc)

        r_x = [[Res(f"x{kc}_{tt}") for tt in range(NTT)] for kc in range(KC)]
        r_h = [[Res(f"h{s}_{kc}") for kc in range(KC)] for s in range(NTT)]
        r_hid = [[Res() for tt in range(NTT)] for fc in range(2 * G)]
        r_slot = [Res(f"slot{i}") for i in range(NSLOT)]
        r_ps = [Res(f"ps{i}") for i in range(8)]
        r_tmp = [Res() for _ in range(NTMP)]
        r_tmpb = [Res() for _ in range(NTMP)]
        r_rstd = [Res(), Res()]
        r_vecs, r_ident, r_ones, r_corr = Res(), Res(), Res(), Res()
        r_small = Res()
        r_sm = [Res() for _ in range(8)]
        r_bvb = r_h[2][0:6]
        r_Th = r_h[2][5:11]
        r_wsT = r_h[2][10:14]
        r_hc = [Res() for _ in range(8)]
        r_cv = [Res() for _ in range(8)]
        r_vn = [Res() for _ in range(3)]
        r_stgx = [Res(), Res()]
        r_arena_all = r_hc + r_cv + r_vn + r_stgx
        r_vtb = [Res(), Res(), Res()]

        HCW = 30 + TT
        hc = arena[:, 0:4 * HCW].bitcast(BF16).rearrange("p (c w) -> p c w", c=8)
        cv = arena[:, 3312:3312 + 8 * TT].rearrange("p (c w) -> p c w", c=8)
        vt = arena[:, 3312:3312 + 3072].rearrange("p (b w) -> p b w", b=3)
        vn = arena[:, 6384:7920].bitcast(BF16).rearrange("p (b w) -> p b w", b=3)
        kT = arena[:, 0:2048].bitcast(BF16).rearrange("p (c m) -> p c m", c=KC)
        vtok = arena[:, 2048:4096].bitcast(BF16).rearrange("p (b d) -> p b d", b=2)
        memT = arena[:, 4096:6144].bitcast(BF16).rearrange("p (c m) -> p c m", c=KC)
        stg = arena[:, 6144:8192]
        expT = arena[:, 6144:6144 + 768].bitcast(BF16).rearrange("p (a b w) -> p a b w", a=2, b=2)
        wst = arena[:, 1656:1656 + 1024].rearrange("p (h q) -> p h q", h=8)
        r_wst = Res("wst")
        r_arena_all.append(r_wst)
        HPW = 16 + TT
        hp = arena[:, 0:16 * HPW].rearrange("p (c w) -> p c w", c=KC)
        pwt = arena[:, 6400:6400 + 4 * HPW].rearrange("p (c w) -> p c w", c=4)

        tmp_i = [0]

        def get_tmp():
            i = tmp_i[0] % NTMP
            tmp_i[0] += 1
            return tmpf[:, i, :], r_tmp[i]

        tmpb_i = [0]

        def get_tmpb():
            i = tmpb_i[0] % NTMP
            tmpb_i[0] += 1
            return tmpb[:, i, :], r_tmpb[i]

        ps_i = [0]

        def get_ps(lo=0, hi=8):
            n = hi - lo
            i = lo + (ps_i[0] % n)
            ps_i[0] += 1
            return ps[i], r_ps[i]

        r_bar = Res("bar")

        def arena_barrier(extra=()):
            P.dve(lambda e: e.memset(small[:, 63:64], 0.0), writes=r_arena_all + [r_bar] + list(extra))

        def vcol(name, c, n=1):
            o = VOFF[name] + c
            return vecs[:, o:o + n]

        P.dma("sp", lambda e: e.dma_start(out=vecs[:], in_=vec_d[:, :]), d_misc[0], writes=[r_vecs])
        P.dma("sp", lambda e: e.dma_start(out=ident[:], in_=ident_d[:, :]), d_misc[0], writes=[r_ident])
        P.dma("sp", lambda e: e.dma_start(out=corr[:], in_=corr_d[:, :]), d_misc[0], writes=[r_corr])
        P.dve(lambda e: e.memset(ones_f[:], 1.0), writes=[r_ones])
        P.dve(lambda e: e.memset(ones_b[:], 1.0), writes=[r_ones])

        stg2 = [arena[:, 6144:8192], arena[:, 4096:6144]]
        for blk in range(T // 128):
            tt, bo = divmod(blk, 3)
            sg_, rsg_ = stg2[blk % 2], r_stgx[blk % 2]
            P.dma("sp", lambda e, blk=blk, sg_=sg_: e.dma_start(out=sg_, in_=x_d[blk * 128:(blk + 1) * 128, :]),
                  d_misc[1 + 2 * (blk % 2)], writes=[rsg_])
            for q in range(4):
                pt, rpt = get_ps()
                for j in range(4):
                    kc = 4 * q + j
                    P.pe(lambda e, pt=pt, j=j, kc=kc, sg_=sg_: e.transpose(out=pt[:, j * 128:(j + 1) * 128],
                                                                           in_=sg_[:, kc * 128:(kc + 1) * 128], identity=ident[:]),
                         reads=[rsg_, r_ident], writes=[rpt])
                dst = xres[:, 4 * q:4 * q + 4, blk * 128:(blk + 1) * 128]
                src = pt[:].rearrange("p (c t) -> p c t", c=4)
                wr = [r_x[4 * q + j][tt] for j in range(4)]
                if q % 2 == 0:
                    P.act(lambda e, dst=dst, src=src: e.activation(out=dst, in_=src, func=AF.Copy), reads=[rpt], writes=wr)
                else:
                    P.dve(lambda e, dst=dst, src=src: e.tensor_copy(out=dst, in_=src), reads=[rpt], writes=wr)

        tasks = []

        def add_c(fn):
            tasks.append(("c", None, fn))

        def add_w(load, fn):
            tasks.append(("w", load, fn))

        def slot_view3(si, a, b):
            return ring[:, si, 0:a * b].rearrange("p (a b) -> p a b", a=a)

        def load_cols(wap, c0, ncols, kchunks):
            def f(si):
                src = wap.rearrange("(kc p) f -> p kc f", p=128)[:, :, c0:c0 + ncols]
                dst = slot_view3(si, kchunks, ncols)
                P.dma("pool", lambda e: e.dma_start(out=dst, in_=src), dsl[si], writes=[r_slot[si]])
            return f

        def load_rows(wap, r0, nrc):
            def f(si):
                src = wap[r0 * 128:(r0 + nrc) * 128, :].rearrange("(fc p) d -> p fc d", p=128)
                dst = slot_view3(si, nrc, D)
                P.dma("pool", lambda e: e.dma_start(out=dst, in_=src), dsl[si], writes=[r_slot[si]])
            return f

        def emit_rstd(xsrc, xres_list, nchunks, n, inv_n, out_idx):
            pst, rpst = ps[7], r_ps[7]
            for c in range(nchunks):
                sq, rsq = get_tmpb()
                P.act(lambda e, sq=sq, c=c: e.activation(out=sq[:, 0:n], in_=xsrc(c), func=AF.Square),
                      reads=[xres_list[c]], writes=[rsq])
                P.pe(lambda e, sq=sq, c=c: e.matmul(pst[:, 0:n], lhsT=ones_b[:], rhs=sq[:, 0:n],
                                                    start=(c == 0), stop=(c == nchunks - 1)),
                     reads=[rsq, r_ones], writes=[rpst])
            rt = rstd_t[:, out_idx, 0:n]
            P.act(lambda e: e.activation(out=rt, in_=pst[:, 0:n], func=AF.Sqrt, bias=EPS_AP(), scale=inv_n),
                  reads=[rpst, r_eps], writes=[r_rstd[out_idx]])
            P.dve(lambda e: e.reciprocal(out=rt, in_=rt), reads=[r_rstd[out_idx]], writes=[r_rstd[out_idx]])
            return rt, r_rstd[out_idx]

        r_eps = Res("eps")

        def EPS_AP():
            return epsc[:, 0:1]

        P.dve(lambda e: e.memset(epsc[:, 0:1], EPS), writes=[r_eps])

        rs_i = [0]

        def emit_norm(gname, tt, dst_fn, dst_res_fn):
            c0 = tt * TT
            oi = rs_i[0] % 2
            rs_i[0] += 1
            rt, rrt = emit_rstd(lambda c: xres[:, c, c0:c0 + TT], [r_x[c][tt] for c in range(KC)], KC, TT, 1.0 / D, oi)
            for kc in range(KC):
                P.dve(lambda e, kc=kc: e.scalar_tensor_tensor(out=dst_fn(kc), in0=xres[:, kc, c0:c0 + TT],
                                                              scalar=vcol(gname, kc), in1=rt,
                                                              op0=ALU.mult, op1=ALU.mult),
                      reads=[r_x[kc][tt], rrt, r_vecs], writes=[dst_res_fn(kc)])

        def ffn_tasks(l, which):
            wg, wu, wdn = wd[f"ffn{which}_gate"][l], wd[f"ffn{which}_up"][l], wd[f"ffn{which}_down"][l]
            gname = f"n_ffn{which}{l}"

            def norm_all():
                for tt in range(NTT):
                    emit_norm(gname, tt, lambda kc, tt=tt: hbuf[:, tt, kc, :], lambda kc, tt=tt: r_h[tt][kc])
            add_c(norm_all)
            st = {}
            for gp in range(NG // 2):
                def c_gate(si, st=st):
                    st["g"] = si

                def c_up(si, half, st=st):
                    sg = st["g"]
                    gv = slot_view3(sg, KC, G * 128)
                    uv = slot_view3(si, KC, G * 128)
                    for fc in range(G):
                        hc_ = half * G + fc
                        for tt in range(NTT):
                            pg, rpg = get_ps(0, 6)
                            pu, rpu = get_ps(0, 6)
                            for kc in range(KC):
                                P.pe(lambda e, pg=pg, kc=kc, fc=fc, tt=tt: e.matmul(
                                    pg[:, 0:TT], lhsT=gv[:, kc, fc * 128:(fc + 1) * 128], rhs=hbuf[:, tt, kc, :],
                                    start=(kc == 0), stop=(kc == KC - 1)),
                                    reads=[r_slot[sg], r_h[tt][kc]], writes=[rpg])
                            for kc in range(KC):
                                P.pe(lambda e, pu=pu, kc=kc, fc=fc, tt=tt: e.matmul(
                                    pu[:, 0:TT], lhsT=uv[:, kc, fc * 128:(fc + 1) * 128], rhs=hbuf[:, tt, kc, :],
                                    start=(kc == 0), stop=(kc == KC - 1)),
                                    reads=[r_slot[si], r_h[tt][kc]], writes=[rpu])
                            sl, rsl = get_tmp()
                            P.act(lambda e, sl=sl, pg=pg: e.activation(out=sl, in_=pg[:, 0:TT], func=AF.Silu),
                                  reads=[rpg], writes=[rsl])
                            P.dve(lambda e, sl=sl, pu=pu, hc_=hc_, tt=tt: e.tensor_tensor(
                                out=hid[:, hc_, tt * TT:(tt + 1) * TT], in0=sl, in1=pu[:, 0:TT], op=ALU.mult),
                                reads=[rsl, rpu], writes=[r_hid[hc_][tt]])

                def c_dstore(si, st=st):
                    st["d0"] = si

                def c_down(si, st=st):
                    s0 = st["d0"]
                    dvs = [slot_view3(s0, G, D), slot_view3(si, G, D)]
                    srs = [r_slot[s0], r_slot[si]]
                    for dc in range(KC):
                        for tt in range(NTT):
                            pd, rpd = get_ps(0, 6)
                            for q in range(2 * G):
                                hf, fc = divmod(q, G)
                                P.pe(lambda e, pd=pd, q=q, hf=hf, fc=fc, dc=dc, tt=tt: e.matmul(
                                    pd[:, 0:TT], lhsT=dvs[hf][:, fc, dc * 128:(dc + 1) * 128],
                                    rhs=hid[:, q, tt * TT:(tt + 1) * TT], start=(q == 0), stop=(q == 2 * G - 1)),
                                    reads=[srs[hf], r_hid[q][tt]], writes=[rpd])
                            xs = xres[:, dc, tt * TT:(tt + 1) * TT]
                            P.dve(lambda e, pd=pd, xs=xs: e.scalar_tensor_tensor(
                                out=xs, in0=pd[:, 0:TT], scalar=0.5, in1=xs, op0=ALU.mult, op1=ALU.add),
                                reads=[rpd, r_x[dc][tt]], writes=[r_x[dc][tt]])

                g0, g1 = 2 * gp, 2 * gp + 1
                add_w(load_cols(wg, g0 * G * 128, G * 128, KC), c_gate)
                add_w(load_cols(wu, g0 * G * 128, G * 128, KC), lambda si, c_up=c_up: c_up(si, 0))
                add_w(load_cols(wg, g1 * G * 128, G * 128, KC), c_gate)
                add_w(load_cols(wu, g1 * G * 128, G * 128, KC), lambda si, c_up=c_up: c_up(si, 1))
                add_w(load_rows(wdn, g0 * G, G), c_dstore)
                add_w(load_rows(wdn, g1 * G, G), c_down)

        def proj_add_tasks(wap, src_sub, tt, bias_name=None, hooks=None):
            for j in range(8):
                if hooks and j in hooks:
                    hooks[j]()

                def c(si, j=j):
                    wv = slot_view3(si, KC, 256)
                    for cc in range(2):
                        dc = 2 * j + cc
                        pd, rpd = get_ps(0, 7)
                        for kc in range(KC):
                            P.pe(lambda e, pd=pd, kc=kc, cc=cc: e.matmul(
                                pd[:, 0:TT], lhsT=wv[:, kc, cc * 128:(cc + 1) * 128], rhs=hbuf[:, src_sub, kc, :],
                                start=(kc == 0), stop=(kc == KC - 1)),
                                reads=[r_slot[si], r_h[src_sub][kc]], writes=[rpd])
                        xs = xres[:, dc, tt * TT:(tt + 1) * TT]
                        if bias_name is None:
                            P.dve(lambda e, pd=pd, xs=xs: e.tensor_tensor(out=xs, in0=pd[:, 0:TT], in1=xs, op=ALU.add),
                                  reads=[rpd, r_x[dc][tt]], writes=[r_x[dc][tt]])
                        else:
                            P.dve(lambda e, pd=pd, xs=xs, dc=dc: e.scalar_tensor_tensor(
                                out=xs, in0=pd[:, 0:TT], scalar=vcol(bias_name, dc), in1=xs, op0=ALU.add, op1=ALU.add),
                                reads=[rpd, r_x[dc][tt], r_vecs], writes=[r_x[dc][tt]])
                add_w(load_cols(wap, j * 256, 256, KC), c)

        def even_setup(e_):
            def f():
                arena_barrier()
                for c in range(8):
                    P.dve(lambda e, c=c: e.memset(hc[:, c, 0:30], 0.0), writes=[r_hc[c]])
                P.dma("sp", lambda e: e.dma_start(out=wst, in_=ws_d[e_].rearrange("h p q -> p h q")), d_misc[3], writes=[r_wst])
                P.op("pool", lambda e: e.memset(wst[0:64, :, 64:128], 0.0), writes=[r_wst])
            return f

        def even_setup_b(e_):
            def f():
                P.dma("sp", lambda e: e.dma_start(out=bvb, in_=bvb_d[e_]), d_misc[2], writes=r_bvb)
                P.dma("sp", lambda e: e.dma_start(out=Th.rearrange("p h q -> p (h q)"), in_=bsb_d[e_]),
                      d_misc[2], writes=r_Th)
                for h0 in (0, 4):
                    pt, rpt = get_ps(0, 7)
                    for h in range(h0, h0 + 4):
                        P.pe(lambda e, pt=pt, h=h, h0=h0: e.transpose(out=pt[:, (h - h0) * 128:(h - h0 + 1) * 128], in_=wst[:, h, :], identity=ident[:]),
                             reads=[r_wst, r_ident], writes=[rpt])
                    P.act(lambda e, pt=pt, h0=h0: e.activation(out=wsT[:, h0:h0 + 4, :], in_=pt[:].rearrange("p (h q) -> p h q", h=4), func=AF.Copy),
                          reads=[rpt], writes=r_wsT)
                for h0 in (0, 4):
                    pr, rpr = get_ps(0, 7)
                    for h in range(h0, h0 + 4):
                        P.pe(lambda e, pr=pr, h=h, h0=h0: e.matmul(pr[:, (h - h0) * 128:(h - h0 + 1) * 128], lhsT=ones_b[:], rhs=wsT[:, h, :], start=True, stop=True),
                             reads=r_wsT + [r_ones], writes=[rpr])
                    for h in range(h0, h0 + 4):
                        P.dve(lambda e, pr=pr, h=h, h0=h0: e.scalar_tensor_tensor(
                            out=Th[:, h, :], in0=pr[:, (h - h0) * 128:(h - h0 + 1) * 128], scalar=vcol(f"gln_b{e_}", h), in1=Th[:, h, :],
                            op0=ALU.mult, op1=ALU.add), reads=[rpr, r_vecs] + r_Th, writes=r_Th)
            return f

        def even_pre(l, tt):
            def pre():
                emit_norm(f"n_mix{l}", tt, lambda kc: hbuf[:, 0, kc, :], lambda kc: r_h[0][kc])
            add_c(pre)

        def even_tasks(l, tt, before_out=None, out_hooks=None):
            e_ = l // 2
            w_in = wd["ab_w_in"][e_]
            w_out = wd["ab_w_out"][e_]
            cwo = VOFF[f"conv_w{e_}"]

            st = {}
            for j in range(4):
                def c_a(si, st=st):
                    st["a"] = si

                def c_g(si, j=j, st=st):
                    sa = st["a"]
                    av = slot_view3(sa, KC, 256)
                    gv = slot_view3(si, KC, 256)
                    for cc in range(2):
                        ch = 2 * j + cc
                        pa, rpa = get_ps(0, 7)
                        pg, rpg = get_ps(0, 7)
                        for kc in range(KC):
                            P.pe(lambda e, pa=pa, kc=kc, cc=cc: e.matmul(
                                pa[:, 0:TT], lhsT=av[:, kc, cc * 128:(cc + 1) * 128], rhs=hbuf[:, 0, kc, :],
                                start=(kc == 0), stop=(kc == KC - 1)), reads=[r_slot[sa], r_h[0][kc]], writes=[rpa])
                        for kc in range(KC):
                            P.pe(lambda e, pg=pg, kc=kc, cc=cc: e.matmul(
                                pg[:, 0:TT], lhsT=gv[:, kc, cc * 128:(cc + 1) * 128], rhs=hbuf[:, 0, kc, :],
                                start=(kc == 0), stop=(kc == KC - 1)), reads=[r_slot[si], r_h[0][kc]], writes=[rpg])
                        sg, rsg = get_tmp()
                        P.act(lambda e, sg=sg, pg=pg, ch=ch: e.activation(out=sg, in_=pg[:, 0:TT], func=AF.Sigmoid,
                                                                          bias=vcol(f"b_g{e_}", ch)),
                              reads=[rpg, r_vecs], writes=[rsg])
                        P.dve(lambda e, sg=sg, pa=pa, ch=ch: e.scalar_tensor_tensor(
                            out=hc[:, ch, 30:30 + TT], in0=pa[:, 0:TT], scalar=vcol(f"b_a{e_}", ch), in1=sg,
                            op0=ALU.add, op1=ALU.mult), reads=[rpa, rsg, r_vecs], writes=[r_hc[ch]])
                add_w(load_cols(w_in, 2048 + j * 256, 256, KC), c_a)
                add_w(load_cols(w_in, 3072 + j * 256, 256, KC), c_g)

            for c in range(8):
                def ld_diag(si, c=c):
                    dv = slot_view3(si, 32, 128)
                    for k in range(CONVW):
                        wcol = vecs[:, cwo + k * 8 + c:cwo + k * 8 + c + 1]
                        P.dve(lambda e, k=k, wcol=wcol: e.tensor_scalar(out=dv[:, k, :], in0=ident[:], scalar1=wcol,
                                                                        scalar2=None, op0=ALU.mult),
                              reads=[r_ident, r_vecs], writes=[r_slot[si]] if k in (0, CONVW - 1) else [])

                def c_conv(si, c=c):
                    dv = slot_view3(si, 32, 128)
                    pc, rpc = get_ps(0, 7)
                    for k in range(CONVW):
                        P.pe(lambda e, pc=pc, k=k: e.matmul(pc[:, 0:TT], lhsT=dv[:, k, :], rhs=hc[:, c, k:k + TT],
                                                            start=(k == 0), stop=(k == CONVW - 1)),
                             reads=[r_slot[si], r_hc[c]], writes=[rpc])
                    P.act(lambda e, pc=pc: e.activation(out=cv[:, c, :], in_=pc[:, 0:TT], func=AF.Identity,
                                                        bias=vcol(f"conv_b{e_}", c)),
                          reads=[rpc, r_vecs], writes=[r_cv[c]])
                    P.act(lambda e: e.activation(out=hc[:, c, 0:30], in_=hc[:, c, TT:TT + 30], func=AF.Copy),
                          reads=[r_hc[c]], writes=[r_hc[c]])
                    if c == 7:
                        conv_ln()
                add_w(ld_diag, c_conv)

            if tt == 0:
                add_c(even_setup_b(e_))

            def conv_ln():
                pm, rpm = ps[6], r_ps[6]
                for c in range(8):
                    P.pe(lambda e, c=c: e.matmul(pm[:, 0:TT], lhsT=ones_f[:], rhs=cv[:, c, :], start=(c == 0), stop=(c == 7)),
                         reads=[r_cv[c], r_ones], writes=[rpm])
                pq, rpq = ps[7], r_ps[7]
                for c in range(8):
                    sq, rsq = get_tmp()
                    P.act(lambda e, sq=sq, c=c: e.activation(out=sq, in_=cv[:, c, :], func=AF.Square),
                          reads=[r_cv[c]], writes=[rsq])
                    P.pe(lambda e, sq=sq, c=c: e.matmul(pq[:, 0:TT], lhsT=ones_f[:], rhs=sq, start=(c == 0), stop=(c == 7)),
                         reads=[rsq, r_ones], writes=[rpq])
                mean, rmean = get_tmp()
                P.dve(lambda e, mean=mean: e.tensor_scalar(out=mean, in0=pm[:, 0:TT], scalar1=1.0 / 1024, scalar2=None, op0=ALU.mult),
                      reads=[rpm], writes=[rmean])
                msq, rmsq = get_tmp()
                P.dve(lambda e, mean=mean, msq=msq: e.tensor_tensor(out=msq, in0=mean, in1=mean, op=ALU.mult),
                      reads=[rmean], writes=[rmsq])
                oi = rs_i[0] % 2
                rs_i[0] += 1
                rt, rrt = rstd_t[:, oi, :], r_rstd[oi]
                P.dve(lambda e, msq=msq: e.scalar_tensor_tensor(out=rt, in0=pq[:, 0:TT], scalar=1.0 / 1024, in1=msq,
                                                                op0=ALU.mult, op1=ALU.subtract),
                      reads=[rpq, rmsq], writes=[rrt])
                P.act(lambda e: e.activation(out=rt, in_=rt, func=AF.Sqrt, bias=EPS_AP(), scale=1.0),
                      reads=[rrt, r_eps], writes=[rrt])
                P.dve(lambda e: e.reciprocal(out=rt, in_=rt), reads=[rrt], writes=[rrt])
                for c in range(8):
                    P.dve(lambda e, c=c, mean=mean: e.tensor_tensor(out=cv[:, c, :], in0=cv[:, c, :], in1=mean, op=ALU.subtract),
                          reads=[r_cv[c], rmean], writes=[r_cv[c]])
                    P.dve(lambda e, c=c: e.tensor_tensor(out=cv[:, c, :], in0=cv[:, c, :], in1=rt, op=ALU.mult),
                          reads=[r_cv[c], rrt], writes=[r_cv[c]])
                    P.act(lambda e, c=c: e.activation(out=hbuf[:, 1, 8 + c, :], in_=cv[:, c, :], func=AF.Silu,
                                                      bias=vcol(f"cln_b{e_}", c), scale=vcol(f"cln_g{e_}", c)),
                          reads=[r_cv[c], r_vecs], writes=[r_h[1][8 + c]])

            for j in range(4):
                def c_v(si, j=j):
                    wv = slot_view3(si, KC, 256)
                    for blk in range(3):
                        pv, rpv = get_ps(0, 7)
                        for kc in range(KC):
                            P.pe(lambda e, pv=pv, kc=kc, blk=blk: e.matmul(
                                pv[:, 0:256], lhsT=hbuf[:, 0, kc, blk * 128:(blk + 1) * 128], rhs=wv[:, kc, :],
                                start=(kc == 0), stop=(kc == KC - 1)), reads=[r_slot[si], r_h[0][kc]], writes=[rpv])
                        P.dve(lambda e, pv=pv, blk=blk, j=j: e.tensor_tensor(
                            out=vt[:, blk, j * 256:(j + 1) * 256], in0=pv[:, 0:256], in1=bvb[:, j * 256:(j + 1) * 256],
                            op=ALU.add), reads=[rpv] + r_bvb,
                            writes=([r_vtb[blk]] + r_cv[(blk * 1024) // TT:((blk + 1) * 1024 - 1) // TT + 1]) if j == 0 else [r_vtb[blk]])
                    if j == 3:
                        v_post()
                add_w(load_cols(w_in, 1024 + j * 256, 256, KC), c_v)

            for j in range(4):
                def c_u(si, j=j):
                    wv = slot_view3(si, KC, 256)
                    for cc in range(2):
                        ch = 2 * j + cc
                        pu, rpu = get_ps(0, 7)
                        for kc in range(KC):
                            P.pe(lambda e, pu=pu, kc=kc, cc=cc: e.matmul(
                                pu[:, 0:TT], lhsT=wv[:, kc, cc * 128:(cc + 1) * 128], rhs=hbuf[:, 0, kc, :],
                                start=(kc == 0), stop=(kc == KC - 1)), reads=[r_slot[si], r_h[0][kc]], writes=[rpu])
                        P.act(lambda e, pu=pu, ch=ch: e.activation(out=hbuf[:, 1, ch, :], in_=pu[:, 0:TT], func=AF.Gelu_apprx_tanh,
                                                                   bias=vcol(f"b_u{e_}", ch)),
                              reads=[rpu, r_vecs], writes=[r_h[1][ch]])
                    if j == 3:
                        v_spatial()
                add_w(load_cols(w_in, j * 256, 256, KC), c_u)

            def v_post():
                B3 = range(3)
                st6 = [small[:, 8 + blk * 16:8 + blk * 16 + 12].rearrange("p (a b) -> p a b", a=2) for blk in B3]
                mv = [small[:, 8 + blk * 16 + 12:8 + blk * 16 + 14] for blk in B3]
                rs = [small[:, 8 + blk * 16 + 14:8 + blk * 16 + 15] for blk in B3]
                for blk in B3:
                    P.act(lambda e, blk=blk: e.activation(out=vt[:, blk, :], in_=vt[:, blk, :], func=AF.Gelu_apprx_tanh),
                          reads=[r_vtb[blk]], writes=[r_vtb[blk]])
                for hh in range(2):
                    for blk in B3:
                        P.dve(lambda e, blk=blk, hh=hh: e.bn_stats(out=st6[blk][:, hh, :], in_=vt[:, blk, hh * 512:(hh + 1) * 512]),
                              reads=[r_vtb[blk]], writes=[r_sm[2 + blk]] if hh == 1 else [])
                for blk in B3:
                    P.dve(lambda e, blk=blk: e.bn_aggr(out=mv[blk], in_=st6[blk].rearrange("p a b -> p (a b)")),
                          reads=[r_sm[2 + blk]], writes=[r_sm[2 + blk]])
                for blk in B3:
                    P.act(lambda e, blk=blk: e.activation(out=rs[blk], in_=mv[blk][:, 1:2], func=AF.Sqrt, bias=EPS_AP(), scale=1.0),
                          reads=[r_sm[2 + blk], r_eps], writes=[r_sm[2 + blk]])
                for blk in B3:
                    P.dve(lambda e, blk=blk: e.reciprocal(out=rs[blk], in_=rs[blk]), reads=[r_sm[2 + blk]], writes=[r_sm[2 + blk]])
                for blk in B3:
                    P.dve(lambda e, blk=blk: e.tensor_scalar(
                        out=vn[:, blk, :], in0=vt[:, blk, :], scalar1=mv[blk][:, 0:1], scalar2=rs[blk],
                        op0=ALU.subtract, op1=ALU.mult),
                        reads=[r_vtb[blk], r_sm[2 + blk]] + r_cv[(blk * 1024) // TT:((blk + 1) * 1024 - 1) // TT + 1], writes=[r_vn[blk]])
            def v_spatial():
                for blk in range(3):
                    for h in range(8):
                        pS, rpS = get_ps(0, 7)
                        P.pe(lambda e, pS=pS, blk=blk, h=h: e.matmul(
                            pS[:, 0:128], lhsT=vn[:, blk, h * 128:(h + 1) * 128], rhs=wsT[:, h, :], start=True, stop=True),
                            reads=[r_vn[blk]] + r_wsT, writes=[rpS])
                        tm, rtm = get_tmp()
                        P.dve(lambda e, pS=pS, tm=tm, h=h: e.scalar_tensor_tensor(
                            out=tm[:, 0:128], in0=pS[:, 0:128], scalar=vcol(f"gln_g{e_}", h), in1=Th[:, h, :],
                            op0=ALU.mult, op1=ALU.add), reads=[rpS, r_vecs] + r_Th, writes=[rtm])
                        ya = hbuf[:, 1, h, blk * 128:(blk + 1) * 128]
                        P.dve(lambda e, tm=tm, ya=ya: e.tensor_tensor(out=ya, in0=tm[:, 0:128], in1=ya, op=ALU.mult),
                              reads=[rtm, r_h[1][h]], writes=[r_h[1][h]])

            if before_out is not None:
                before_out()
            proj_add_tasks(w_out, 1, tt, bias_name=f"b_out{e_}", hooks=out_hooks)

        def pool_tasks(l, tt, before_mm=None):
            o_ = l // 2
            c0 = tt * TT
            r_hp = r_hc + r_cv
            r_pw = r_vn

            def pre():
                if tt == 0:
                    arena_barrier()
                    for kc in range(KC):
                        P.dve(lambda e, kc=kc: e.memset(hp[:, kc, 0:16], 0.0), writes=[r_hp[kc]])
                emit_norm(f"n_mix{l}", tt, lambda kc: hp[:, kc, 16:16 + TT], lambda kc: r_hp[kc % 16])
                W = HPW
                for kc in range(KC):
                    gi = kc // 4
                    src = hp[:, kc, :]
                    rsrc = r_hp[kc % 16]
                    cur = src
                    rcur = rsrc
                    sh = 1
                    for step in range(gi + 1):
                        dst = pwt[:, (step % 2) + 2 * (kc % 2), :]
                        rdst = r_pw[(step % 2 + 2 * (kc % 2)) % 3]
                        P.dve(lambda e, dst=dst, cur=cur, sh=sh: e.tensor_tensor(
                            out=dst[:, sh:W], in0=cur[:, sh:W], in1=cur[:, 0:W - sh], op=ALU.add),
                            reads=[rcur, rsrc], writes=[rdst])
                        cur, rcur = dst, rdst
                        sh *= 2
                    win = 2 ** (gi + 1)
                    if tt == 0:
                        P.dve(lambda e, cur=cur, gi=gi: e.tensor_tensor(
                            out=cur[:, 16:32], in0=cur[:, 16:32], in1=corr[:, gi * 16:(gi + 1) * 16], op=ALU.mult),
                            reads=[rcur, r_corr], writes=[rcur])
                    P.dve(lambda e, cur=cur, kc=kc, win=win: e.scalar_tensor_tensor(
                        out=hbuf[:, 1, kc, :], in0=cur[:, 16:16 + TT], scalar=1.0 / win, in1=hp[:, kc, 16:16 + TT],
                        op0=ALU.mult, op1=ALU.subtract), reads=[rcur, rsrc], writes=[r_h[1][kc]])
                    P.act(lambda e, kc=kc: e.activation(out=hp[:, kc, 0:16], in_=hp[:, kc, TT:TT + 16], func=AF.Copy),
                          reads=[rsrc], writes=[rsrc])
                if tt == 0:
                    P.dve(lambda e: e.tensor_tensor(out=small[:, 40:56], in0=vcol(f"pool_b{o_}", 0, 16),
                                                    in1=vcol(f"pool_s{o_}", 0, 16), op=ALU.mult),
                          reads=[r_vecs], writes=[r_sm[5]])
            add_c(pre)
            if before_mm is not None:
                before_mm()
            for gi in range(4):
                def c(si, gi=gi):
                    wv = slot_view3(si, 4, 512)
                    for oc in range(4):
                        dc = 4 * gi + oc
                        pd, rpd = get_ps(0, 7)
                        for k4 in range(4):
                            P.pe(lambda e, pd=pd, k4=k4, oc=oc: e.matmul(
                                pd[:, 0:TT], lhsT=wv[:, k4, oc * 128:(oc + 1) * 128], rhs=hbuf[:, 1, 4 * gi + k4, :],
                                start=(k4 == 0), stop=(k4 == 3)), reads=[r_slot[si], r_h[1][4 * gi + k4]], writes=[rpd])
                        tm, rtm = get_tmp()
                        P.act(lambda e, pd=pd, tm=tm, dc=dc: e.activation(
                            out=tm, in_=pd[:, 0:TT], func=AF.Identity, bias=small[:, 40 + dc:41 + dc],
                            scale=vcol(f"pool_s{o_}", dc)), reads=[rpd, r_sm[5], r_vecs], writes=[rtm])
                        xs = xres[:, dc, c0:c0 + TT]
                        P.dve(lambda e, tm=tm, xs=xs: e.tensor_tensor(out=xs, in0=tm, in1=xs, op=ALU.add),
                              reads=[rtm, r_x[dc][tt]], writes=[r_x[dc][tt]])

                def ld(si, gi=gi):
                    src = wd["pool_w"][o_, gi].rearrange("(kc p) f -> p kc f", p=128)
                    dst = slot_view3(si, 4, 512)
                    P.dma("pool", lambda e: e.dma_start(out=dst, in_=src), dsl[si], writes=[r_slot[si]])
                add_w(ld, c)

        r_kT = r_hc[0:4]
        r_vtok = r_hc[4:8]
        r_memT = r_cv[0:4]
        r_A = r_cv[4:8] + r_vn

        def xattn_kv_tasks(l, only=None):
            memT_hold = hbuf[:, 2, :, 0:256]

            stgs = [arena[:, 6144:8192], arena[:, 4096:6144]]
            r_stgs = [r_A, r_memT]
            junk = arena[:, 3072:4096].bitcast(BF16)

            def front(mb):
                if mb == 0:
                    arena_barrier()
                sg_, rsg_ = stgs[mb], r_stgs[mb]
                P.dma("sp", lambda e: e.dma_start(out=sg_, in_=mem_d[mb * 128:(mb + 1) * 128, :]),
                      d_misc[1 + 2 * mb], writes=rsg_)
                ss = small[:, 4 + mb:5 + mb]
                rss = r_sm[mb]
                P.act(lambda e: e.activation(out=junk, in_=sg_, func=AF.Square, accum_out=ss),
                      reads=rsg_, writes=r_vtok + [rss])
                P.act(lambda e: e.activation(out=ss, in_=ss, func=AF.Sqrt, bias=EPS_AP(), scale=1.0 / D),
                      reads=[rss, r_eps], writes=[rss])
                P.dve(lambda e: e.reciprocal(out=ss, in_=ss), reads=[rss], writes=[rss])
                P.dve(lambda e: e.tensor_scalar(out=sg_, in0=sg_, scalar1=ss, scalar2=None, op0=ALU.mult),
                      reads=rsg_ + [rss], writes=rsg_)

            def pe_part(mb):
                sg_, rsg_ = stgs[mb], r_stgs[mb]
                for q in range(4):
                    pt, rpt = get_ps(0, 7)
                    for j in range(4):
                        kc = 4 * q + j
                        P.pe(lambda e, pt=pt, j=j, kc=kc: e.transpose(out=pt[:, j * 128:(j + 1) * 128],
                                                                      in_=sg_[:, kc * 128:(kc + 1) * 128], identity=ident[:]),
                             reads=rsg_ + [r_ident], writes=[rpt])
                    for j in range(4):
                        kc = 4 * q + j
                        P.dve(lambda e, pt=pt, j=j, kc=kc: e.tensor_scalar(
                            out=memT_hold[:, kc, mb * 128:(mb + 1) * 128], in0=pt[:, j * 128:(j + 1) * 128],
                            scalar1=vcol(f"n_xkv{l}", kc), scalar2=None, op0=ALU.mult),
                            reads=[rpt, r_vecs], writes=[r_h[2][kc]])
            if only is not None and only != "w":
                kind, mb = only[:-1], int(only[-1])
                add_c((lambda: front(mb)) if kind == "front" else (lambda: pe_part(mb)))
                return
            wk, wv_ = wd["xattn_wk"][l], wd["xattn_wv"][l]
            for j in range(8):
                def c_k(si, j=j):
                    wv = slot_view3(si, KC, 256)
                    for cc in range(2):
                        dc = 2 * j + cc
                        pk, rpk = get_ps(0, 7)
                        for kc in range(KC):
                            P.pe(lambda e, pk=pk, kc=kc, cc=cc: e.matmul(
                                pk[:, 0:256], lhsT=wv[:, kc, cc * 128:(cc + 1) * 128], rhs=memT_hold[:, kc, :],
                                start=(kc == 0), stop=(kc == KC - 1)), reads=[r_slot[si], r_h[2][kc]], writes=[rpk])
                        P.act(lambda e, pk=pk, dc=dc: e.activation(out=kT[:, dc, :], in_=pk[:, 0:256], func=AF.Copy),
                              reads=[rpk], writes=r_kT)
                add_w(load_cols(wk, j * 256, 256, KC), c_k)
            for j in range(8):
                def c_v(si, j=j):
                    wv = slot_view3(si, KC, 256)
                    for mb in range(2):
                        pv, rpv = get_ps(0, 7)
                        for kc in range(KC):
                            P.pe(lambda e, pv=pv, kc=kc, mb=mb: e.matmul(
                                pv[:, 0:256], lhsT=memT_hold[:, kc, mb * 128:(mb + 1) * 128], rhs=wv[:, kc, :],
                                start=(kc == 0), stop=(kc == KC - 1)), reads=[r_slot[si], r_h[2][kc]], writes=[rpv])
                        P.dve(lambda e, pv=pv, mb=mb, j=j: e.tensor_copy(out=vtok[:, mb, j * 256:(j + 1) * 256], in_=pv[:, 0:256]),
                              reads=[rpv], writes=r_vtok)
                add_w(load_cols(wv_, j * 256, 256, KC), c_v)

        def xattn_pre(l, tt):
            def pre():
                emit_norm(f"n_xq{l}", tt, lambda kc: hbuf[:, 0, kc, :], lambda kc: r_h[0][kc])
            add_c(pre)

        def xattn_tasks(l, tt, before_out=None):
            wq, wo = wd["xattn_wq"][l], wd["xattn_wo"][l]
            scl = 512.0 ** -0.5
            for j in range(8):
                def c_q(si, j=j):
                    wv = slot_view3(si, KC, 256)
                    for cc in range(2):
                        dc = 2 * j + cc
                        pq, rpq = get_ps(0, 7)
                        for kc in range(KC):
                            P.pe(lambda e, pq=pq, kc=kc, cc=cc: e.matmul(
                                pq[:, 0:TT], lhsT=wv[:, kc, cc * 128:(cc + 1) * 128], rhs=hbuf[:, 0, kc, :],
                                start=(kc == 0), stop=(kc == KC - 1)), reads=[r_slot[si], r_h[0][kc]], writes=[rpq])
                        P.act(lambda e, pq=pq, dc=dc: e.activation(out=hbuf[:, 1, dc, :], in_=pq[:, 0:TT], func=AF.Copy, scale=scl),
                              reads=[rpq], writes=[r_h[1][dc]])
                    if j % 2 == 1:
                        attn_head(j // 2)
                add_w(load_cols(wq, j * 256, 256, KC), c_q)

            def attn_head(hd):
                eb = hd % 2
                for mb in range(2):
                    pS, rpS = get_ps(0, 7)
                    for j in range(4):
                        P.pe(lambda e, pS=pS, j=j, mb=mb: e.matmul(
                            pS[:, 0:TT], lhsT=kT[:, 4 * hd + j, mb * 128:(mb + 1) * 128], rhs=hbuf[:, 1, 4 * hd + j, :],
                            start=(j == 0), stop=(j == 3)), reads=r_kT + [r_h[1][4 * hd + j]], writes=[rpS])
                    P.act(lambda e, pS=pS, mb=mb: e.activation(out=expT[:, eb, mb, :], in_=pS[:, 0:TT], func=AF.Exp),
                          reads=[rpS], writes=[r_A[eb * 2 + mb]])
                pden, rpden = get_ps(0, 7)
                for mb in range(2):
                    P.pe(lambda e, mb=mb: e.matmul(pden[:, 0:TT], lhsT=ones_b[:], rhs=expT[:, eb, mb, :],
                                                   start=(mb == 0), stop=(mb == 1)),
                         reads=[r_A[eb * 2 + mb], r_ones], writes=[rpden])
                rd, rrd = get_tmp()
                P.dve(lambda e, rd=rd: e.reciprocal(out=rd, in_=pden[:, 0:TT]), reads=[rpden], writes=[rrd])
                for j in range(4):
                    po, rpo = get_ps(0, 7)
                    for mb in range(2):
                        P.pe(lambda e, po=po, j=j, mb=mb: e.matmul(
                            po[:, 0:TT], lhsT=vtok[:, mb, (4 * hd + j) * 128:(4 * hd + j + 1) * 128], rhs=expT[:, eb, mb, :],
                            start=(mb == 0), stop=(mb == 1)), reads=r_vtok + [r_A[eb * 2 + mb]], writes=[rpo])
                    P.dve(lambda e, po=po, j=j, rd=rd: e.tensor_tensor(out=hbuf[:, 2, 4 * hd + j, :], in0=po[:, 0:TT], in1=rd, op=ALU.mult),
                          reads=[rpo, rrd], writes=[r_h[2][4 * hd + j]])
            if before_out is not None:
                before_out()
            proj_add_tasks(wo, 2, tt)

        def final_tasks():
            def f():
                r_ost = [[Res() for _ in range(4)] for _ in range(3)]
                arena_barrier([r for k in range(3) for r in r_ost[k]])
                for tt in range(NTT):
                    c0 = tt * TT
                    oi = rs_i[0] % 2
                    rs_i[0] += 1
                    if do_final:
                        rt, rrt = emit_rstd(lambda c, c0=c0: xres[:, c, c0:c0 + TT], [r_x[c][tt] for c in range(KC)], KC, TT, 1.0 / D, oi)
                    for bo in range(3):
                        blk = tt * 3 + bo
                        ob = blk % 3
                        ost = arena[:, ob * 2048:(ob + 1) * 2048]
                        for q in range(4):
                            pt, rpt = get_ps(0, 7)
                            for j in range(4):
                                kc = 4 * q + j
                                cs = slice(c0 + bo * 128, c0 + (bo + 1) * 128)
                                if do_final:
                                    yt, ryt = get_tmp()
                                    P.dve(lambda e, yt=yt, kc=kc, cs=cs, bo=bo, rt=rt: e.scalar_tensor_tensor(
                                        out=yt[:, 0:128], in0=xres[:, kc, cs], scalar=vcol("n_final", kc),
                                        in1=rt[:, bo * 128:(bo + 1) * 128], op0=ALU.mult, op1=ALU.mult),
                                        reads=[r_x[kc][tt], rrt, r_vecs], writes=[ryt])
                                    P.pe(lambda e, pt=pt, j=j, yt=yt: e.transpose(out=pt[:, j * 128:(j + 1) * 128], in_=yt[:, 0:128], identity=ident[:]),
                                         reads=[ryt, r_ident], writes=[rpt])
                                else:
                                    P.pe(lambda e, pt=pt, j=j, kc=kc, cs=cs: e.transpose(out=pt[:, j * 128:(j + 1) * 128], in_=xres[:, kc, cs], identity=ident[:]),
                                         reads=[r_x[kc][tt], r_ident], writes=[rpt])
                            if q % 2 == 0:
                                P.act(lambda e, pt=pt, q=q, ost=ost: e.activation(out=ost[:, q * 512:(q + 1) * 512], in_=pt[:], func=AF.Copy),
                                      reads=[rpt], writes=[r_ost[ob][q]])
                            else:
                                P.dve(lambda e, pt=pt, q=q, ost=ost: e.tensor_copy(out=ost[:, q * 512:(q + 1) * 512], in_=pt[:]),
                                      reads=[rpt], writes=[r_ost[ob][q]])
                        P.dma("sp", lambda e, blk=blk, ost=ost: e.dma_start(out=out_d[blk * 128:(blk + 1) * 128, :], in_=ost),
                              d_out[ob], reads=r_ost[ob])
            add_c(f)

        for l in range(n_layers):
            ffn_tasks(l, 1)
            if stop_stage == (l, "ffn1"):
                break
            kv = (lambda what, l=l: (lambda: xattn_kv_tasks(l, only=what)))
            if l % 2 == 0:
                add_c(even_setup(l // 2))
                even_pre(l, 0)
                for tt in range(NTT):
                    if tt < NTT - 1:
                        even_tasks(l, tt, before_out=(lambda l=l, tt=tt: even_pre(l, tt + 1)))
                    else:
                        even_tasks(l, tt, out_hooks={0: kv("front0"), 1: kv("front1"), 4: kv("pe0"), 6: kv("pe1")})
            else:
                for tt in range(NTT):
                    if tt < NTT - 1:
                        pool_tasks(l, tt)
                    else:
                        pool_tasks(l, tt, before_mm=(lambda l=l: (xattn_kv_tasks(l, only="front0"), xattn_kv_tasks(l, only="front1"))))
                xattn_kv_tasks(l, only="pe0")
                xattn_kv_tasks(l, only="pe1")
            if stop_stage == (l, "mix"):
                break
            xattn_kv_tasks(l, only="w")
            xattn_pre(l, 0)
            for tt in range(NTT):
                xattn_tasks(l, tt, before_out=(lambda l=l, tt=tt: xattn_pre(l, tt + 1)) if tt < NTT - 1 else None)
            if stop_stage == (l, "xattn"):
                break
            ffn_tasks(l, 2)
        final_tasks()

        widx = [i for i, t in enumerate(tasks) if t[0] == "w"]
        nloaded = [0]

        def ensure_loaded(upto):
            while nloaded[0] < min(upto + 1, len(widx)):
                k = nloaded[0]
                tasks[widx[k]][1](k % NSLOT)
                nloaded[0] += 1

        wk_i = 0
        for t in tasks:
            if t[0] == "w":
                ensure_loaded(wk_i + NSLOT - 1 - LOOKBACK)
                t[2](wk_i % NSLOT)
                wk_i += 1
            else:
                ensure_loaded(wk_i + NSLOT - 2 - LOOKBACK)
                t[2]()

        P.finalize(sems, final_waits=lambda: [(d.h, d.count) for d in d_out])
    return nc


_NC_CACHE = {}


def make_in_maps(inputs):
    inp = {k: np.asarray(v) for k, v in inputs.items()}
    V = pack_vecs(inp)
    ident = np.eye(128, dtype=np.float32)
    b_in = inp["ab_b_in"]
    bvb = np.ascontiguousarray(np.broadcast_to(b_in[:, None, 1024:2048], (2, 128, 1024))).astype(np.float32)
    bsb = np.ascontiguousarray(np.broadcast_to(inp["gmlp_b_s"].reshape(2, 1, 1024), (2, 128, 1024))).astype(np.float32)
    shared = {"vecs": V, "ident": ident, "bvb": bvb, "bsb": bsb,
              "gmlp_w_s": np.ascontiguousarray(inp["gmlp_w_s"], dtype=np.float32)}
    for nm in ("ffn1_gate", "ffn1_up", "ffn1_down", "ffn2_gate", "ffn2_up", "ffn2_down", "ab_w_in", "ab_w_out",
               "pool_w", "xattn_wq", "xattn_wk", "xattn_wv", "xattn_wo"):
        shared[nm] = np.ascontiguousarray(inp[nm], dtype=np.float32)
    in_maps = []
    for c in range(8):
        b, half = divmod(c, 2)
        t0 = 0 if half == 0 else HALO0
        m = dict(shared)
        m["x"] = np.ascontiguousarray(inp["x"][b, t0:t0 + T, :], dtype=np.float32)
        m["mem"] = np.ascontiguousarray(inp["mem"][b], dtype=np.float32)
        corr = np.ones((128, 4, 16), np.float32)
        if half == 0:
            for gi, win in enumerate((2, 4, 8, 16)):
                for t in range(16):
                    corr[:, gi, t] = float(win) / float(min(t + 1, win))
        m["corr"] = corr.reshape(128, 64)
        in_maps.append(m)
    return in_maps


def assemble(results):
    out = np.empty((BATCH, SEQ, D), np.float32)
    for c in range(8):
        b, half = divmod(c, 2)
        y = results[c]["out"]
        if half == 0:
            out[b, 0:T, :] = y
        else:
            out[b, T:SEQ, :] = y[T - HALO0:, :]
    return out


def kernel(**inputs):
    key = "full"
    if key not in _NC_CACHE:
        _NC_CACHE[key] = build_nc()
    nc = _NC_CACHE[key]
    in_maps = make_in_maps(inputs)
    res = run_bass_kernel_spmd(nc, in_maps, core_ids=list(range(8)))
    return assemble(res.results)
```
